# Optimizing a Trainium2 kernel written in Bass

```python
import jax, jax.numpy as jnp
from jax import lax
import numpy as np

D_MODEL = 1024
BATCH = 8
SEQ = 2048
DEPTH = 1
DEC_BATCH = 128
DEC_SEQ = 1
PAST_LEN = 16384
PAGE_SIZE = 128

D_MIX = D_MODEL
LRU_WIDTH = D_MIX // 2
LRU_HEADS = 8
LRU_BLOCK = LRU_WIDTH // LRU_HEADS
CONV_WIDTH = 4
LRU_C = 8.0
HG_WIDTH = D_MIX - LRU_WIDTH
HG_HEAD_DIM = 128
HG_HEADS = HG_WIDTH // HG_HEAD_DIM
HG_CHUNK = 64
D_FF = 2816
N_MOD = 9
EPS = 1e-6

kernel_name = 'hymba_hawk_hgrn2_macaron_step'


def rms_norm(x, w):
    x32 = x.astype(jnp.float32)
    y = x32 * lax.rsqrt(jnp.mean(x32 * x32, axis=-1, keepdims=True) + EPS)
    return (y * w.astype(jnp.float32)).astype(x.dtype)


def swiglu(h, w_gate, w_up, w_down):
    return (jax.nn.silu(h @ w_gate) * (h @ w_up)) @ w_down


def causal_conv(u, buf, w, b):
    T = u.shape[1]
    full = jnp.concatenate([buf.astype(u.dtype), u], axis=1)
    out = b
    for k in range(CONV_WIDTH):
        out = out + full[:, k:k + T] * w[k]
    return out, full[:, T:]


def rg_lru(u, h0, w_a, b_a, w_x, b_x, lam, starts_sequence):
    B, T, C = u.shape
    f32 = jnp.float32
    u32 = u.astype(f32)
    ub = u32.reshape(B, T, LRU_HEADS, LRU_BLOCK)
    r = jax.nn.sigmoid(jnp.einsum('bthi,hij->bthj', ub, w_a.astype(f32)).reshape(B, T, C) + b_a.astype(f32))
    ig = jax.nn.sigmoid(jnp.einsum('bthi,hij->bthj', ub, w_x.astype(f32)).reshape(B, T, C) + b_x.astype(f32))
    log_a = -LRU_C * r * jax.nn.softplus(-lam.astype(f32))
    a = jnp.exp(log_a)
    mult = jnp.sqrt(-jnp.expm1(2.0 * log_a))
    if starts_sequence:
        mult = mult.at[:, 0].set(1.0)
    bterm = mult * ig * u32
    bterm = bterm.at[:, 0].add(a[:, 0] * h0.astype(f32))

    def combine(e1, e2):
        a1, b1 = e1
        a2, b2 = e2
        return a1 * a2, a2 * b1 + b2

    _, h = lax.associative_scan(combine, (a, bterm), axis=1)
    return h, h[:, -1]


def hgrn2(q, f_raw, v, g, lb, S0, norm_w):
    B, T, _ = q.shape
    f32 = jnp.float32

    def heads(z):
        return z.astype(f32).reshape(B, T, HG_HEADS, HG_HEAD_DIM)

    f = lb + (1.0 - lb) * jax.nn.sigmoid(f_raw.astype(f32))
    log_f = heads(jnp.log(f))
    k = heads(1.0 - f)
    qh = heads(q) * (HG_HEAD_DIM ** -0.5)
    vh = heads(v)
    C = min(HG_CHUNK, T)
    n = -(-T // C)
    pad = n * C - T

    def blocks(z):
        z = jnp.pad(z, ((0, 0), (0, pad), (0, 0), (0, 0)))
        return z.reshape(B, n, C, HG_HEADS, HG_HEAD_DIM).transpose(1, 0, 3, 2, 4)

    causal = jnp.tril(jnp.ones((C, C), dtype=bool))[:, :, None]

    def step(S, blk):
        qc, lfc, kc, vc = blk
        bc = jnp.cumsum(lfc, axis=2)
        diff = bc[:, :, :, None, :] - bc[:, :, None, :, :]
        decay = jnp.where(causal, jnp.exp(jnp.where(causal, diff, 0.0)), 0.0)
        scores = jnp.einsum('bhtk,bhtsk,bhsk->bhts', qc, decay, kc)
        o = (jnp.einsum('bhtk,bhkv->bhtv', qc * jnp.exp(bc), S)
             + jnp.einsum('bhts,bhsv->bhtv', scores, vc))
        b_last = bc[:, :, -1:]
        S_new = (jnp.exp(b_last[:, :, 0])[..., None] * S
                 + jnp.einsum('bhsk,bhsv->bhkv', kc * jnp.exp(b_last - bc), vc))
        return S_new, o

    S_T, o = lax.scan(step, S0.astype(f32), (blocks(qh), blocks(log_f), blocks(k), blocks(vh)))
    o = o.transpose(1, 0, 3, 2, 4).reshape(B, n * C, HG_HEADS, HG_HEAD_DIM)[:, :T]
    o = rms_norm(o, norm_w)
    o = o.reshape(B, T, HG_WIDTH) * jax.nn.silu(g.astype(f32))
    return o, S_T


def modulate(h, shift, scale):
    return h * (1.0 + scale) + shift


def layer(x, c, h0, conv0, S0, starts_sequence, lb, p):
    (w_ada, b_ada, ln_ffn1_pre, ln_ffn1_post, ffn1_w_gate, ffn1_w_up, ffn1_w_down,
     ln_mix_pre, ln_mix_post, w_in, lru_conv_w, lru_conv_b, lru_w_a, lru_b_a, lru_w_x, lru_b_x,
     lru_lambda, hg_norm_w, w_out, ln_ffn2_pre, ln_ffn2_post, ffn2_w_gate, ffn2_w_up, ffn2_w_down) = p
    mod = (jax.nn.silu(c) @ w_ada + b_ada).reshape(c.shape[0], N_MOD, 1, D_MODEL)
    sh1, sc1, g1, shm, scm, gm, sh2, sc2, g2 = [mod[:, i] for i in range(N_MOD)]
    h = modulate(rms_norm(x, ln_ffn1_pre), sh1, sc1)
    x = x + 0.5 * g1 * rms_norm(swiglu(h, ffn1_w_gate, ffn1_w_up, ffn1_w_down), ln_ffn1_post)
    h = modulate(rms_norm(x, ln_mix_pre), shm, scm)
    proj = h @ w_in
    splits = [LRU_WIDTH, 2 * LRU_WIDTH, 2 * LRU_WIDTH + HG_WIDTH, 2 * LRU_WIDTH + 2 * HG_WIDTH,
              2 * LRU_WIDTH + 3 * HG_WIDTH]
    u_lru, y_lru, q, f_raw, v, g = jnp.split(proj, splits, axis=-1)
    u_conv, conv_new = causal_conv(u_lru, conv0, lru_conv_w, lru_conv_b)
    hs, h_new = rg_lru(u_conv, h0, lru_w_a, lru_b_a, lru_w_x, lru_b_x, lru_lambda, starts_sequence)
    o_lru = hs.astype(x.dtype) * jax.nn.gelu(y_lru, approximate=True)
    o_hg, S_new = hgrn2(q, f_raw, v, g, lb, S0, hg_norm_w)
    mix = jnp.concatenate([o_lru, o_hg.astype(x.dtype)], axis=-1) @ w_out
    x = x + gm * rms_norm(mix, ln_mix_post)
    h = modulate(rms_norm(x, ln_ffn2_pre), sh2, sc2)
    x = x + 0.5 * g2 * rms_norm(swiglu(h, ffn2_w_gate, ffn2_w_up, ffn2_w_down), ln_ffn2_post)
    return x, h_new.astype(x.dtype), conv_new.astype(x.dtype), S_new.astype(x.dtype)


def setup_inputs(seed: int = 0) -> dict:
    key = jax.random.key(seed)
    ks = iter(jax.random.split(key, 48))
    f32 = jnp.float32
    L = DEPTH

    def normal(shape, scale):
        return scale * jax.random.normal(next(ks), shape, f32)

    def gain(shape):
        return 1.0 + 0.05 * jax.random.normal(next(ks), shape, f32)

    s = jax.random.uniform(next(ks), (L, LRU_WIDTH), f32, 0.9, 0.999) ** (1.0 / LRU_C)
    lru_lambda = jnp.log(s) - jnp.log1p(-s)
    return {
        'x_prompt': normal((BATCH, SEQ, D_MODEL), 1.0),
        'x_sample': normal((DEC_BATCH, DEC_SEQ, D_MODEL), 1.0),
        'c_prompt': normal((BATCH, D_MODEL), 1.0),
        'c_sample': normal((DEC_BATCH, D_MODEL), 1.0),
        'state_lru_h': normal((L, DEC_BATCH, LRU_WIDTH), 0.5),
        'state_lru_conv': normal((L, DEC_BATCH, CONV_WIDTH - 1, LRU_WIDTH), 1.0),
        'state_hgrn_S': normal((L, DEC_BATCH, HG_HEADS, HG_HEAD_DIM, HG_HEAD_DIM), 0.5),
        'w_ada': normal((L, D_MODEL, N_MOD * D_MODEL), 0.5 * D_MODEL ** -0.5),
        'b_ada': normal((L, N_MOD * D_MODEL), 0.02),
        'ln_ffn1_pre': gain((L, D_MODEL)),
        'ln_ffn1_post': gain((L, D_MODEL)),
        'ffn1_w_gate': normal((L, D_MODEL, D_FF), D_MODEL ** -0.5),
        'ffn1_w_up': normal((L, D_MODEL, D_FF), D_MODEL ** -0.5),
        'ffn1_w_down': normal((L, D_FF, D_MODEL), D_FF ** -0.5),
        'ln_mix_pre': gain((L, D_MODEL)),
        'ln_mix_post': gain((L, D_MODEL)),
        'w_in': normal((L, D_MODEL, 2 * LRU_WIDTH + 4 * HG_WIDTH), D_MODEL ** -0.5),
        'lru_conv_w': normal((L, CONV_WIDTH, LRU_WIDTH), CONV_WIDTH ** -0.5),
        'lru_conv_b': normal((L, LRU_WIDTH), 0.02),
        'lru_w_a': normal((L, LRU_HEADS, LRU_BLOCK, LRU_BLOCK), LRU_BLOCK ** -0.5),
        'lru_b_a': normal((L, LRU_WIDTH), 0.02),
        'lru_w_x': normal((L, LRU_HEADS, LRU_BLOCK, LRU_BLOCK), LRU_BLOCK ** -0.5),
        'lru_b_x': normal((L, LRU_WIDTH), 0.02),
        'lru_lambda': lru_lambda,
        'hg_lb_logits': normal((L + 1, HG_WIDTH), 0.5),
        'hg_norm_w': gain((L, HG_HEAD_DIM)),
        'w_out': normal((L, D_MIX, D_MODEL), D_MIX ** -0.5),
        'ln_ffn2_pre': gain((L, D_MODEL)),
        'ln_ffn2_post': gain((L, D_MODEL)),
        'ffn2_w_gate': normal((L, D_MODEL, D_FF), D_MODEL ** -0.5),
        'ffn2_w_up': normal((L, D_MODEL, D_FF), D_MODEL ** -0.5),
        'ffn2_w_down': normal((L, D_FF, D_MODEL), D_FF ** -0.5),
    }


def reference(x_prompt, x_sample, c_prompt, c_sample, state_lru_h, state_lru_conv, state_hgrn_S,
              w_ada, b_ada, ln_ffn1_pre, ln_ffn1_post, ffn1_w_gate, ffn1_w_up, ffn1_w_down,
              ln_mix_pre, ln_mix_post, w_in, lru_conv_w, lru_conv_b, lru_w_a, lru_b_a, lru_w_x, lru_b_x,
              lru_lambda, hg_lb_logits, hg_norm_w, w_out, ln_ffn2_pre, ln_ffn2_post,
              ffn2_w_gate, ffn2_w_up, ffn2_w_down):
    dt = x_prompt.dtype
    lbs = jnp.cumsum(jax.nn.softmax(hg_lb_logits.astype(jnp.float32), axis=0), axis=0)[:DEPTH]
    stacked = (w_ada, b_ada, ln_ffn1_pre, ln_ffn1_post, ffn1_w_gate, ffn1_w_up, ffn1_w_down,
               ln_mix_pre, ln_mix_post, w_in, lru_conv_w, lru_conv_b, lru_w_a, lru_b_a, lru_w_x, lru_b_x,
               lru_lambda, hg_norm_w, w_out, ln_ffn2_pre, ln_ffn2_post, ffn2_w_gate, ffn2_w_up, ffn2_w_down)
    B = x_prompt.shape[0]
    xp, xs = x_prompt, x_sample
    ph, pc, pS, sh, sc, sS = [], [], [], [], [], []
    for l in range(DEPTH):
        p = tuple(w[l] for w in stacked)
        h0 = jnp.zeros((B, LRU_WIDTH), dt)
        conv0 = jnp.zeros((B, CONV_WIDTH - 1, LRU_WIDTH), dt)
        S0 = jnp.zeros((B, HG_HEADS, HG_HEAD_DIM, HG_HEAD_DIM), dt)
        xp, h_p, c_p, S_p = layer(xp, c_prompt, h0, conv0, S0, True, lbs[l], p)
        xs, h_s, c_s, S_s = layer(xs, c_sample, state_lru_h[l], state_lru_conv[l], state_hgrn_S[l],
                                  False, lbs[l], p)
        ph.append(h_p); pc.append(c_p); pS.append(S_p)
        sh.append(h_s); sc.append(c_s); sS.append(S_s)
    return (xp, xs, jnp.stack(ph), jnp.stack(pc), jnp.stack(pS), jnp.stack(sh), jnp.stack(sc), jnp.stack(sS))
```

```python
import contextlib
import numpy as np
import concourse.bass as bass
import concourse.mybir as mybir
from concourse.bass_utils import run_bass_kernel_spmd

F32 = mybir.dt.float32
BF16 = mybir.dt.bfloat16
ALU = mybir.AluOpType
AF = mybir.ActivationFunctionType

NCORES = 8
D = 1024
T = 2048
NS = 16
DFF = 2816
NJ = DFF // 128
EPS = 1e-6
CFG = {"order": "ABL", "steps": {"A": 1, "B": 1, "L": 1}, "delay": {"B": 6}}
STAGE = 99

C_LN = 0
C_BADA = 48
C_CONVW = 120
C_CONVB = 136
C_BA = 140
C_BX = 144
C_LAM = 148
C_LB = 152
C_NW = 160
NCST = 161
K_ID = 0
K_MASK = 128
NKC = 192


class Sl:
    def __init__(self, ap, *slots):
        self.ap = ap
        self.slots = slots


def _ap(x):
    return x.ap if isinstance(x, Sl) else x


class _Eng:
    def __init__(self, name, h, sem):
        self.name, self.h, self.sem = name, h, sem
        self.count = 0
        self.seen = {}


class _Chan:
    def __init__(self, name, sem):
        self.name, self.sem = name, sem
        self.count = 0


class KB:
    def __init__(self, nc, es):
        self.nc = nc
        self.es = es
        mk = lambda n: es.enter_context(nc.semaphore(n))
        self.pe = _Eng("pe", nc.tensor, mk("s_pe"))
        self.act = _Eng("act", nc.scalar, mk("s_act"))
        self.dve = _Eng("dve", nc.vector, mk("s_dve"))
        self.pool = _Eng("pool", nc.gpsimd, mk("s_pool"))
        self.sp = _Eng("sp", nc.sync, mk("s_sp"))
        self.chans = {}
        self.res = {}
        self.rr = {}
        self.dry = False

    def chan(self, name):
        if name not in self.chans:
            self.chans[name] = _Chan(name, self.es.enter_context(self.nc.semaphore("c_" + name)))
        return self.chans[name]

    def _keys(self, xs):
        out = []
        for x in xs:
            if x is None or isinstance(x, (int, float)):
                continue
            if isinstance(x, Sl):
                name = getattr(x.ap, "tensor", x.ap).name
                if x.slots and not name.startswith("pb"):
                    out += [(name, s) for s in x.slots]
                else:
                    out.append((name, None))
            else:
                out.append((getattr(x, "tensor", x).name, None))
        return out

    def _conf(self, name, slot):
        d = self.res.setdefault(name, {})
        if slot is None:
            return list(d.values())
        return [d[k] for k in (None, slot) if k in d]

    def _deps(self, eng, R, W):
        need = {}

        def add(src, cnt, raw):
            if src is eng and (eng is self.pe):
                return
            if cnt > need.get(src, 0):
                need[src] = cnt

        for k in R:
            for ent in self._conf(*k):
                if ent[0] is not None:
                    add(ent[0][0], ent[0][1], True)
        for k in W:
            for ent in self._conf(*k):
                if ent[0] is not None:
                    add(ent[0][0], ent[0][1], False)
                for src, cnt in ent[1].items():
                    add(src, cnt, False)
        return need

    def _wait(self, eng, need):
        for src, cnt in need.items():
            if eng.seen.get(src, 0) >= cnt:
                continue
            eng.h.wait_ge(src.sem, cnt * (16 if isinstance(src, _Chan) else 1))
            eng.seen[src] = cnt

    def _register(self, tok, R, W):
        for name, slot in W:
            d = self.res.setdefault(name, {})
            if slot is None:
                d.clear()
            d[slot] = [tok, {}]
        for name, slot in R:
            d = self.res.setdefault(name, {})
            ent = d.get(slot)
            if ent is None:
                ent = d[slot] = [None, {}]
            if tok[1] > ent[1].get(tok[0], 0):
                ent[1][tok[0]] = tok[1]

    def op(self, eng, fn, R=(), W=(), inc=True):
        if self.dry:
            return
        R = self._keys(R)
        W = self._keys(W)
        self._wait(eng, self._deps(eng, R, W))
        ins = fn(eng.h)
        if inc:
            ins.then_inc(eng.sem, 1)
            eng.count += 1
            tok = (eng, eng.count)
        else:
            tok = (eng, eng.count + 1)
        self._register(tok, R, W)

    NCH = {"in": 8, "w": 8, "out": 8, "outp": 8}

    def dma(self, qeng, chan, out, in_, R=(), W=()):
        if self.dry:
            return
        i = self.rr.get(chan, 0)
        self.rr[chan] = i + 1
        ch = self.chan(f"{chan}{i % self.NCH[chan]}")
        R = self._keys(R)
        W = self._keys(W)
        need = self._deps(qeng, R, W)
        if ch.count:
            need[ch] = max(need.get(ch, 0), ch.count)
        self._wait(qeng, need)
        qeng.h.dma_start(out=_ap(out), in_=_ap(in_)).then_inc(ch.sem, 16)
        ch.count += 1
        self._register((ch, ch.count), R, W)

    def mm(self, out, lhsT, rhs, start, stop, inc=None):
        inc = stop if inc is None else inc
        self.op(self.pe, lambda e: e.matmul(_ap(out), _ap(lhsT), _ap(rhs), start=start, stop=stop),
                R=[lhsT, rhs], W=[out], inc=inc)

    def tr(self, out, in_, ident, inc=True):
        self.op(self.pe, lambda e: e.transpose(_ap(out), _ap(in_), _ap(ident)), R=[in_, ident], W=[out], inc=inc)

    def A(self, out, in_, func, bias=None, scale=1.0, eng=None):
        kw = {}
        if bias is not None:
            kw["bias"] = _ap(bias)
        kw["scale"] = _ap(scale)
        self.op(self.act, lambda e: e.activation(out=_ap(out), in_=_ap(in_), func=func, **kw),
                R=[in_, bias, scale], W=[out])

    def tt(self, out, in0, in1, op, eng=None):
        eng = eng or self.dve
        self.op(eng, lambda e: e.tensor_tensor(_ap(out), _ap(in0), _ap(in1), op), R=[in0, in1], W=[out])

    def ts(self, out, in0, s1, op0, s2=None, op1=None, eng=None):
        eng = eng or self.dve
        if op1 is None:
            fn = lambda e: e.tensor_scalar(_ap(out), _ap(in0), _ap(s1), None, op0)
        else:
            fn = lambda e: e.tensor_scalar(_ap(out), _ap(in0), _ap(s1), _ap(s2), op0, op1)
        self.op(eng, fn, R=[in0, s1, s2], W=[out])

    def stt(self, out, in0, sc, in1, op0, op1, eng=None):
        eng = eng or self.dve
        self.op(eng, lambda e: e.scalar_tensor_tensor(_ap(out), _ap(in0), _ap(sc), _ap(in1), op0, op1),
                R=[in0, sc, in1], W=[out])

    def scan(self, out, d0, d1, init, op0=ALU.mult, op1=ALU.add):
        self.op(self.dve, lambda e: e.tensor_tensor_scan(_ap(out), _ap(d0), _ap(d1), _ap(init), op0, op1),
                R=[d0, d1, init], W=[out])

    def cp(self, out, in_, eng=None):
        eng = eng or self.dve
        self.op(eng, lambda e: e.tensor_copy(_ap(out), _ap(in_)), R=[in_], W=[out])

    def rsqrt(self, out, in_, eps_col):
        self.A(out, in_, AF.Ln, bias=eps_col)
        self.A(out, out, AF.Exp, scale=-0.5)

    def finish(self):
        for ch in self.chans.values():
            if self.sp.seen.get(ch, 0) < ch.count:
                self.sp.h.wait_ge(ch.sem, ch.count * 16)
        for e in (self.pe, self.act, self.dve, self.pool):
            if e.count:
                self.sp.h.wait_ge(e.sem, e.count)


def build(stage=STAGE):
    nc = bass.Bass("TRN2", target_bir_lowering=False)
    es = contextlib.ExitStack()
    with es:
        _build(nc, es, stage)
    return nc


def _build(nc, es, stage):
    def DI(name, shape):
        return nc.dram_tensor(name, shape, F32, kind="ExternalInput").ap()

    def DO(name, shape):
        return nc.dram_tensor(name, shape, F32, kind="ExternalOutput").ap()

    xT_d = DI("xT", [D, T])
    xsT_d = DI("xsT", [D, NS])
    cT_d = DI("cT", [D, 17])
    cst_d = DI("cst", [128, NCST])
    kc_d = DI("kc", [128, NKC])
    wada_d = DI("w_ada", [D, 9 * D])
    wg_d = [DI("w_gate1", [D, DFF]), DI("w_gate2", [D, DFF])]
    wu_d = [DI("w_up1", [D, DFF]), DI("w_up2", [D, DFF])]
    wd_d = [DI("w_down1", [DFF, D]), DI("w_down2", [DFF, D])]
    win_d = DI("w_in", [D, 3 * D])
    wout_d = DI("w_out", [D, D])
    wabd_d = DI("w_a_bd", [128, 4, 128])
    wxbd_d = DI("w_x_bd", [128, 4, 128])
    h0T_d = DI("h0T", [512, NS])
    conv0T_d = DI("conv0T", [512, 3, NS])
    S0_d = DI("S0", [NS, 4, 128, 128])

    win_bf = nc.dram_tensor("win_bf16", [24, 128, 8, 128], BF16, kind="Internal").ap()
    wout_bf = nc.dram_tensor("wout_bf16", [8, 128, 8, 128], BF16, kind="Internal").ap()

    S_in_l = nc.dram_tensor("S_in_l", [128, NS * 512], F32, kind="Internal").ap()
    S_out_l = nc.dram_tensor("S_out_l", [128, NS * 512], F32, kind="Internal").ap()

    yT_d = DO("yT", [D, T])
    ysT_d = DO("ysT", [D, NS])
    hP_d = DO("hP", [128, 4])
    convP_d = DO("convP", [128, 4, 3])
    SP_d = DO("SP", [4, 128, 128])
    hS_d = DO("hS", [128, 4, NS])
    convS_d = DO("convS", [128, 4, 3, NS])
    SS_d = DO("SS", [NS, 4, 128, 128])

    kb = KB(nc, es)
    pe, act, dve, pool, sp = kb.pe, kb.act, kb.dve, kb.pool, kb.sp

    def SB(name, shape, dt=F32, stack=es):
        return stack.enter_context(nc.sbuf_tensor(name, shape, dt))

    xT = SB("xT_sb", [128, 8, T])
    xsT = SB("xsT_sb", [128, 8, NS])
    cst = SB("cst_sb", [128, NCST])
    kc = SB("kc_sb", [128, NKC])
    modT = SB("modT", [128, 72, 17])
    Acoef = [SB(f"Acoef{i}", [128, 8, 17]) for i in range(3)]
    Gcoef = [SB(f"Gcoef{i}", [128, 8, 17]) for i in range(3)]
    ones_d = SB("ones_d", [128, 128], BF16)
    ones_h = SB("ones_h", [128, 128], BF16)
    ident_bf = SB("ident_bf", [128, 128], BF16)
    eps_col = SB("eps_col", [128, 1])
    one_col = SB("one_col", [128, 1])
    pb = [es.enter_context(nc.psum_tensor(f"pb{i}", [128, 512], F32)) for i in range(8)]

    xT_dv = xT_d.rearrange("(c p) t -> p c t", p=128)
    yT_dv = yT_d.rearrange("(c p) t -> p c t", p=128)

    kb.dma(sp, "in", cst[:], cst_d, W=[cst])
    kb.dma(sp, "in", kc[:], kc_d, W=[kc])
    cT = SB("cT_sb", [128, 8, 17], F32)
    kb.dma(sp, "in", cT[:], cT_d.rearrange("(c p) t -> p c t", p=128), W=[cT])
    kb.dma(sp, "in", xsT[:], xsT_d.rearrange("(c p) t -> p c t", p=128), W=[xsT])
    for blk in range(4):
        kb.dma(sp, "in", xT[:, :, blk * 512:(blk + 1) * 512], xT_dv[:, :, blk * 512:(blk + 1) * 512],
               W=[Sl(xT, blk)])
    kb.op(dve, lambda e: e.memset(ones_d[:], 1.0 / 1024.0), W=[ones_d])
    kb.op(dve, lambda e: e.memset(ones_h[:], 1.0 / 128.0), W=[ones_h])
    kb.cp(ident_bf[:], kc[:, K_ID:K_ID + 128])
    kb.op(dve, lambda e: e.memset(eps_col[:], EPS), W=[eps_col])
    kb.op(dve, lambda e: e.memset(one_col[:], 1.0), W=[one_col])

    siluT = SB("siluT", [128, 8, 17], BF16)
    kb.A(siluT[:], cT[:], AF.Silu)
    wada_v = wada_d.rearrange("(c p) n -> p c n", p=128)
    mod_state = {"ring": None, "order": list(range(72)), "issued": 0, "done": 0, "nbank": 0}

    def mod_prefetch(upto):
        ring = mod_state["ring"]
        while mod_state["issued"] < min(upto, 72):
            n = mod_state["issued"]
            g = mod_state["order"][n]
            kb.dma(pool, "w", ring[n % len(ring)][:], wada_v[:, :, g * 128:(g + 1) * 128], W=[ring[n % len(ring)]])
            mod_state["issued"] += 1

    def mod_coefs(l, which):
        lnpre = cst[:, C_LN + 16 * l:C_LN + 16 * l + 8, None].broadcast_to([128, 8, 17])
        lnpost = cst[:, C_LN + 16 * l + 8:C_LN + 16 * l + 16, None].broadcast_to([128, 8, 17])
        if which == "A":
            kb.stt(Acoef[l][:], modT[:, (3 * l + 1) * 8:(3 * l + 2) * 8, :], 1.0, lnpre, ALU.add, ALU.mult)
        else:
            kb.stt(Gcoef[l][:], modT[:, (3 * l + 2) * 8:(3 * l + 3) * 8, :], 0.5 if l != 1 else 1.0, lnpost,
                   ALU.mult, ALU.mult)

    def mod_piece(banks):
        n = mod_state["done"]
        if n >= 72:
            return False
        ring = mod_state["ring"]
        mod_prefetch(n + len(ring))
        g = mod_state["order"][n]
        wb = ring[n % len(ring)]
        bank = banks[mod_state["nbank"] % len(banks)]
        mod_state["nbank"] += 1
        for kcx in range(8):
            kb.mm(bank[:, 0:17], wb[:, kcx, :], siluT[:, kcx, :], start=(kcx == 0), stop=(kcx == 7))
        kb.ts(Sl(modT[:, g, :], g), bank[:, 0:17], cst[:, C_BADA + g:C_BADA + g + 1], ALU.add)
        mod_state["done"] += 1
        i = g // 8
        if g % 8 == 7:
            if i % 3 == 1:
                mod_coefs(i // 3, "A")
            elif i % 3 == 2:
                mod_coefs(i // 3, "G")
        return True

    def shift(l):
        return modT[:, (3 * l) * 8:(3 * l + 1) * 8, :]

    def ffn(l, fi, last):
        A, G, sh = Acoef[l], Gcoef[l], shift(l)
        with contextlib.ExitStack() as fs:
            hy = SB(f"hy{fi}", [128, 8192], F32, fs)
            actT = [SB(f"actT{fi}_{i}", [128, NJ, 512], BF16, fs) for i in range(2)]
            actsT = SB(f"actsT{fi}", [128, NJ, NS], BF16, fs)
            wg = [SB(f"wg{fi}_{i}", [128, 8, 256], BF16, fs) for i in range(2)]
            wu = [SB(f"wu{fi}_{i}", [128, 8, 256], BF16, fs) for i in range(2)]
            wd = [SB(f"wd{fi}_{i}", [128, NJ, 128], BF16, fs) for i in range(2)]
            sq = [SB(f"sq{fi}_{i}", [128, 512], BF16, fs) for i in range(4)]
            tmp = [SB(f"tmp{fi}_{i}", [128, 512], F32, fs) for i in range(2)]
            sg = [SB(f"sg{fi}_{i}", [128, 512], F32, fs) for i in range(2)]
            rstd = [SB(f"rstd{fi}_{i}", [128, 512], F32, fs) for i in range(2)]
            hsT = SB(f"hsT{fi}", [128, 8, NS], BF16, fs)
            sm = [SB(f"sm{fi}_{i}", [128, 8, NS], F32, fs) for i in range(3)]
            sqs = SB(f"sqs{fi}", [128, 8, NS], BF16, fs)
            rs = SB(f"rs{fi}", [128, NS], F32, fs)
            sgs = SB(f"sgs{fi}", [128, NS], F32, fs)

            if fi == 0:
                mod_state["ring"] = [SB(f"wada{i}", [128, 8, 128], BF16, fs) for i in range(3)]
                for _ in range(16):
                    mod_piece([pb[5], pb[6], pb[7]])
            hT = [Sl(hy[:, b * 2048:(b + 1) * 2048].bitcast(BF16).rearrange("p (c t) -> p c t", t=512), b)
                  for b in range(2)]
            yv = [Sl(hy[:, 4096:8192].rearrange("p (c t) -> p c t", t=512), 2),
                  Sl(hy[:, 0:4096].rearrange("p (c t) -> p c t", t=512), 0, 1)]
            wg_v = wg_d[fi].rearrange("(c p) n -> p c n", p=128)
            wu_v = wu_d[fi].rearrange("(c p) n -> p c n", p=128)
            wd_v = wd_d[fi].rearrange("(j p) n -> p j n", p=128)
            cnt = {"n": 0, "m": 0, "q": 0, "t": 0, "r": 0}

            def prenorm(blk, bi):
                cols = slice(blk * 512, (blk + 1) * 512)
                for c in range(8):
                    s = sq[cnt["q"] % 4]
                    cnt["q"] += 1
                    kb.A(s[:], Sl(xT[:, c, cols], blk), AF.Square)
                    kb.mm(pb[4][:, :], ones_d[:], s[:], start=(c == 0), stop=(c == 7), inc=True)
                r = rstd[cnt["r"] % 2]
                cnt["r"] += 1
                kb.rsqrt(r[:], pb[4][:, :], eps_col[:])
                for c in range(8):
                    t = tmp[cnt["t"] % 2]
                    cnt["t"] += 1
                    kb.stt(t[:], Sl(xT[:, c, cols], blk), A[:, c, 0:1], r[:], ALU.mult, ALU.mult)
                    kb.A(Sl(hT[bi].ap[:, c, :], bi), t[:], AF.Identity, bias=sh[:, c, 0:1])

            def prenorm_s():
                kb.A(sqs[:], xsT[:], AF.Square)
                for c in range(8):
                    kb.mm(pb[4][:, 0:NS], ones_d[:], sqs[:, c, :], start=(c == 0), stop=(c == 7), inc=True)
                kb.rsqrt(rs[:], pb[4][:, 0:NS], eps_col[:])
                kb.tt(sm[0][:], xsT[:], rs[:, None, :].broadcast_to([128, 8, NS]), ALU.mult)
                kb.tt(sm[1][:], sm[0][:], A[:, :, 1:17], ALU.mult)
                kb.tt(hsT[:], sm[1][:], sh[:, :, 1:17], ALU.add)

            for half in range(2):
                blks = [2 * half, 2 * half + 1]
                if half == 0:
                    prenorm_s()
                for bi, blk in enumerate(blks):
                    prenorm(blk, bi)
                def load_gu(jg_):
                    kb.dma(pool, "w", wg[jg_ % 2][:], wg_v[:, :, jg_ * 256:(jg_ + 1) * 256], W=[wg[jg_ % 2]])
                    kb.dma(pool, "w", wu[jg_ % 2][:], wu_v[:, :, jg_ * 256:(jg_ + 1) * 256], W=[wu[jg_ % 2]])

                load_gu(0)
                for jg in range(NJ // 2):
                    wgb, wub = wg[jg % 2], wu[jg % 2]
                    if fi == 0 and half == 1 and jg < 8:
                        for b_ in (2 * jg, 2 * jg + 1):
                            kb.dma(sp, "in", S_in_l[:, b_ * 512:(b_ + 1) * 512].rearrange("k (h v) -> k h v", v=128),
                                   S0_d[b_].rearrange("h k v -> k h v"), W=[Sl(S_in_l, b_ // 4)])
                    if fi == 0 and half == 1:
                        for p_ in range(2) if jg < 8 else []:
                            i_ = 2 * jg + p_
                            for f_ in range(2):
                                if i_ < 12:
                                    fc_ = 2 * i_ + f_
                                    kb.dma(pool, "w", win_bf[fc_],
                                           win_d[:, fc_ * 128:(fc_ + 1) * 128].rearrange("(c p) n -> p c n", p=128),
                                           W=[Sl(win_bf, fc_)])
                                else:
                                    fc_ = 2 * (i_ - 12) + f_
                                    kb.dma(pool, "w", wout_bf[fc_],
                                           wout_d[:, fc_ * 128:(fc_ + 1) * 128].rearrange("(c p) n -> p c n", p=128),
                                           W=[Sl(wout_bf, fc_)])
                    if jg + 1 < NJ // 2:
                        load_gu(jg + 1)
                    for jj in range(2):
                        j = 2 * jg + jj
                        js = slice(jj * 128, (jj + 1) * 128)
                        for bi in range(2):
                            n = cnt["n"]
                            cnt["n"] += 1
                            pg, pu = pb[n % 2], pb[2 + n % 2]
                            for kcx in range(8):
                                kb.mm(pg[:], wgb[:, kcx, js], Sl(hT[bi].ap[:, kcx, :], bi), kcx == 0, kcx == 7)
                            for kcx in range(8):
                                kb.mm(pu[:], wub[:, kcx, js], Sl(hT[bi].ap[:, kcx, :], bi), kcx == 0, kcx == 7)
                            s = sg[n % 2]
                            kb.A(s[:], pg[:], AF.Silu)
                            kb.tt(Sl(actT[bi][:, j, :], j), s[:], pu[:], ALU.mult)
                            if fi == 0:
                                mod_piece([pb[5], pb[6], pb[7]])
                        if half == 0:
                            o = 64 + 32 * (j % 2)
                            psg = Sl(pb[4][:, o:o + NS], ("sg", j % 2))
                            psu = Sl(pb[4][:, o + NS:o + 2 * NS], ("su", j % 2))
                            for kcx in range(8):
                                kb.mm(psg, wgb[:, kcx, js], hsT[:, kcx, :], kcx == 0, kcx == 7)
                            for kcx in range(8):
                                kb.mm(psu, wub[:, kcx, js], hsT[:, kcx, :], kcx == 0, kcx == 7)
                            kb.A(sgs[:], psg, AF.Silu)
                            kb.tt(Sl(actsT[:, j, :], j), sgs[:], psu, ALU.mult)
                pend = []
                def load_d(dc_):
                    kb.dma(pool, "w", wd[dc_ % 2][:], wd_v[:, :, dc_ * 128:(dc_ + 1) * 128], W=[wd[dc_ % 2]])

                load_d(0)
                for dc in range(8):
                    wdb = wd[dc % 2]
                    if dc + 1 < 8:
                        load_d(dc + 1)
                    for bi in range(2):
                        m = cnt["m"]
                        cnt["m"] += 1
                        py = pb[5 + m % 2]
                        for j in range(NJ):
                            kb.mm(py[:], wdb[:, j, :], Sl(actT[bi][:, j, :], j), j == 0, j == NJ - 1)
                        while pend:
                            pend.pop(0)()
                        kb.A(Sl(yv[bi].ap[:, dc, :], *yv[bi].slots), py[:], AF.Copy)
                        s = sq[cnt["q"] % 4]
                        cnt["q"] += 1
                        kb.A(s[:], py[:], AF.Square)
                        pend.append(lambda s=s, bi=bi, dc=dc: kb.mm(pb[4][:, :] if bi == 0 else pb[7][:, :], ones_d[:],
                                                                   s[:], start=(dc == 0), stop=(dc == 7), inc=True))
                    if half == 0:
                        psy = Sl(pb[3][:, dc * NS:(dc + 1) * NS], ("ys", dc))
                        for j in range(NJ):
                            kb.mm(psy, wdb[:, j, :], Sl(actsT[:, j, :], j), j == 0, j == NJ - 1)
                while pend:
                    pend.pop(0)()
                for bi, blk in enumerate(blks):
                    cols = slice(blk * 512, (blk + 1) * 512)
                    stp = pb[4][:, :] if bi == 0 else pb[7][:, :]
                    r = rstd[cnt["r"] % 2]
                    cnt["r"] += 1
                    kb.rsqrt(r[:], stp, eps_col[:])
                    for dc in range(8):
                        t = tmp[cnt["t"] % 2]
                        cnt["t"] += 1
                        kb.stt(t[:], Sl(yv[bi].ap[:, dc, :], *yv[bi].slots), G[:, dc, 0:1], r[:], ALU.mult, ALU.mult)
                        kb.tt(Sl(xT[:, dc, cols], blk), Sl(xT[:, dc, cols], blk), t[:], ALU.add)
                    if last:
                        kb.dma(sp, "out", yT_dv[:, :, cols], Sl(xT[:, :, cols], blk), R=[Sl(xT, blk)])
                if half == 0:
                    ysv = pb[3][:, 0:8 * NS].rearrange("p (c t) -> p c t", t=NS)
                    kb.cp(sm[0][:], Sl(ysv, *[("ys", dc) for dc in range(8)]))
                    kb.tt(sqs[:], sm[0][:], sm[0][:], ALU.mult)
                    for c in range(8):
                        kb.mm(pb[2][:, 0:NS], ones_d[:], sqs[:, c, :], start=(c == 0), stop=(c == 7), inc=True)
                    kb.rsqrt(rs[:], pb[2][:, 0:NS], eps_col[:])
                    kb.tt(sm[1][:], sm[0][:], rs[:, None, :].broadcast_to([128, 8, NS]), ALU.mult)
                    kb.tt(sm[2][:], sm[1][:], G[:, :, 1:17], ALU.mult)
                    kb.tt(xsT[:], xsT[:], sm[2][:], ALU.add)
                    if last:
                        kb.dma(sp, "out", ysT_d.rearrange("(c p) t -> p c t", p=128), xsT[:], R=[xsT])
            barrier(kb)


    def mixer(last):
        A, G, sh = Acoef[1], Gcoef[1], shift(1)
        with contextlib.ExitStack() as mx:
            NWB = 4
            w_ring = [SB(f"w_ring{i}", [128, 8, 128], BF16, mx) for i in range(NWB)]
            wabd = SB("wabd", [128, 4, 128], BF16, mx)
            wxbd = SB("wxbd", [128, 4, 128], BF16, mx)
            wo = [SB(f"wo{i}", [128, 8, 128], BF16, mx) for i in range(2)]
            dc_ = SB("dcoef", [128, 48], F32, mx)
            sq = [SB(f"msq{i}", [128, 512], BF16, mx) for i in range(4)]
            tmp = [SB(f"mtmp{i}", [128, 512], F32, mx) for i in range(2)]
            rstd = SB("mrstd", [128, 512], F32, mx)
            pr_order = [c2 for c in range(4) for c2 in (c, 4 + c)] + [8 + 4 * i + h for h in range(4) for i in range(4)]
            hg_order = [8 + 4 * i + h for h in range(4) for i in range(4)]
            lru_order = [c2 for c in range(4) for c2 in (c, 4 + c)]
            tb_order = []
            for c in range(4):
                tb_order += [c, 4 + c] + [8 + 4 * i + c for i in range(4)]
            w_seq = []
            dry_order = []
            wst = {"issued": 0, "used": 0}

            def w_prefetch(upto):
                while wst["issued"] < min(upto, len(w_seq)):
                    i = wst["issued"]
                    fc = w_seq[i]
                    kb.dma(sp, "in", w_ring[i % NWB][:], win_bf[fc], R=[Sl(win_bf, fc)], W=[w_ring[i % NWB]])
                    wst["issued"] += 1

            def w_get(fc):
                i = wst["used"]
                assert w_seq[i] == fc, (i, w_seq[i], fc)
                w_prefetch(i + NWB)
                wst["used"] += 1
                return w_ring[i % NWB]

            w_prefetch(NWB)
            kb.dma(pool, "w", wabd[:], wabd_d, W=[wabd])
            kb.dma(pool, "w", wxbd[:], wxbd_d, W=[wxbd])

            lam = cst[:, C_LAM:C_LAM + 4]
            t4 = [SB(f"t4_{i}", [128, 4], F32, mx) for i in range(6)]
            kb.ts(t4[5][:], lam, -1.0, ALU.mult)
            kb.tt(t4[0][:], lam, t4[5][:], ALU.max)
            kb.A(t4[1][:], t4[0][:], AF.Exp, scale=-1.0)
            kb.ts(t4[2][:], t4[1][:], 2.0, ALU.add)
            kb.op(dve, lambda e_: e_.reciprocal(t4[2][:], t4[2][:]), R=[t4[2]], W=[t4[2]])
            kb.tt(t4[2][:], t4[1][:], t4[2][:], ALU.mult)
            kb.tt(t4[3][:], t4[2][:], t4[2][:], ALU.mult)
            kb.ts(t4[4][:], t4[3][:], 1.0 / 11.0, ALU.mult, 1.0 / 9.0, ALU.add)
            for cf in (1.0 / 7.0, 1.0 / 5.0, 1.0 / 3.0, 1.0):
                kb.tt(t4[4][:], t4[4][:], t4[3][:], ALU.mult)
                kb.ts(t4[4][:], t4[4][:], cf, ALU.add)
            kb.tt(t4[4][:], t4[4][:], t4[2][:], ALU.mult)
            kb.ts(t4[5][:], lam, -1.0, ALU.mult, 0.0, ALU.max)
            kb.stt(dc_[:, 0:4], t4[4][:], 2.0, t4[5][:], ALU.mult, ALU.add)
            kb.ts(dc_[:, 4:8], dc_[:, 0:4], -4.0, ALU.mult)
            kb.ts(dc_[:, 8:12], dc_[:, 0:4], -8.0, ALU.mult)
            kb.ts(dc_[:, 12:16], cst[:, C_BA:C_BA + 4], 0.5, ALU.mult)
            kb.ts(dc_[:, 16:20], cst[:, C_BX:C_BX + 4], 0.5, ALU.mult)
            kb.tt(t4[0][:], cst[:, C_LB:C_LB + 4], cst[:, C_LB + 4:C_LB + 8], ALU.subtract)
            kb.A(t4[1][:], t4[0][:], AF.Tanh, scale=0.5)
            kb.ts(dc_[:, 20:24], t4[1][:], 0.5, ALU.mult, 0.5, ALU.add)
            kb.ts(dc_[:, 24:28], dc_[:, 20:24], -0.5, ALU.mult, 0.5, ALU.add)
            kb.tt(dc_[:, 28:32], dc_[:, 20:24], dc_[:, 24:28], ALU.add)
            kb.ts(dc_[:, 32:36], dc_[:, 24:28], -1.0, ALU.mult)
            kb.ts(dc_[:, 36:40], dc_[:, 28:32], -1.0, ALU.mult, 1.0, ALU.add)
            kb.ts(dc_[:, 40:41], cst[:, C_NW:C_NW + 1], 0.5, ALU.mult)
            SP_, M4, M8, HBA, HBX, C1, C0, NC1, OMC0 = (slice(4 * i, 4 * i + 4) for i in range(9))
            SP_, M4, M8, HBA, HBX = slice(0, 4), slice(4, 8), slice(8, 12), slice(12, 16), slice(16, 20)
            C1, C0, NC1, OMC0 = slice(24, 28), slice(28, 32), slice(32, 36), slice(36, 40)
            QS = 128.0 ** -0.5
            GC = 0.7978845608028654

            def bc(ap4, n=NS):
                return ap4[:, :, None].broadcast_to([128, 4, n])

            def wout_phase(mixT_, ncol, ysb_fn, xdst, gsl, rs_t):
                for dcx in range(8):
                    wb = wo[dcx % 2]
                    kb.dma(sp, "in", wb[:], wout_bf[dcx], R=[Sl(wout_bf, dcx)], W=[wb])
                    py = pb[dcx % 2]
                    for kcx in range(8):
                        kb.mm(py[:, 0:ncol], wb[:, kcx, :], mixT_[:, kcx, :], kcx == 0, kcx == 7)
                    kb.A(ysb_fn(dcx), py[:, 0:ncol], AF.Copy)
                    s_ = sq[dcx % 4]
                    kb.A(s_[:, 0:ncol], py[:, 0:ncol], AF.Square)
                    kb.mm(pb[7][:, 0:ncol], ones_d[:], s_[:, 0:ncol], start=(dcx == 0), stop=(dcx == 7), inc=True)
                kb.rsqrt(rs_t, pb[7][:, 0:ncol], eps_col[:])
                for dcx in range(8):
                    t_ = tmp[dcx % 2]
                    if ncol == 512:
                        kb.stt(t_[:], ysb_fn(dcx), G[:, dcx, 0:1], rs_t, ALU.mult, ALU.mult)
                    else:
                        kb.tt(t_[:, 0:ncol], ysb_fn(dcx), rs_t, ALU.mult)
                        kb.tt(t_[:, 0:ncol], t_[:, 0:ncol], G[:, dcx, 1:17], ALU.mult)
                    kb.tt(xdst(dcx), xdst(dcx), t_[:, 0:ncol], ALU.add)

            mixTs = [SB(f"mixT{i}", [128, 8, 512], BF16, mx) for i in range(2)]
            prog = {"lru_done": False}

            def wout_chain(tbp, mixp, gated=True):
                colsp = slice(tbp * 512, (tbp + 1) * 512)
                while gated and not prog["lru_done"]:
                    yield
                pend_ = None
                for dcx in range(8):
                    wb = wo[dcx % 2]
                    kb.dma(sp, "in", wb[:], wout_bf[dcx], R=[Sl(wout_bf, dcx)], W=[wb])
                    for kcx in range(8):
                        kb.mm(pb[6][:, :], wb[:, kcx, :], mixp[:, kcx, :], kcx == 0, kcx == 7)
                    yield
                    if pend_ is not None:
                        pend_()
                    s_ = sq[dcx % 2]
                    kb.A(s_[:], pb[6][:, :], AF.Square)
                    pend_ = (lambda s_=s_, dcx=dcx: kb.mm(pb[7][:, :], ones_d[:], s_[:], start=(dcx == 0),
                                                          stop=(dcx == 7), inc=True))
                    yield
                pend_()
                yield
                kb.rsqrt(rstd[:], pb[7][:, :], eps_col[:])
                yield
                for dcx in range(8):
                    wb = wo[dcx % 2]
                    kb.dma(sp, "in", wb[:], wout_bf[dcx], R=[Sl(wout_bf, dcx)], W=[wb])
                    py = pb[6 + dcx % 2]
                    for kcx in range(8):
                        kb.mm(py[:, :], wb[:, kcx, :], mixp[:, kcx, :], kcx == 0, kcx == 7)
                    yield
                    t_ = tmp[dcx % 2]
                    kb.stt(t_[:], py[:, :], G[:, dcx, 0:1], rstd[:], ALU.mult, ALU.mult)
                    kb.tt(Sl(xT[:, dcx, colsp], tbp), Sl(xT[:, dcx, colsp], tbp), t_[:], ALU.add)
                    yield


            with contextlib.ExitStack() as ps:
                hT = SB("mhT", [128, 8, 512], BF16, ps)
                ub = [SB(f"ub{c}", [128, 515], F32, ps) for c in range(4)]
                hst = SB("hstate", [128, 4], F32, ps)
                cP = SB("convP_sb", [128, 4, 3], F32, ps)
                Tt = [SB(f"T{i}", [128, 512], F32, ps) for i in range(8)]
                TL = [SB(f"TL{i}", [128, 512], F32, ps) for i in range(7)]
                ucb = SB("ucb", [128, 512], BF16, ps)
                S = [SB(f"Sst{h}", [128, 128], F32, ps) for h in range(4)]
                HR = []
                for r_ in range(2):
                    HR.append({
                        "T": Tt[4 * r_:4 * r_ + 4] + [SB(f"TH{r_}_{i}", [128, 512], F32, ps) for i in range(2)],
                        "khT": SB(f"khT{r_}", [128, 512], BF16, ps), "qT": SB(f"qT{r_}", [128, 512], BF16, ps),
                        "vT": SB(f"vT{r_}", [128, 512], BF16, ps),
                        "ktok": SB(f"ktok{r_}", [64, 8, 128], BF16, ps), "vtok": SB(f"vtok{r_}", [64, 8, 128], BF16, ps),
                        "scT": SB(f"scT{r_}", [64, 8, 64], BF16, ps), "hsq": SB(f"hsq{r_}", [128, 512], BF16, ps),
                        "Zs": SB(f"Zs{r_}", [128, 128, 8], F32, ps), "AzB": SB(f"AzB{r_}", [128, 64, 8], F32, ps),
                        "Az": SB(f"Az{r_}", [128, 8], F32, ps), "Sbf": SB(f"Sbf{r_}", [128, 9, 128], BF16, ps),
                        "banks": (pb[3 * r_], pb[3 * r_ + 1], pb[3 * r_ + 2]),
                    })
                for c in range(4):
                    kb.op(dve, lambda e_, c=c: e_.memset(ub[c][:, 0:3], 0.0), W=[ub[c]])
                    kb.op(dve, lambda e_, c=c: e_.memset(S[c][:], 0.0), W=[S[c]])
                kb.op(dve, lambda e_: e_.memset(hst[:], 0.0), W=[hst])
                maskb = kc[0:64, None, K_MASK:K_MASK + 64].broadcast_to([64, 8, 64])
                for tb in range(4):
                    cols = slice(tb * 512, (tb + 1) * 512)
                    mixT = mixTs[tb % 2]
                    for c in range(8):
                        s_ = sq[c % 4]
                        kb.A(s_[:], Sl(xT[:, c, cols], tb), AF.Square)
                        kb.mm(pb[7][:, :], ones_d[:], s_[:], start=(c == 0), stop=(c == 7), inc=True)
                    kb.rsqrt(rstd[:], pb[7][:, :], eps_col[:])
                    for c in range(8):
                        t_ = tmp[c % 2]
                        kb.stt(t_[:], Sl(xT[:, c, cols], tb), A[:, c, 0:1], rstd[:], ALU.mult, ALU.mult)
                        kb.A(hT[:, c, :], t_[:], AF.Identity, bias=sh[:, c, 0:1])

                    def proj(pdst, fc):
                        if kb.dry:
                            dry_order.append(fc)
                            return
                        wb_ = w_get(fc)
                        for kcx in range(8):
                            kb.mm(pdst[:], wb_[:, kcx, :], hT[:, kcx, :], kcx == 0, kcx == 7)

                    def lru_chain():
                        for c in range(4):
                            uc, thr, a_, a2, thi, hs, ge = TL
                            proj(pb[6], c)
                            yield
                            proj(pb[7], 4 + c)
                            yield
                            kb.A(ub[c][:, 3:515], pb[6][:], AF.Copy)
                            yield
                            kb.A(ge[:], pb[7][:], AF.Gelu_apprx_tanh)
                            yield
                            cw_ = lambda k: cst[:, C_CONVW + 4 * c + k:C_CONVW + 4 * c + k + 1]
                            kb.ts(uc[:], ub[c][:, 3:515], cw_(3), ALU.mult, cst[:, C_CONVB + c:C_CONVB + c + 1], ALU.add,
                                  eng=pool)
                            for k in range(3):
                                kb.ts(hs[:], ub[c][:, k:k + 512], cw_(k), ALU.mult, 0.0, ALU.add, eng=pool)
                                kb.tt(uc[:], uc[:], hs[:], ALU.add, eng=pool)
                            yield
                            if tb == 3:
                                kb.cp(cP[:, c, :], ub[c][:, 512:515])
                            else:
                                kb.cp(ub[c][:, 0:3], ub[c][:, 512:515])
                            kb.A(ucb[:], uc[:], AF.Copy)
                            yield
                            kb.mm(pb[6][:], wabd[:, c, :], ucb[:], True, True)
                            yield
                            kb.mm(pb[7][:], wxbd[:, c, :], ucb[:], True, True)
                            yield
                            kb.A(thr[:], pb[6][:], AF.Tanh, bias=dc_[:, 12 + c:13 + c], scale=0.5)
                            yield
                            kb.A(thi[:], pb[7][:], AF.Tanh, bias=dc_[:, 16 + c:17 + c], scale=0.5)
                            yield
                            kb.A(a_[:], thr[:], AF.Exp, bias=dc_[:, 4 + c:5 + c], scale=dc_[:, 4 + c:5 + c])
                            yield
                            kb.A(a2[:], thr[:], AF.Exp, bias=dc_[:, 8 + c:9 + c], scale=dc_[:, 8 + c:9 + c])
                            yield
                            kb.A(a2[:], a2[:], AF.Ln, bias=one_col[:], scale=-1.0)
                            yield
                            kb.A(a2[:], a2[:], AF.Exp, scale=0.5)
                            kb.stt(thi[:], thi[:], 1.0, uc[:], ALU.add, ALU.mult)
                            kb.stt(thr[:], a2[:], 0.5, thi[:], ALU.mult, ALU.mult)
                            if tb == 0:
                                kb.ts(thr[:, 0:1], thi[:, 0:1], 0.5, ALU.mult)
                            kb.scan(hs[:], a_[:], thr[:], hst[:, c:c + 1])
                            yield
                            kb.cp(hst[:, c:c + 1], hs[:, 511:512])
                            yield
                            kb.tt(mixT[:, c, :], ge[:], hs[:], ALU.mult)
                            yield
                        prog["lru_done"] = True
                    def hgrn_chain(heads, R):
                        thf, d0, kk, P_, gq, d1 = R["T"]
                        osb_, t1 = kk, thf
                        khT, qT, vT, ktok, vtok, scT = R["khT"], R["qT"], R["vT"], R["ktok"], R["vtok"], R["scT"]
                        Zs, AzB, Az, Sb, hsq = R["Zs"], R["AzB"], R["Az"], R["Sbf"], R["hsq"]
                        bX, bY, bZ = R["banks"]
                        kb.op(dve, lambda e_: e_.memset(d1[:], 0.0), W=[d1])
                        for hd in heads:
                            proj(bX, 8 + hd)
                            yield
                            proj(bY, 12 + hd)
                            yield
                            proj(bZ, 16 + hd)
                            yield
                            kb.A(thf[:], bY[:], AF.Tanh, scale=0.5)
                            yield
                            proj(bY, 20 + hd)
                            yield
                            kb.A(vT[:], bZ[:], AF.Copy)
                            yield
                            kb.A(gq[:], bY[:], AF.Tanh, scale=0.5)
                            yield
                            kb.ts(d0[:], thf[:], dc_[:, 24 + hd:25 + hd], ALU.mult, dc_[:, 28 + hd:29 + hd], ALU.add,
                                  eng=pool)
                            kb.ts(kk[:], thf[:], dc_[:, 32 + hd:33 + hd], ALU.mult, dc_[:, 36 + hd:37 + hd], ALU.add,
                                  eng=pool)
                            yield
                            kb.stt(gq[:], gq[:], 1.0, bY[:], ALU.add, ALU.mult)
                            kb.cp(d1[:, 0:512:64], d0[:, 0:512:64])
                            kb.op(dve, lambda e_: e_.memset(d0[:, 0:512:64], 0.0), W=[d0])
                            yield
                            kb.scan(P_[:], d0[:], d1[:], 0.0)
                            yield
                            kb.op(dve, lambda e_: e_.reciprocal(d0[:], P_[:]), R=[P_], W=[d0])
                            yield
                            kb.tt(khT[:], kk[:], d0[:], ALU.mult)
                            yield
                            kb.stt(qT[:], bX[:], QS, P_[:], ALU.mult, ALU.mult)
                            yield
                            ptk = bY[:, :].bitcast(BF16)
                            ptv = bZ[:, :].bitcast(BF16)
                            for cc in range(8):
                                kb.tr(ptk[0:64, cc * 128:(cc + 1) * 128], khT[:, cc * 64:(cc + 1) * 64], ident_bf[:],
                                      inc=(cc == 7))
                            yield
                            for cc in range(8):
                                kb.tr(ptv[0:64, cc * 128:(cc + 1) * 128], vT[:, cc * 64:(cc + 1) * 64], ident_bf[:],
                                      inc=(cc == 7))
                            yield
                            kb.cp(ktok[:], ptk[0:64, :].rearrange("p (c k) -> p c k", k=128))
                            yield
                            kb.A(vtok[:], ptv[0:64, :].rearrange("p (c k) -> p c k", k=128), AF.Copy)
                            yield
                            for cc in range(8):
                                kb.mm(bX[0:64, cc * 64:(cc + 1) * 64], khT[:, cc * 64:(cc + 1) * 64],
                                      qT[:, cc * 64:(cc + 1) * 64], True, True, inc=(cc == 7))
                            yield
                            kb.tt(scT[:], bX[0:64, :].rearrange("p (c t) -> p c t", t=64), maskb, ALU.mult)
                            yield
                            for cc in range(8):
                                pd = (bY, bZ)[cc // 4]
                                o_ = (cc % 4) * 128
                                kb.mm(pd[:, o_:o_ + 128], ktok[:, cc, :], vtok[:, cc, :], True, True, inc=(cc % 4 == 3))
                            yield
                            Av = P_[:, 63:512:64]
                            kb.cp(Az[:], Av)
                            kb.op(dve, lambda e_: e_.memset(Az[:, 0:1], 0.0), W=[Az])
                            yield
                            kb.A(AzB[:], Az[:, None, :].broadcast_to([128, 64, 8]), AF.Copy)
                            yield
                            for b2 in range(2):
                                kb.tt(Zs[:].rearrange("k v c -> k c v")[:, 4 * b2:4 * b2 + 4, :],
                                      (bY, bZ)[b2][:, :].rearrange("k (c v) -> k c v", v=128),
                                      Av[:, 4 * b2:4 * b2 + 4, None].broadcast_to([128, 4, 128]), ALU.mult)
                                yield
                            kb.stt(Zs[:, :, 0], S[hd][:], Av[:, 0:1], Zs[:, :, 0], ALU.mult, ALU.add)
                            kb.cp(Sb[:, 0, :], S[hd][:])
                            yield
                            for vh in range(2):
                                zz = Zs[:, vh * 64:(vh + 1) * 64, :].rearrange("k v c -> k (v c)")
                                kb.scan(zz, AzB[:].rearrange("k v c -> k (v c)"), zz, 0.0)
                                yield
                            kb.cp(S[hd][:], Zs[:, :, 7])
                            yield
                            kb.A(Sb[:, 1:9, :], Zs[:].rearrange("k v c -> k c v"), AF.Copy)
                            yield
                            for cc in range(8):
                                po_ = bX[:, cc * 64:(cc + 1) * 64]
                                kb.mm(po_, Sb[:, cc, :], qT[:, cc * 64:(cc + 1) * 64], True, False, inc=False)
                                kb.mm(po_, vtok[:, cc, :], scT[:, cc, :], False, True, inc=(cc == 7))
                            yield
                            kb.A(osb_[:], bX[:], AF.Copy)
                            yield
                            kb.A(hsq[:], bX[:], AF.Square)
                            yield
                            kb.mm(bY[:], ones_h[:], hsq[:], True, True)
                            yield
                            kb.rsqrt(t1[:], bY[:], eps_col[:])
                            yield
                            kb.stt(osb_[:], osb_[:], dc_[:, 40:41], t1[:], ALU.mult, ALU.mult)
                            yield
                            kb.tt(mixT[:, 4 + hd, :], osb_[:], gq[:], ALU.mult)
                            yield

                    def run_chains():
                        prog["lru_done"] = False
                        gens = {"A": hgrn_chain((0, 1), HR[0]), "B": hgrn_chain((2, 3), HR[1]), "L": lru_chain()}
                        live = [c_ for c_ in CFG["order"]]
                        if tb > 0:
                            gens["W"] = wout_chain(tb - 1, mixTs[(tb - 1) % 2])
                            live.append("W")
                        rnd = 0
                        while live:
                            for c_ in list(live):
                                if rnd < CFG["delay"].get(c_, 0):
                                    continue
                                try:
                                    for _ in range(CFG["steps"].get(c_, 1)):
                                        next(gens[c_])
                                except StopIteration:
                                    live.remove(c_)
                            rnd += 1

                    if tb == 0:
                        kb.dry = True
                        run_chains()
                        kb.dry = False
                        w_seq.extend(dry_order * 4 + list(range(24)))
                    run_chains()
                    if last:
                        kb.dma(sp, "out", yT_dv[:, :, cols], Sl(xT[:, :, cols], tb), R=[Sl(xT, tb)])
                kb.dma(sp, "out", hP_d, hst[:], R=[hst])
                kb.dma(sp, "out", convP_d, cP[:], R=[cP])
                for hd in range(4):
                    kb.dma(sp, "out", SP_d[hd], S[hd][:], R=[S[hd]])
                if last:
                    kb.dma(sp, "out", ysT_d.rearrange("(c p) t -> p c t", p=128), xsT[:], R=[xsT])
                barrier(kb)
            with contextlib.ExitStack() as ss:
                Ssb = SB("Ssb", [128, NS, 4, 128], F32, ss)
                Sbf = SB("Sbf_s", [128, NS, 4, 128], BF16, ss)
                h0 = SB("h0_s", [128, 4, NS], F32, ss)
                c0s = SB("c0_s", [128, 4, 3, NS], F32, ss)
                cnew = SB("cnew_s", [128, 4, 3, NS], F32, ss)
                hsT = SB("mhsT", [128, 8, NS], BF16, ss)
                mixs = SB("mixs", [128, 8, NS], BF16, ss)
                ps_sb = SB("ps_sb", [128, 24, NS], F32, ss)
                sm = [SB(f"msm{i}", [128, 8, NS], F32, ss) for i in range(3)]
                sqs = SB("msqs", [128, 8, NS], BF16, ss)
                rs = SB("mrs", [128, NS], F32, ss)
                e = [SB(f"e{i}", [128, 4, NS], F32, ss) for i in range(10)]
                eb = [SB(f"eb{i}", [128, 4, NS], BF16, ss) for i in range(3)]
                kvt = SB("kvt", [16, 8, 128], BF16, ss)
                vbd = [SB("vbd0", [16, NS, 128], BF16, ss)] * 2
                osb = SB("osb", [128, 4, NS], F32, ss)
                ys_s = SB("ys_s", [128, 8, NS], F32, ss)
                wg3 = wout_chain(3, mixTs[1], gated=False)

                def adv(n_=1):
                    for _ in range(n_):
                        next(wg3, None)

                kb.dma(sp, "in", h0[:], h0T_d.rearrange("(c p) b -> p c b", p=128), W=[h0])
                kb.dma(sp, "in", c0s[:], conv0T_d.rearrange("(c p) k b -> p c k b", p=128), W=[c0s])
                for g_ in range(4):
                    kb.dma(sp, "in", Ssb[:, 4 * g_:4 * g_ + 4, :, :],
                           S_in_l[:, g_ * 2048:(g_ + 1) * 2048].rearrange("k (b h v) -> k b h v", b=4, v=128),
                           R=[Sl(S_in_l, g_)], W=[Sl(Ssb, *range(4 * g_, 4 * g_ + 4))])
                kb.A(sqs[:], xsT[:], AF.Square)
                for c in range(8):
                    kb.mm(pb[4][:, 0:NS], ones_d[:], sqs[:, c, :], start=(c == 0), stop=(c == 7), inc=True)
                kb.rsqrt(rs[:], pb[4][:, 0:NS], eps_col[:])
                kb.tt(sm[0][:], xsT[:], rs[:, None, :].broadcast_to([128, 8, NS]), ALU.mult)
                kb.tt(sm[1][:], sm[0][:], A[:, :, 1:17], ALU.mult)
                kb.tt(hsT[:], sm[1][:], sh[:, :, 1:17], ALU.add)
                for fc in range(24):
                    wb_ = w_get(fc)
                    for kcx in range(8):
                        kb.mm(pb[5][:, fc * NS:(fc + 1) * NS], wb_[:, kcx, :], hsT[:, kcx, :], kcx == 0, kcx == 7)
                    adv(1)
                kb.cp(ps_sb[:], pb[5][:, 0:24 * NS].rearrange("p (f t) -> p f t", t=NS))
                u, yv = ps_sb[:, 0:4, :], ps_sb[:, 4:8, :]
                q, fr, v, gg = ps_sb[:, 8:12, :], ps_sb[:, 12:16, :], ps_sb[:, 16:20, :], ps_sb[:, 20:24, :]
                cw = lambda k: cst[:, C_CONVW + k:C_CONVW + 16:4]
                uc = e[0]
                kb.tt(uc[:], u, bc(cw(3)), ALU.mult)
                kb.tt(uc[:], uc[:], bc(cst[:, C_CONVB:C_CONVB + 4]), ALU.add)
                for k in range(3):
                    kb.tt(e[1][:], c0s[:, :, k, :], bc(cw(k)), ALU.mult)
                    kb.tt(uc[:], uc[:], e[1][:], ALU.add)
                kb.cp(cnew[:, :, 0:2, :], c0s[:, :, 1:3, :])
                kb.cp(cnew[:, :, 2, :], u)
                kb.dma(pool, "outp", convS_d, cnew[:], R=[cnew])
                kb.cp(eb[0][:], uc[:])
                for c in range(4):
                    kb.mm(pb[5][:, c * NS:(c + 1) * NS], wabd[:, c, :], eb[0][:, c, :], True, True)
                for c in range(4):
                    kb.mm(pb[5][:, (4 + c) * NS:(5 + c) * NS], wxbd[:, c, :], eb[0][:, c, :], True, True)
                pa = pb[5][:, 0:4 * NS].rearrange("p (c t) -> p c t", t=NS)
                px = pb[5][:, 4 * NS:8 * NS].rearrange("p (c t) -> p c t", t=NS)
                kb.stt(e[1][:], pa, 0.5, bc(dc_[:, HBA]), ALU.mult, ALU.add)
                kb.A(e[1][:], e[1][:], AF.Tanh)
                kb.stt(e[2][:], px, 0.5, bc(dc_[:, HBX]), ALU.mult, ALU.add)
                kb.A(e[2][:], e[2][:], AF.Tanh)
                kb.stt(e[3][:], e[1][:], 1.0, bc(dc_[:, M4]), ALU.add, ALU.mult)
                kb.A(e[3][:], e[3][:], AF.Exp)
                kb.tt(e[4][:], e[3][:], e[3][:], ALU.mult)
                kb.ts(e[4][:], e[4][:], -1.0, ALU.mult, 1.0, ALU.add)
                kb.A(e[4][:], e[4][:], AF.Ln)
                kb.A(e[4][:], e[4][:], AF.Exp, scale=0.5)
                kb.stt(e[5][:], e[2][:], 1.0, uc[:], ALU.add, ALU.mult)
                kb.stt(e[5][:], e[4][:], 0.5, e[5][:], ALU.mult, ALU.mult)
                kb.tt(e[6][:], e[3][:], h0[:], ALU.mult)
                kb.tt(e[6][:], e[6][:], e[5][:], ALU.add)
                kb.dma(pool, "outp", hS_d, e[6][:], R=[e[6]])
                kb.tt(e[7][:], yv, yv, ALU.mult)
                kb.ts(e[7][:], e[7][:], 0.044715, ALU.mult, 1.0, ALU.add)
                kb.tt(e[7][:], e[7][:], yv, ALU.mult)
                kb.A(e[7][:], e[7][:], AF.Tanh, scale=GC)
                kb.stt(e[7][:], e[7][:], 1.0, yv, ALU.add, ALU.mult)
                kb.stt(mixs[:, 0:4, :], e[7][:], 0.5, e[6][:], ALU.mult, ALU.mult)
                kb.A(e[1][:], fr, AF.Tanh, scale=0.5)
                kb.tt(e[2][:], e[1][:], bc(dc_[:, C1]), ALU.mult)
                kb.tt(e[2][:], e[2][:], bc(dc_[:, C0]), ALU.add)
                kb.ts(e[3][:], e[2][:], -1.0, ALU.mult, 1.0, ALU.add)
                kb.cp(eb[0][:], e[3][:])
                kb.cp(eb[1][:], v)
                kb.ts(eb[2][:], q, QS, ALU.mult)
                ptb = pb[0][:, :].bitcast(BF16)
                for hd in range(4):
                    kb.tr(ptb[0:NS, hd * 128:(hd + 1) * 128], eb[0][:, hd, :], ident_bf[:])
                    kb.tr(ptb[0:NS, (4 + hd) * 128:(5 + hd) * 128], eb[1][:, hd, :], ident_bf[:])
                kb.cp(kvt[:], ptb[0:NS, :].rearrange("p (g k) -> p g k", k=128))
                n_ = 0
                for hd in range(4):
                    vb = vbd[hd % 2]
                    kb.tt(vb[:], kvt[:, 4 + hd, None, :].broadcast_to([NS, NS, 128]),
                          kc[0:NS, K_ID:K_ID + NS, None].broadcast_to([NS, NS, 128]), ALU.mult)
                    for bg in range(4):
                        pp = pb[1 + n_ % 2]
                        n_ += 1
                        kb.mm(pp[:], kvt[:, hd, :], vb[:, 4 * bg:4 * bg + 4, :], True, True)
                        ssl = Ssb[:, 4 * bg:4 * bg + 4, hd, :]
                        fb = e[2][:, hd, 4 * bg:4 * bg + 4, None].broadcast_to([128, 4, 128])
                        kb.tt(Sl(ssl, *range(4 * bg, 4 * bg + 4)), Sl(ssl, *range(4 * bg, 4 * bg + 4)), fb, ALU.mult)
                        kb.tt(Sl(ssl, *range(4 * bg, 4 * bg + 4)), Sl(ssl, *range(4 * bg, 4 * bg + 4)),
                              pp[:].rearrange("p (b v) -> p b v", v=128), ALU.add)
                        adv(1)
                for g_ in range(4):
                    kb.dma(pool, "outp", S_out_l[:, g_ * 2048:(g_ + 1) * 2048].rearrange("k (b h v) -> k b h v", b=4, v=128),
                           Ssb[:, 4 * g_:4 * g_ + 4, :, :], R=[Sl(Ssb, *range(4 * g_, 4 * g_ + 4))], W=[Sl(S_out_l, g_)])
                for b in range(NS):
                    kb.dma(pool, "outp", SS_d[b].rearrange("h k v -> k h v"),
                           S_out_l[:, b * 512:(b + 1) * 512].rearrange("k (h v) -> k h v", v=128), R=[Sl(S_out_l, b // 4)])
                    kb.A(Sl(Sbf[:, b, :, :], b), Sl(Ssb[:, b, :, :], b), AF.Copy)
                for hd in range(4):
                    for b in range(NS):
                        kb.mm(pb[3][:, hd * NS + b:hd * NS + b + 1], Sl(Sbf[:, b, hd, :], b), eb[2][:, hd, b:b + 1],
                              True, True)
                po = pb[3][:, 0:4 * NS].rearrange("p (h b) -> p h b", b=NS)
                kb.cp(osb[:], po)
                kb.tt(eb[0][:], osb[:], osb[:], ALU.mult)
                kb.mm(pb[4][:, 0:4 * NS], ones_h[:], eb[0][:].rearrange("p h b -> p (h b)"), True, True)
                kb.rsqrt(e[4][:], pb[4][:, 0:4 * NS].rearrange("p (h b) -> p h b", b=NS), eps_col[:])
                kb.A(e[5][:], gg, AF.Tanh, scale=0.5)
                kb.stt(e[5][:], e[5][:], 1.0, gg, ALU.add, ALU.mult)
                kb.stt(e[6][:], osb[:], dc_[:, 40:41], e[4][:], ALU.mult, ALU.mult)
                kb.tt(mixs[:, 4:8, :], e[6][:], e[5][:], ALU.mult)
                for _ in wg3:
                    pass
                wout_phase(mixs, NS, lambda dcx: ys_s[:, dcx, :], lambda dcx: xsT[:, dcx, :], None, rs[:])
                barrier(kb)

            barrier(kb)

    ffn(0, 0, last=(stage == 1))
    if stage >= 2:
        mixer(last=(stage == 2))
    if stage >= 3:
        ffn(2, 1, last=True)
    kb.finish()


def barrier(kb):
    engs = [kb.pe, kb.act, kb.dve, kb.pool, kb.sp]
    for e in engs:
        for o in engs:
            if o is not e and o.count and e.seen.get(o, 0) < o.count:
                e.h.wait_ge(o.sem, o.count)
                e.seen[o] = o.count
        for ch in kb.chans.values():
            if ch.count and e.seen.get(ch, 0) < ch.count:
                e.h.wait_ge(ch.sem, ch.count * 16)
                e.seen[ch] = ch.count
    kb.res.clear()


def _consts():
    kcv = np.zeros((128, NKC), np.float32)
    kcv[:, K_ID:K_ID + 128] = np.eye(128, dtype=np.float32)
    s = np.arange(64)[:, None]
    t = np.arange(64)[None, :]
    kcv[:64, K_MASK:K_MASK + 64] = (s <= t).astype(np.float32)
    return kcv


def _col(v, n):
    return np.ascontiguousarray(np.asarray(v, np.float32).reshape(n, 128).T)


def make_in_maps(inp):
    f = lambda k: np.asarray(inp[k], np.float32)
    cst = np.zeros((128, NCST), np.float32)
    for i, k in enumerate(["ln_ffn1_pre", "ln_ffn1_post", "ln_mix_pre", "ln_mix_post", "ln_ffn2_pre", "ln_ffn2_post"]):
        cst[:, C_LN + 8 * i:C_LN + 8 * i + 8] = _col(f(k)[0], 8)
    cst[:, C_BADA:C_BADA + 72] = _col(f("b_ada")[0], 72)
    cw = f("lru_conv_w")[0]
    for c in range(4):
        for k in range(4):
            cst[:, C_CONVW + c * 4 + k] = cw[k, c * 128:(c + 1) * 128]
    cst[:, C_CONVB:C_CONVB + 4] = _col(f("lru_conv_b")[0], 4)
    cst[:, C_BA:C_BA + 4] = _col(f("lru_b_a")[0], 4)
    cst[:, C_BX:C_BX + 4] = _col(f("lru_b_x")[0], 4)
    cst[:, C_LAM:C_LAM + 4] = _col(f("lru_lambda")[0], 4)
    lbl = f("hg_lb_logits")
    for r in range(2):
        cst[:, C_LB + r * 4:C_LB + r * 4 + 4] = _col(lbl[r], 4)
    cst[:, C_NW] = f("hg_norm_w")[0]
    kcv = _consts()

    def bd(w):
        o = np.zeros((128, 4, 128), np.float32)
        for c in range(4):
            o[0:64, c, 0:64] = w[2 * c]
            o[64:128, c, 64:128] = w[2 * c + 1]
        return o

    shared = {
        "cst": cst, "kc": kcv,
        "w_ada": np.ascontiguousarray(f("w_ada")[0]),
        "w_gate1": np.ascontiguousarray(f("ffn1_w_gate")[0]), "w_up1": np.ascontiguousarray(f("ffn1_w_up")[0]),
        "w_down1": np.ascontiguousarray(f("ffn1_w_down")[0]),
        "w_gate2": np.ascontiguousarray(f("ffn2_w_gate")[0]), "w_up2": np.ascontiguousarray(f("ffn2_w_up")[0]),
        "w_down2": np.ascontiguousarray(f("ffn2_w_down")[0]),
        "w_in": np.ascontiguousarray(f("w_in")[0]), "w_out": np.ascontiguousarray(f("w_out")[0]),
        "w_a_bd": bd(f("lru_w_a")[0]), "w_x_bd": bd(f("lru_w_x")[0]),
    }
    xp, xs = f("x_prompt"), f("x_sample")
    cp, cs = f("c_prompt"), f("c_sample")
    sh, scv, sS = f("state_lru_h")[0], f("state_lru_conv")[0], f("state_hgrn_S")[0]
    maps = []
    for b in range(NCORES):
        rows = slice(NS * b, NS * (b + 1))
        m = dict(shared)
        m["xT"] = np.ascontiguousarray(xp[b].T)
        m["xsT"] = np.ascontiguousarray(xs[rows, 0, :].T)
        m["cT"] = np.ascontiguousarray(np.concatenate([cp[b:b + 1], cs[rows]], axis=0).T)
        m["h0T"] = np.ascontiguousarray(sh[rows].T)
        m["conv0T"] = np.ascontiguousarray(scv[rows].transpose(2, 1, 0))
        m["S0"] = np.ascontiguousarray(sS[rows])
        maps.append(m)
    return maps


_NC_CACHE = {}


def run(inp, stage=STAGE):
    if stage not in _NC_CACHE:
        _NC_CACHE[stage] = build(stage)
    nc = _NC_CACHE[stage]
    maps = make_in_maps(inp)
    res = run_bass_kernel_spmd(nc, maps, core_ids=list(range(NCORES)))
    return res.results


def kernel(**inp):
    rs = run(inp)
    y = np.stack([r["yT"].T for r in rs]).astype(np.float32)
    ys = np.concatenate([r["ysT"].T for r in rs])[:, None, :].astype(np.float32)
    hP = np.stack([r["hP"].T.reshape(512) for r in rs])[None]
    cP = np.stack([r["convP"].transpose(2, 1, 0).reshape(3, 512) for r in rs])[None]
    SPo = np.stack([r["SP"] for r in rs])[None]
    hS = np.concatenate([r["hS"].transpose(2, 1, 0).reshape(NS, 512) for r in rs])[None]
    cS = np.concatenate([r["convS"].transpose(3, 2, 1, 0).reshape(NS, 3, 512) for r in rs])[None]
    SSo = np.concatenate([r["SS"] for r in rs])[None]
    f32 = lambda a: np.ascontiguousarray(a, dtype=np.float32)
    return (f32(y), f32(ys), f32(hP), f32(cP), f32(SPo), f32(hS), f32(cS), f32(SSo))
```

```python
import contextlib
import numpy as np
import concourse.bass as bass
import concourse.mybir as mybir
from concourse.bass_utils import run_bass_kernel_spmd

F32 = mybir.dt.float32
BF16 = mybir.dt.bfloat16
ALU = mybir.AluOpType
AF = mybir.ActivationFunctionType

NCORES = 8
D = 1024
T = 2048
NS = 16
DFF = 2816
NJ = DFF // 128
EPS = 1e-6
CFG = {"order": "ABL", "steps": {"A": 1, "B": 1, "L": 1}, "delay": {"B": 6}}
STAGE = 99

C_LN = 0
C_BADA = 48
C_CONVW = 120
C_CONVB = 136
C_BA = 140
C_BX = 144
C_LAM = 148
C_LB = 152
C_NW = 160
NCST = 161
K_ID = 0
K_MASK = 128
NKC = 192


class Sl:
    def __init__(self, ap, *slots):
        self.ap = ap
        self.slots = slots


def _ap(x):
    return x.ap if isinstance(x, Sl) else x


class _Eng:
    def __init__(self, name, h, sem):
        self.name, self.h, self.sem = name, h, sem
        self.count = 0
        self.seen = {}


class _Chan:
    def __init__(self, name, sem):
        self.name, self.sem = name, sem
        self.count = 0


class KB:
    def __init__(self, nc, es):
        self.nc = nc
        self.es = es
        mk = lambda n: es.enter_context(nc.semaphore(n))
        self.pe = _Eng("pe", nc.tensor, mk("s_pe"))
        self.act = _Eng("act", nc.scalar, mk("s_act"))
        self.dve = _Eng("dve", nc.vector, mk("s_dve"))
        self.pool = _Eng("pool", nc.gpsimd, mk("s_pool"))
        self.sp = _Eng("sp", nc.sync, mk("s_sp"))
        self.chans = {}
        self.res = {}
        self.rr = {}
        self.dry = False
        self.hook = None
        self.hook_every = 3
        self._in_hook = False
        self._hcnt = 0

    def chan(self, name):
        if name not in self.chans:
            self.chans[name] = _Chan(name, self.es.enter_context(self.nc.semaphore("c_" + name)))
        return self.chans[name]

    def _keys(self, xs):
        out = []
        for x in xs:
            if x is None or isinstance(x, (int, float)):
                continue
            if isinstance(x, Sl):
                name = getattr(x.ap, "tensor", x.ap).name
                if x.slots and not name.startswith("pb"):
                    out += [(name, s) for s in x.slots]
                else:
                    out.append((name, None))
            else:
                out.append((getattr(x, "tensor", x).name, None))
        return out

    def _conf(self, name, slot):
        d = self.res.setdefault(name, {})
        if slot is None:
            return list(d.values())
        return [d[k] for k in (None, slot) if k in d]

    def _deps(self, eng, R, W):
        need = {}

        def add(src, cnt, raw):
            if src is eng and (eng is self.pe):
                return
            if cnt > need.get(src, 0):
                need[src] = cnt

        for k in R:
            for ent in self._conf(*k):
                if ent[0] is not None:
                    add(ent[0][0], ent[0][1], True)
        for k in W:
            for ent in self._conf(*k):
                if ent[0] is not None:
                    add(ent[0][0], ent[0][1], False)
                for src, cnt in ent[1].items():
                    add(src, cnt, False)
        return need

    def _wait(self, eng, need):
        for src, cnt in need.items():
            if eng.seen.get(src, 0) >= cnt:
                continue
            eng.h.wait_ge(src.sem, cnt * (16 if isinstance(src, _Chan) else 1))
            eng.seen[src] = cnt

    def _register(self, tok, R, W):
        for name, slot in W:
            d = self.res.setdefault(name, {})
            if slot is None:
                d.clear()
            d[slot] = [tok, {}]
        for name, slot in R:
            d = self.res.setdefault(name, {})
            ent = d.get(slot)
            if ent is None:
                ent = d[slot] = [None, {}]
            if tok[1] > ent[1].get(tok[0], 0):
                ent[1][tok[0]] = tok[1]

    def op(self, eng, fn, R=(), W=(), inc=True):
        if self.dry:
            return
        R = self._keys(R)
        W = self._keys(W)
        self._wait(eng, self._deps(eng, R, W))
        ins = fn(eng.h)
        if inc:
            ins.then_inc(eng.sem, 1)
            eng.count += 1
            tok = (eng, eng.count)
        else:
            tok = (eng, eng.count + 1)
        self._register(tok, R, W)
        if self.hook is not None and not self._in_hook:
            self._hcnt += 1
            if self._hcnt % self.hook_every == 0:
                self._in_hook = True
                self.hook()
                self._in_hook = False

    NCH = {"in": 8, "w": 8, "out": 8, "outp": 8}

    def dma(self, qeng, chan, out, in_, R=(), W=()):
        if self.dry:
            return
        i = self.rr.get(chan, 0)
        self.rr[chan] = i + 1
        ch = self.chan(f"{chan}{i % self.NCH[chan]}")
        R = self._keys(R)
        W = self._keys(W)
        need = self._deps(qeng, R, W)
        if ch.count:
            need[ch] = max(need.get(ch, 0), ch.count)
        self._wait(qeng, need)
        qeng.h.dma_start(out=_ap(out), in_=_ap(in_)).then_inc(ch.sem, 16)
        ch.count += 1
        self._register((ch, ch.count), R, W)

    def mm(self, out, lhsT, rhs, start, stop, inc=None):
        inc = stop if inc is None else inc
        self.op(self.pe, lambda e: e.matmul(_ap(out), _ap(lhsT), _ap(rhs), start=start, stop=stop),
                R=[lhsT, rhs], W=[out], inc=inc)

    def tr(self, out, in_, ident, inc=True):
        self.op(self.pe, lambda e: e.transpose(_ap(out), _ap(in_), _ap(ident)), R=[in_, ident], W=[out], inc=inc)

    def A(self, out, in_, func, bias=None, scale=1.0, eng=None):
        kw = {}
        if bias is not None:
            kw["bias"] = _ap(bias)
        kw["scale"] = _ap(scale)
        self.op(self.act, lambda e: e.activation(out=_ap(out), in_=_ap(in_), func=func, **kw),
                R=[in_, bias, scale], W=[out])

    def tt(self, out, in0, in1, op, eng=None):
        eng = eng or self.dve
        self.op(eng, lambda e: e.tensor_tensor(_ap(out), _ap(in0), _ap(in1), op), R=[in0, in1], W=[out])

    def ts(self, out, in0, s1, op0, s2=None, op1=None, eng=None):
        eng = eng or self.dve
        if op1 is None:
            fn = lambda e: e.tensor_scalar(_ap(out), _ap(in0), _ap(s1), None, op0)
        else:
            fn = lambda e: e.tensor_scalar(_ap(out), _ap(in0), _ap(s1), _ap(s2), op0, op1)
        self.op(eng, fn, R=[in0, s1, s2], W=[out])

    def stt(self, out, in0, sc, in1, op0, op1, eng=None):
        eng = eng or self.dve
        self.op(eng, lambda e: e.scalar_tensor_tensor(_ap(out), _ap(in0), _ap(sc), _ap(in1), op0, op1),
                R=[in0, sc, in1], W=[out])

    def scan(self, out, d0, d1, init, op0=ALU.mult, op1=ALU.add):
        self.op(self.dve, lambda e: e.tensor_tensor_scan(_ap(out), _ap(d0), _ap(d1), _ap(init), op0, op1),
                R=[d0, d1, init], W=[out])

    def cp(self, out, in_, eng=None):
        eng = eng or self.dve
        self.op(eng, lambda e: e.tensor_copy(_ap(out), _ap(in_)), R=[in_], W=[out])

    def rsqrt(self, out, in_, eps_col):
        self.A(out, in_, AF.Ln, bias=eps_col)
        self.A(out, out, AF.Exp, scale=-0.5)

    def finish(self):
        for ch in self.chans.values():
            if self.sp.seen.get(ch, 0) < ch.count:
                self.sp.h.wait_ge(ch.sem, ch.count * 16)
        for e in (self.pe, self.act, self.dve, self.pool):
            if e.count:
                self.sp.h.wait_ge(e.sem, e.count)


def build(stage=STAGE):
    nc = bass.Bass("TRN2", target_bir_lowering=False)
    es = contextlib.ExitStack()
    with es:
        _build(nc, es, stage)
    return nc


def _build(nc, es, stage):
    def DI(name, shape):
        return nc.dram_tensor(name, shape, F32, kind="ExternalInput").ap()

    def DO(name, shape):
        return nc.dram_tensor(name, shape, F32, kind="ExternalOutput").ap()

    xT_d = DI("xT", [D, T])
    xsT_d = DI("xsT", [D, NS])
    cT_d = DI("cT", [D, 17])
    cst_d = DI("cst", [128, NCST])
    kc_d = DI("kc", [128, NKC])
    wada_d = DI("w_ada", [D, 9 * D])
    wg_d = [DI("w_gate1", [D, DFF]), DI("w_gate2", [D, DFF])]
    wu_d = [DI("w_up1", [D, DFF]), DI("w_up2", [D, DFF])]
    wd_d = [DI("w_down1", [DFF, D]), DI("w_down2", [DFF, D])]
    win_d = DI("w_in", [D, 3 * D])
    wout_d = DI("w_out", [D, D])
    wabd_d = DI("w_a_bd", [128, 4, 128])
    wxbd_d = DI("w_x_bd", [128, 4, 128])
    h0T_d = DI("h0T", [512, NS])
    conv0T_d = DI("conv0T", [512, 3, NS])
    S0_d = DI("S0", [NS, 4, 128, 128])

    win_bf = nc.dram_tensor("win_bf16", [24, 128, 8, 128], BF16, kind="Internal").ap()
    wout_bf = nc.dram_tensor("wout_bf16", [8, 128, 8, 128], BF16, kind="Internal").ap()

    S_in_l = nc.dram_tensor("S_in_l", [128, NS * 512], F32, kind="Internal").ap()
    S_out_l = nc.dram_tensor("S_out_l", [128, NS * 512], F32, kind="Internal").ap()

    yT_d = DO("yT", [D, T])
    ysT_d = DO("ysT", [D, NS])
    hP_d = DO("hP", [128, 4])
    convP_d = DO("convP", [128, 4, 3])
    SP_d = DO("SP", [4, 128, 128])
    hS_d = DO("hS", [128, 4, NS])
    convS_d = DO("convS", [128, 4, 3, NS])
    SS_d = DO("SS", [NS, 4, 128, 128])

    kb = KB(nc, es)
    pe, act, dve, pool, sp = kb.pe, kb.act, kb.dve, kb.pool, kb.sp

    def SB(name, shape, dt=F32, stack=es):
        return stack.enter_context(nc.sbuf_tensor(name, shape, dt))

    xT = SB("xT_sb", [128, 8, T])
    xsT = SB("xsT_sb", [128, 8, NS])
    cst = SB("cst_sb", [128, NCST])
    kc = SB("kc_sb", [128, NKC])
    modT = SB("modT", [128, 72, 17])
    Acoef = [SB(f"Acoef{i}", [128, 8, 17]) for i in range(3)]
    Gcoef = [SB(f"Gcoef{i}", [128, 8, 17]) for i in range(3)]
    ones_d = SB("ones_d", [128, 128], BF16)
    ones_h = SB("ones_h", [128, 128], BF16)
    ident_bf = SB("ident_bf", [128, 128], BF16)
    eps_col = SB("eps_col", [128, 1])
    one_col = SB("one_col", [128, 1])
    pb = [es.enter_context(nc.psum_tensor(f"pb{i}", [128, 512], F32)) for i in range(8)]

    xT_dv = xT_d.rearrange("(c p) t -> p c t", p=128)
    yT_dv = yT_d.rearrange("(c p) t -> p c t", p=128)

    kb.dma(sp, "in", cst[:], cst_d, W=[cst])
    kb.dma(sp, "in", kc[:], kc_d, W=[kc])
    cT = SB("cT_sb", [128, 8, 17], F32)
    kb.dma(sp, "in", cT[:], cT_d.rearrange("(c p) t -> p c t", p=128), W=[cT])
    kb.dma(sp, "in", xsT[:], xsT_d.rearrange("(c p) t -> p c t", p=128), W=[xsT])
    for blk in range(4):
        kb.dma(sp, "in", xT[:, :, blk * 512:(blk + 1) * 512], xT_dv[:, :, blk * 512:(blk + 1) * 512],
               W=[Sl(xT, blk)])
    kb.op(dve, lambda e: e.memset(ones_d[:], 1.0 / 1024.0), W=[ones_d])
    kb.op(dve, lambda e: e.memset(ones_h[:], 1.0 / 128.0), W=[ones_h])
    kb.cp(ident_bf[:], kc[:, K_ID:K_ID + 128])
    kb.op(dve, lambda e: e.memset(eps_col[:], EPS), W=[eps_col])
    kb.op(dve, lambda e: e.memset(one_col[:], 1.0), W=[one_col])

    siluT = SB("siluT", [128, 8, 17], BF16)
    kb.A(siluT[:], cT[:], AF.Silu)
    wada_v = wada_d.rearrange("(c p) n -> p c n", p=128)
    mod_state = {"ring": None, "order": list(range(72)), "issued": 0, "done": 0, "nbank": 0}

    def mod_prefetch(upto):
        ring = mod_state["ring"]
        while mod_state["issued"] < min(upto, 72):
            n = mod_state["issued"]
            g = mod_state["order"][n]
            kb.dma(pool, "w", ring[n % len(ring)][:], wada_v[:, :, g * 128:(g + 1) * 128], W=[ring[n % len(ring)]])
            mod_state["issued"] += 1

    def mod_coefs(l, which):
        lnpre = cst[:, C_LN + 16 * l:C_LN + 16 * l + 8, None].broadcast_to([128, 8, 17])
        lnpost = cst[:, C_LN + 16 * l + 8:C_LN + 16 * l + 16, None].broadcast_to([128, 8, 17])
        if which == "A":
            kb.stt(Acoef[l][:], modT[:, (3 * l + 1) * 8:(3 * l + 2) * 8, :], 1.0, lnpre, ALU.add, ALU.mult)
        else:
            kb.stt(Gcoef[l][:], modT[:, (3 * l + 2) * 8:(3 * l + 3) * 8, :], 0.5 if l != 1 else 1.0, lnpost,
                   ALU.mult, ALU.mult)

    def mod_piece(banks):
        n = mod_state["done"]
        if n >= 72:
            return False
        ring = mod_state["ring"]
        mod_prefetch(n + len(ring))
        g = mod_state["order"][n]
        wb = ring[n % len(ring)]
        bank = banks[mod_state["nbank"] % len(banks)]
        mod_state["nbank"] += 1
        for kcx in range(8):
            kb.mm(bank[:, 0:17], wb[:, kcx, :], siluT[:, kcx, :], start=(kcx == 0), stop=(kcx == 7))
        kb.ts(Sl(modT[:, g, :], g), bank[:, 0:17], cst[:, C_BADA + g:C_BADA + g + 1], ALU.add)
        mod_state["done"] += 1
        i = g // 8
        if g % 8 == 7:
            if i % 3 == 1:
                mod_coefs(i // 3, "A")
            elif i % 3 == 2:
                mod_coefs(i // 3, "G")
        return True

    def shift(l):
        return modT[:, (3 * l) * 8:(3 * l + 1) * 8, :]

    def ffn(l, fi, last):
        A, G, sh = Acoef[l], Gcoef[l], shift(l)
        with contextlib.ExitStack() as fs:
            hy = SB(f"hy{fi}", [128, 8192], F32, fs)
            actT = [SB(f"actT{fi}_{i}", [128, NJ, 512], BF16, fs) for i in range(2)]
            actsT = SB(f"actsT{fi}", [128, NJ, NS], BF16, fs)
            wg = [SB(f"wg{fi}_{i}", [128, 8, 256], BF16, fs) for i in range(2)]
            wu = [SB(f"wu{fi}_{i}", [128, 8, 256], BF16, fs) for i in range(2)]
            wd = [SB(f"wd{fi}_{i}", [128, NJ, 128], BF16, fs) for i in range(2)]
            sq = [SB(f"sq{fi}_{i}", [128, 512], BF16, fs) for i in range(4)]
            tmp = [SB(f"tmp{fi}_{i}", [128, 512], F32, fs) for i in range(2)]
            sg = [SB(f"sg{fi}_{i}", [128, 512], F32, fs) for i in range(2)]
            rstd = [SB(f"rstd{fi}_{i}", [128, 512], F32, fs) for i in range(2)]
            hsT = SB(f"hsT{fi}", [128, 8, NS], BF16, fs)
            sm = [SB(f"sm{fi}_{i}", [128, 8, NS], F32, fs) for i in range(3)]
            sqs = SB(f"sqs{fi}", [128, 8, NS], BF16, fs)
            rs = SB(f"rs{fi}", [128, NS], F32, fs)
            sgs = SB(f"sgs{fi}", [128, NS], F32, fs)

            if fi == 0:
                mod_state["ring"] = [SB(f"wada{i}", [128, 8, 128], BF16, fs) for i in range(3)]
                for _ in range(16):
                    mod_piece([pb[5], pb[6], pb[7]])
            hT = [Sl(hy[:, b * 2048:(b + 1) * 2048].bitcast(BF16).rearrange("p (c t) -> p c t", t=512), b)
                  for b in range(2)]
            yv = [Sl(hy[:, 4096:8192].rearrange("p (c t) -> p c t", t=512), 2),
                  Sl(hy[:, 0:4096].rearrange("p (c t) -> p c t", t=512), 0, 1)]
            wg_v = wg_d[fi].rearrange("(c p) n -> p c n", p=128)
            wu_v = wu_d[fi].rearrange("(c p) n -> p c n", p=128)
            wd_v = wd_d[fi].rearrange("(j p) n -> p j n", p=128)
            cnt = {"n": 0, "m": 0, "q": 0, "t": 0, "r": 0}

            def prenorm(blk, bi):
                cols = slice(blk * 512, (blk + 1) * 512)
                for c in range(8):
                    s = sq[cnt["q"] % 4]
                    cnt["q"] += 1
                    kb.A(s[:], Sl(xT[:, c, cols], blk), AF.Square)
                    kb.mm(pb[4][:, :], ones_d[:], s[:], start=(c == 0), stop=(c == 7), inc=True)
                r = rstd[cnt["r"] % 2]
                cnt["r"] += 1
                kb.rsqrt(r[:], pb[4][:, :], eps_col[:])
                for c in range(8):
                    t = tmp[cnt["t"] % 2]
                    cnt["t"] += 1
                    kb.stt(t[:], Sl(xT[:, c, cols], blk), A[:, c, 0:1], r[:], ALU.mult, ALU.mult)
                    kb.A(Sl(hT[bi].ap[:, c, :], bi), t[:], AF.Identity, bias=sh[:, c, 0:1])

            def prenorm_s():
                kb.A(sqs[:], xsT[:], AF.Square)
                for c in range(8):
                    kb.mm(pb[4][:, 0:NS], ones_d[:], sqs[:, c, :], start=(c == 0), stop=(c == 7), inc=True)
                kb.rsqrt(rs[:], pb[4][:, 0:NS], eps_col[:])
                kb.tt(sm[0][:], xsT[:], rs[:, None, :].broadcast_to([128, 8, NS]), ALU.mult)
                kb.tt(sm[1][:], sm[0][:], A[:, :, 1:17], ALU.mult)
                kb.tt(hsT[:], sm[1][:], sh[:, :, 1:17], ALU.add)

            for half in range(2):
                blks = [2 * half, 2 * half + 1]
                if half == 0:
                    prenorm_s()
                for bi, blk in enumerate(blks):
                    prenorm(blk, bi)
                def load_gu(jg_):
                    kb.dma(pool, "w", wg[jg_ % 2][:], wg_v[:, :, jg_ * 256:(jg_ + 1) * 256], W=[wg[jg_ % 2]])
                    kb.dma(pool, "w", wu[jg_ % 2][:], wu_v[:, :, jg_ * 256:(jg_ + 1) * 256], W=[wu[jg_ % 2]])

                load_gu(0)
                for jg in range(NJ // 2):
                    wgb, wub = wg[jg % 2], wu[jg % 2]
                    if fi == 0 and half == 1 and jg < 8:
                        for b_ in (2 * jg, 2 * jg + 1):
                            kb.dma(sp, "in", S_in_l[:, b_ * 512:(b_ + 1) * 512].rearrange("k (h v) -> k h v", v=128),
                                   S0_d[b_].rearrange("h k v -> k h v"), W=[Sl(S_in_l, b_ // 4)])
                    if fi == 0 and half == 1:
                        for p_ in range(2) if jg < 8 else []:
                            i_ = 2 * jg + p_
                            for f_ in range(2):
                                if i_ < 12:
                                    fc_ = 2 * i_ + f_
                                    kb.dma(pool, "w", win_bf[fc_],
                                           win_d[:, fc_ * 128:(fc_ + 1) * 128].rearrange("(c p) n -> p c n", p=128),
                                           W=[Sl(win_bf, fc_)])
                                else:
                                    fc_ = 2 * (i_ - 12) + f_
                                    kb.dma(pool, "w", wout_bf[fc_],
                                           wout_d[:, fc_ * 128:(fc_ + 1) * 128].rearrange("(c p) n -> p c n", p=128),
                                           W=[Sl(wout_bf, fc_)])
                    if jg + 1 < NJ // 2:
                        load_gu(jg + 1)
                    for jj in range(2):
                        j = 2 * jg + jj
                        js = slice(jj * 128, (jj + 1) * 128)
                        for bi in range(2):
                            n = cnt["n"]
                            cnt["n"] += 1
                            pg, pu = pb[n % 2], pb[2 + n % 2]
                            for kcx in range(8):
                                kb.mm(pg[:], wgb[:, kcx, js], Sl(hT[bi].ap[:, kcx, :], bi), kcx == 0, kcx == 7)
                            for kcx in range(8):
                                kb.mm(pu[:], wub[:, kcx, js], Sl(hT[bi].ap[:, kcx, :], bi), kcx == 0, kcx == 7)
                            s = sg[n % 2]
                            kb.A(s[:], pg[:], AF.Silu)
                            kb.tt(Sl(actT[bi][:, j, :], j), s[:], pu[:], ALU.mult)
                            if fi == 0:
                                mod_piece([pb[5], pb[6], pb[7]])
                        if half == 0:
                            o = 64 + 32 * (j % 2)
                            psg = Sl(pb[4][:, o:o + NS], ("sg", j % 2))
                            psu = Sl(pb[4][:, o + NS:o + 2 * NS], ("su", j % 2))
                            for kcx in range(8):
                                kb.mm(psg, wgb[:, kcx, js], hsT[:, kcx, :], kcx == 0, kcx == 7)
                            for kcx in range(8):
                                kb.mm(psu, wub[:, kcx, js], hsT[:, kcx, :], kcx == 0, kcx == 7)
                            kb.A(sgs[:], psg, AF.Silu)
                            kb.tt(Sl(actsT[:, j, :], j), sgs[:], psu, ALU.mult)
                pend = []
                def load_d(dc_):
                    kb.dma(pool, "w", wd[dc_ % 2][:], wd_v[:, :, dc_ * 128:(dc_ + 1) * 128], W=[wd[dc_ % 2]])

                load_d(0)
                for dc in range(8):
                    wdb = wd[dc % 2]
                    if dc + 1 < 8:
                        load_d(dc + 1)
                    for bi in range(2):
                        m = cnt["m"]
                        cnt["m"] += 1
                        py = pb[5 + m % 2]
                        for j in range(NJ):
                            kb.mm(py[:], wdb[:, j, :], Sl(actT[bi][:, j, :], j), j == 0, j == NJ - 1)
                        while pend:
                            pend.pop(0)()
                        kb.A(Sl(yv[bi].ap[:, dc, :], *yv[bi].slots), py[:], AF.Copy)
                        s = sq[cnt["q"] % 4]
                        cnt["q"] += 1
                        kb.A(s[:], py[:], AF.Square)
                        pend.append(lambda s=s, bi=bi, dc=dc: kb.mm(pb[4][:, :] if bi == 0 else pb[7][:, :], ones_d[:],
                                                                   s[:], start=(dc == 0), stop=(dc == 7), inc=True))
                    if half == 0:
                        psy = Sl(pb[3][:, dc * NS:(dc + 1) * NS], ("ys", dc))
                        for j in range(NJ):
                            kb.mm(psy, wdb[:, j, :], Sl(actsT[:, j, :], j), j == 0, j == NJ - 1)
                while pend:
                    pend.pop(0)()
                for bi, blk in enumerate(blks):
                    cols = slice(blk * 512, (blk + 1) * 512)
                    stp = pb[4][:, :] if bi == 0 else pb[7][:, :]
                    r = rstd[cnt["r"] % 2]
                    cnt["r"] += 1
                    kb.rsqrt(r[:], stp, eps_col[:])
                    for dc in range(8):
                        t = tmp[cnt["t"] % 2]
                        cnt["t"] += 1
                        kb.stt(t[:], Sl(yv[bi].ap[:, dc, :], *yv[bi].slots), G[:, dc, 0:1], r[:], ALU.mult, ALU.mult)
                        kb.tt(Sl(xT[:, dc, cols], blk), Sl(xT[:, dc, cols], blk), t[:], ALU.add)
                    if last:
                        kb.dma(sp, "out", yT_dv[:, :, cols], Sl(xT[:, :, cols], blk), R=[Sl(xT, blk)])
                if half == 0:
                    ysv = pb[3][:, 0:8 * NS].rearrange("p (c t) -> p c t", t=NS)
                    kb.cp(sm[0][:], Sl(ysv, *[("ys", dc) for dc in range(8)]))
                    kb.tt(sqs[:], sm[0][:], sm[0][:], ALU.mult)
                    for c in range(8):
                        kb.mm(pb[2][:, 0:NS], ones_d[:], sqs[:, c, :], start=(c == 0), stop=(c == 7), inc=True)
                    kb.rsqrt(rs[:], pb[2][:, 0:NS], eps_col[:])
                    kb.tt(sm[1][:], sm[0][:], rs[:, None, :].broadcast_to([128, 8, NS]), ALU.mult)
                    kb.tt(sm[2][:], sm[1][:], G[:, :, 1:17], ALU.mult)
                    kb.tt(xsT[:], xsT[:], sm[2][:], ALU.add)
                    if last:
                        kb.dma(sp, "out", ysT_d.rearrange("(c p) t -> p c t", p=128), xsT[:], R=[xsT])
            barrier(kb)


    def mixer(last):
        A, G, sh = Acoef[1], Gcoef[1], shift(1)
        with contextlib.ExitStack() as mx:
            NWB = 4
            w_ring = [SB(f"w_ring{i}", [128, 8, 128], BF16, mx) for i in range(NWB)]
            wabd = SB("wabd", [128, 4, 128], BF16, mx)
            wxbd = SB("wxbd", [128, 4, 128], BF16, mx)
            wo = [SB(f"wo{i}", [128, 8, 128], BF16, mx) for i in range(2)]
            dc_ = SB("dcoef", [128, 48], F32, mx)
            sq = [SB(f"msq{i}", [128, 512], BF16, mx) for i in range(4)]
            tmp = [SB(f"mtmp{i}", [128, 512], F32, mx) for i in range(2)]
            rstd = SB("mrstd", [128, 512], F32, mx)
            pr_order = [c2 for c in range(4) for c2 in (c, 4 + c)] + [8 + 4 * i + h for h in range(4) for i in range(4)]
            hg_order = [8 + 4 * i + h for h in range(4) for i in range(4)]
            lru_order = [c2 for c in range(4) for c2 in (c, 4 + c)]
            tb_order = []
            for c in range(4):
                tb_order += [c, 4 + c] + [8 + 4 * i + c for i in range(4)]
            w_seq = []
            dry_order = []
            wst = {"issued": 0, "used": 0}

            def w_prefetch(upto):
                while wst["issued"] < min(upto, len(w_seq)):
                    i = wst["issued"]
                    fc = w_seq[i]
                    kb.dma(sp, "in", w_ring[i % NWB][:], win_bf[fc], R=[Sl(win_bf, fc)], W=[w_ring[i % NWB]])
                    wst["issued"] += 1

            def w_get(fc):
                i = wst["used"]
                assert w_seq[i] == fc, (i, w_seq[i], fc)
                w_prefetch(i + NWB)
                wst["used"] += 1
                return w_ring[i % NWB]

            w_prefetch(NWB)
            kb.dma(pool, "w", wabd[:], wabd_d, W=[wabd])
            kb.dma(pool, "w", wxbd[:], wxbd_d, W=[wxbd])

            lam = cst[:, C_LAM:C_LAM + 4]
            t4 = [SB(f"t4_{i}", [128, 4], F32, mx) for i in range(6)]
            kb.ts(t4[5][:], lam, -1.0, ALU.mult)
            kb.tt(t4[0][:], lam, t4[5][:], ALU.max)
            kb.A(t4[1][:], t4[0][:], AF.Exp, scale=-1.0)
            kb.ts(t4[2][:], t4[1][:], 2.0, ALU.add)
            kb.op(dve, lambda e_: e_.reciprocal(t4[2][:], t4[2][:]), R=[t4[2]], W=[t4[2]])
            kb.tt(t4[2][:], t4[1][:], t4[2][:], ALU.mult)
            kb.tt(t4[3][:], t4[2][:], t4[2][:], ALU.mult)
            kb.ts(t4[4][:], t4[3][:], 1.0 / 11.0, ALU.mult, 1.0 / 9.0, ALU.add)
            for cf in (1.0 / 7.0, 1.0 / 5.0, 1.0 / 3.0, 1.0):
                kb.tt(t4[4][:], t4[4][:], t4[3][:], ALU.mult)
                kb.ts(t4[4][:], t4[4][:], cf, ALU.add)
            kb.tt(t4[4][:], t4[4][:], t4[2][:], ALU.mult)
            kb.ts(t4[5][:], lam, -1.0, ALU.mult, 0.0, ALU.max)
            kb.stt(dc_[:, 0:4], t4[4][:], 2.0, t4[5][:], ALU.mult, ALU.add)
            kb.ts(dc_[:, 4:8], dc_[:, 0:4], -4.0, ALU.mult)
            kb.ts(dc_[:, 8:12], dc_[:, 0:4], -8.0, ALU.mult)
            kb.ts(dc_[:, 12:16], cst[:, C_BA:C_BA + 4], 0.5, ALU.mult)
            kb.ts(dc_[:, 16:20], cst[:, C_BX:C_BX + 4], 0.5, ALU.mult)
            kb.tt(t4[0][:], cst[:, C_LB:C_LB + 4], cst[:, C_LB + 4:C_LB + 8], ALU.subtract)
            kb.A(t4[1][:], t4[0][:], AF.Tanh, scale=0.5)
            kb.ts(dc_[:, 20:24], t4[1][:], 0.5, ALU.mult, 0.5, ALU.add)
            kb.ts(dc_[:, 24:28], dc_[:, 20:24], -0.5, ALU.mult, 0.5, ALU.add)
            kb.tt(dc_[:, 28:32], dc_[:, 20:24], dc_[:, 24:28], ALU.add)
            kb.ts(dc_[:, 32:36], dc_[:, 24:28], -1.0, ALU.mult)
            kb.ts(dc_[:, 36:40], dc_[:, 28:32], -1.0, ALU.mult, 1.0, ALU.add)
            kb.ts(dc_[:, 40:41], cst[:, C_NW:C_NW + 1], 0.5, ALU.mult)
            SP_, M4, M8, HBA, HBX, C1, C0, NC1, OMC0 = (slice(4 * i, 4 * i + 4) for i in range(9))
            SP_, M4, M8, HBA, HBX = slice(0, 4), slice(4, 8), slice(8, 12), slice(12, 16), slice(16, 20)
            C1, C0, NC1, OMC0 = slice(24, 28), slice(28, 32), slice(32, 36), slice(36, 40)
            QS = 128.0 ** -0.5
            GC = 0.7978845608028654

            def bc(ap4, n=NS):
                return ap4[:, :, None].broadcast_to([128, 4, n])

            def wout_phase(mixT_, ncol, ysb_fn, xdst, gsl, rs_t):
                for dcx in range(8):
                    wb = wo[dcx % 2]
                    kb.dma(sp, "in", wb[:], wout_bf[dcx], R=[Sl(wout_bf, dcx)], W=[wb])
                    py = pb[dcx % 2]
                    for kcx in range(8):
                        kb.mm(py[:, 0:ncol], wb[:, kcx, :], mixT_[:, kcx, :], kcx == 0, kcx == 7)
                    kb.A(ysb_fn(dcx), py[:, 0:ncol], AF.Copy)
                    s_ = sq[dcx % 4]
                    kb.A(s_[:, 0:ncol], py[:, 0:ncol], AF.Square)
                    kb.mm(pb[7][:, 0:ncol], ones_d[:], s_[:, 0:ncol], start=(dcx == 0), stop=(dcx == 7), inc=True)
                kb.rsqrt(rs_t, pb[7][:, 0:ncol], eps_col[:])
                for dcx in range(8):
                    t_ = tmp[dcx % 2]
                    if ncol == 512:
                        kb.stt(t_[:], ysb_fn(dcx), G[:, dcx, 0:1], rs_t, ALU.mult, ALU.mult)
                    else:
                        kb.tt(t_[:, 0:ncol], ysb_fn(dcx), rs_t, ALU.mult)
                        kb.tt(t_[:, 0:ncol], t_[:, 0:ncol], G[:, dcx, 1:17], ALU.mult)
                    kb.tt(xdst(dcx), xdst(dcx), t_[:, 0:ncol], ALU.add)

            mixTs = [SB(f"mixT{i}", [128, 8, 512], BF16, mx) for i in range(2)]
            prog = {"lru_done": False}

            def wout_chain(tbp, mixp, gated=True):
                colsp = slice(tbp * 512, (tbp + 1) * 512)
                while gated and not prog["lru_done"]:
                    yield
                pend_ = None
                for dcx in range(8):
                    wb = wo[dcx % 2]
                    kb.dma(sp, "in", wb[:], wout_bf[dcx], R=[Sl(wout_bf, dcx)], W=[wb])
                    for kcx in range(8):
                        kb.mm(pb[6][:, :], wb[:, kcx, :], mixp[:, kcx, :], kcx == 0, kcx == 7)
                    yield
                    if pend_ is not None:
                        pend_()
                    s_ = sq[dcx % 2]
                    kb.A(s_[:], pb[6][:, :], AF.Square)
                    pend_ = (lambda s_=s_, dcx=dcx: kb.mm(pb[7][:, :], ones_d[:], s_[:], start=(dcx == 0),
                                                          stop=(dcx == 7), inc=True))
                    yield
                pend_()
                yield
                kb.rsqrt(rstd[:], pb[7][:, :], eps_col[:])
                yield
                for dcx in range(8):
                    wb = wo[dcx % 2]
                    kb.dma(sp, "in", wb[:], wout_bf[dcx], R=[Sl(wout_bf, dcx)], W=[wb])
                    py = pb[6 + dcx % 2]
                    for kcx in range(8):
                        kb.mm(py[:, :], wb[:, kcx, :], mixp[:, kcx, :], kcx == 0, kcx == 7)
                    yield
                    t_ = tmp[dcx % 2]
                    kb.stt(t_[:], py[:, :], G[:, dcx, 0:1], rstd[:], ALU.mult, ALU.mult)
                    kb.tt(Sl(xT[:, dcx, colsp], tbp), Sl(xT[:, dcx, colsp], tbp), t_[:], ALU.add)
                    yield


            with contextlib.ExitStack() as ps:
                hT = SB("mhT", [128, 8, 512], BF16, ps)
                ub = [SB(f"ub{c}", [128, 515], F32, ps) for c in range(4)]
                hst = SB("hstate", [128, 4], F32, ps)
                cP = SB("convP_sb", [128, 4, 3], F32, ps)
                Tt = [SB(f"T{i}", [128, 512], F32, ps) for i in range(8)]
                TL = [SB(f"TL{i}", [128, 512], F32, ps) for i in range(7)]
                ucb = SB("ucb", [128, 512], BF16, ps)
                S = [SB(f"Sst{h}", [128, 128], F32, ps) for h in range(4)]
                HR = []
                for r_ in range(2):
                    HR.append({
                        "T": Tt[4 * r_:4 * r_ + 4] + [SB(f"TH{r_}_{i}", [128, 512], F32, ps) for i in range(2)],
                        "khT": SB(f"khT{r_}", [128, 512], BF16, ps), "qT": SB(f"qT{r_}", [128, 512], BF16, ps),
                        "vT": SB(f"vT{r_}", [128, 512], BF16, ps),
                        "ktok": SB(f"ktok{r_}", [64, 8, 128], BF16, ps), "vtok": SB(f"vtok{r_}", [64, 8, 128], BF16, ps),
                        "scT": SB(f"scT{r_}", [64, 8, 64], BF16, ps), "hsq": SB(f"hsq{r_}", [128, 512], BF16, ps),
                        "Zs": SB(f"Zs{r_}", [128, 128, 8], F32, ps), "AzB": SB(f"AzB{r_}", [128, 64, 8], F32, ps),
                        "Az": SB(f"Az{r_}", [128, 8], F32, ps), "Sbf": SB(f"Sbf{r_}", [128, 9, 128], BF16, ps),
                        "banks": (pb[3 * r_], pb[3 * r_ + 1], pb[3 * r_ + 2]),
                    })
                for c in range(4):
                    kb.op(dve, lambda e_, c=c: e_.memset(ub[c][:, 0:3], 0.0), W=[ub[c]])
                    kb.op(dve, lambda e_, c=c: e_.memset(S[c][:], 0.0), W=[S[c]])
                kb.op(dve, lambda e_: e_.memset(hst[:], 0.0), W=[hst])
                maskb = kc[0:64, None, K_MASK:K_MASK + 64].broadcast_to([64, 8, 64])
                for tb in range(4):
                    cols = slice(tb * 512, (tb + 1) * 512)
                    mixT = mixTs[tb % 2]
                    for c in range(8):
                        s_ = sq[c % 4]
                        kb.A(s_[:], Sl(xT[:, c, cols], tb), AF.Square)
                        kb.mm(pb[7][:, :], ones_d[:], s_[:], start=(c == 0), stop=(c == 7), inc=True)
                    kb.rsqrt(rstd[:], pb[7][:, :], eps_col[:])
                    for c in range(8):
                        t_ = tmp[c % 2]
                        kb.stt(t_[:], Sl(xT[:, c, cols], tb), A[:, c, 0:1], rstd[:], ALU.mult, ALU.mult)
                        kb.A(hT[:, c, :], t_[:], AF.Identity, bias=sh[:, c, 0:1])

                    def proj(pdst, fc):
                        if kb.dry:
                            dry_order.append(fc)
                            return
                        wb_ = w_get(fc)
                        for kcx in range(8):
                            kb.mm(pdst[:], wb_[:, kcx, :], hT[:, kcx, :], kcx == 0, kcx == 7)

                    def lru_chain():
                        for c in range(4):
                            uc, thr, a_, a2, thi, hs, ge = TL
                            proj(pb[6], c)
                            yield
                            proj(pb[7], 4 + c)
                            yield
                            kb.A(ub[c][:, 3:515], pb[6][:], AF.Copy)
                            yield
                            kb.A(ge[:], pb[7][:], AF.Gelu_apprx_tanh)
                            yield
                            cw_ = lambda k: cst[:, C_CONVW + 4 * c + k:C_CONVW + 4 * c + k + 1]
                            kb.ts(uc[:], ub[c][:, 3:515], cw_(3), ALU.mult, cst[:, C_CONVB + c:C_CONVB + c + 1], ALU.add,
                                  eng=pool)
                            for k in range(3):
                                kb.ts(hs[:], ub[c][:, k:k + 512], cw_(k), ALU.mult, 0.0, ALU.add, eng=pool)
                                kb.tt(uc[:], uc[:], hs[:], ALU.add, eng=pool)
                            yield
                            if tb == 3:
                                kb.cp(cP[:, c, :], ub[c][:, 512:515])
                            else:
                                kb.cp(ub[c][:, 0:3], ub[c][:, 512:515])
                            kb.A(ucb[:], uc[:], AF.Copy)
                            yield
                            kb.mm(pb[6][:], wabd[:, c, :], ucb[:], True, True)
                            yield
                            kb.mm(pb[7][:], wxbd[:, c, :], ucb[:], True, True)
                            yield
                            kb.A(thr[:], pb[6][:], AF.Tanh, bias=dc_[:, 12 + c:13 + c], scale=0.5)
                            yield
                            kb.A(thi[:], pb[7][:], AF.Tanh, bias=dc_[:, 16 + c:17 + c], scale=0.5)
                            yield
                            kb.A(a_[:], thr[:], AF.Exp, bias=dc_[:, 4 + c:5 + c], scale=dc_[:, 4 + c:5 + c])
                            yield
                            kb.A(a2[:], thr[:], AF.Exp, bias=dc_[:, 8 + c:9 + c], scale=dc_[:, 8 + c:9 + c])
                            yield
                            kb.A(a2[:], a2[:], AF.Ln, bias=one_col[:], scale=-1.0)
                            yield
                            kb.A(a2[:], a2[:], AF.Exp, scale=0.5)
                            kb.stt(thi[:], thi[:], 1.0, uc[:], ALU.add, ALU.mult)
                            kb.stt(thr[:], a2[:], 0.5, thi[:], ALU.mult, ALU.mult)
                            if tb == 0:
                                kb.ts(thr[:, 0:1], thi[:, 0:1], 0.5, ALU.mult)
                            kb.scan(hs[:], a_[:], thr[:], hst[:, c:c + 1])
                            yield
                            kb.cp(hst[:, c:c + 1], hs[:, 511:512])
                            yield
                            kb.tt(mixT[:, c, :], ge[:], hs[:], ALU.mult)
                            yield
                        prog["lru_done"] = True
                    def hgrn_chain(heads, R):
                        thf, d0, kk, P_, gq, d1 = R["T"]
                        osb_, t1 = kk, thf
                        khT, qT, vT, ktok, vtok, scT = R["khT"], R["qT"], R["vT"], R["ktok"], R["vtok"], R["scT"]
                        Zs, AzB, Az, Sb, hsq = R["Zs"], R["AzB"], R["Az"], R["Sbf"], R["hsq"]
                        bX, bY, bZ = R["banks"]
                        kb.op(dve, lambda e_: e_.memset(d1[:], 0.0), W=[d1])
                        for hd in heads:
                            proj(bX, 8 + hd)
                            yield
                            proj(bY, 12 + hd)
                            yield
                            proj(bZ, 16 + hd)
                            yield
                            kb.A(thf[:], bY[:], AF.Tanh, scale=0.5)
                            yield
                            proj(bY, 20 + hd)
                            yield
                            kb.A(vT[:], bZ[:], AF.Copy)
                            yield
                            kb.A(gq[:], bY[:], AF.Tanh, scale=0.5)
                            yield
                            kb.ts(d0[:], thf[:], dc_[:, 24 + hd:25 + hd], ALU.mult, dc_[:, 28 + hd:29 + hd], ALU.add,
                                  eng=pool)
                            kb.ts(kk[:], thf[:], dc_[:, 32 + hd:33 + hd], ALU.mult, dc_[:, 36 + hd:37 + hd], ALU.add,
                                  eng=pool)
                            yield
                            kb.stt(gq[:], gq[:], 1.0, bY[:], ALU.add, ALU.mult)
                            kb.cp(d1[:, 0:512:64], d0[:, 0:512:64])
                            kb.op(dve, lambda e_: e_.memset(d0[:, 0:512:64], 0.0), W=[d0])
                            yield
                            kb.scan(P_[:], d0[:], d1[:], 0.0)
                            yield
                            kb.op(dve, lambda e_: e_.reciprocal(d0[:], P_[:]), R=[P_], W=[d0])
                            yield
                            kb.tt(khT[:], kk[:], d0[:], ALU.mult)
                            yield
                            kb.stt(qT[:], bX[:], QS, P_[:], ALU.mult, ALU.mult)
                            yield
                            ptk = bY[:, :].bitcast(BF16)
                            ptv = bZ[:, :].bitcast(BF16)
                            for cc in range(8):
                                kb.tr(ptk[0:64, cc * 128:(cc + 1) * 128], khT[:, cc * 64:(cc + 1) * 64], ident_bf[:],
                                      inc=(cc == 7))
                            yield
                            for cc in range(8):
                                kb.tr(ptv[0:64, cc * 128:(cc + 1) * 128], vT[:, cc * 64:(cc + 1) * 64], ident_bf[:],
                                      inc=(cc == 7))
                            yield
                            kb.cp(ktok[:], ptk[0:64, :].rearrange("p (c k) -> p c k", k=128))
                            yield
                            kb.A(vtok[:], ptv[0:64, :].rearrange("p (c k) -> p c k", k=128), AF.Copy)
                            yield
                            for cc in range(8):
                                kb.mm(bX[0:64, cc * 64:(cc + 1) * 64], khT[:, cc * 64:(cc + 1) * 64],
                                      qT[:, cc * 64:(cc + 1) * 64], True, True, inc=(cc == 7))
                            yield
                            kb.tt(scT[:], bX[0:64, :].rearrange("p (c t) -> p c t", t=64), maskb, ALU.mult)
                            yield
                            for cc in range(8):
                                pd = (bY, bZ)[cc // 4]
                                o_ = (cc % 4) * 128
                                kb.mm(pd[:, o_:o_ + 128], ktok[:, cc, :], vtok[:, cc, :], True, True, inc=(cc % 4 == 3))
                            yield
                            Av = P_[:, 63:512:64]
                            kb.cp(Az[:], Av)
                            kb.op(dve, lambda e_: e_.memset(Az[:, 0:1], 0.0), W=[Az])
                            yield
                            kb.A(AzB[:], Az[:, None, :].broadcast_to([128, 64, 8]), AF.Copy)
                            yield
                            for b2 in range(2):
                                kb.tt(Zs[:].rearrange("k v c -> k c v")[:, 4 * b2:4 * b2 + 4, :],
                                      (bY, bZ)[b2][:, :].rearrange("k (c v) -> k c v", v=128),
                                      Av[:, 4 * b2:4 * b2 + 4, None].broadcast_to([128, 4, 128]), ALU.mult)
                                yield
                            kb.stt(Zs[:, :, 0], S[hd][:], Av[:, 0:1], Zs[:, :, 0], ALU.mult, ALU.add)
                            kb.cp(Sb[:, 0, :], S[hd][:])
                            yield
                            for vh in range(2):
                                zz = Zs[:, vh * 64:(vh + 1) * 64, :].rearrange("k v c -> k (v c)")
                                kb.scan(zz, AzB[:].rearrange("k v c -> k (v c)"), zz, 0.0)
                                yield
                            kb.cp(S[hd][:], Zs[:, :, 7])
                            yield
                            kb.A(Sb[:, 1:9, :], Zs[:].rearrange("k v c -> k c v"), AF.Copy)
                            yield
                            for cc in range(8):
                                po_ = bX[:, cc * 64:(cc + 1) * 64]
                                kb.mm(po_, Sb[:, cc, :], qT[:, cc * 64:(cc + 1) * 64], True, False, inc=False)
                                kb.mm(po_, vtok[:, cc, :], scT[:, cc, :], False, True, inc=(cc == 7))
                            yield
                            kb.A(osb_[:], bX[:], AF.Copy)
                            yield
                            kb.A(hsq[:], bX[:], AF.Square)
                            yield
                            kb.mm(bY[:], ones_h[:], hsq[:], True, True)
                            yield
                            kb.rsqrt(t1[:], bY[:], eps_col[:])
                            yield
                            kb.stt(osb_[:], osb_[:], dc_[:, 40:41], t1[:], ALU.mult, ALU.mult)
                            yield
                            kb.tt(mixT[:, 4 + hd, :], osb_[:], gq[:], ALU.mult)
                            yield

                    def run_chains():
                        prog["lru_done"] = False
                        gens = {"A": hgrn_chain((0, 1), HR[0]), "B": hgrn_chain((2, 3), HR[1]), "L": lru_chain()}
                        live = [c_ for c_ in CFG["order"]]
                        if tb > 0:
                            gens["W"] = wout_chain(tb - 1, mixTs[(tb - 1) % 2])
                            live.append("W")
                        rnd = 0
                        while live:
                            for c_ in list(live):
                                if rnd < CFG["delay"].get(c_, 0):
                                    continue
                                try:
                                    for _ in range(CFG["steps"].get(c_, 1)):
                                        next(gens[c_])
                                except StopIteration:
                                    live.remove(c_)
                            rnd += 1

                    if tb == 0:
                        kb.dry = True
                        run_chains()
                        kb.dry = False
                        w_seq.extend(dry_order * 4 + list(range(24)))
                    run_chains()
                    if last:
                        kb.dma(sp, "out", yT_dv[:, :, cols], Sl(xT[:, :, cols], tb), R=[Sl(xT, tb)])
                kb.dma(sp, "out", hP_d, hst[:], R=[hst])
                kb.dma(sp, "out", convP_d, cP[:], R=[cP])
                for hd in range(4):
                    kb.dma(sp, "out", SP_d[hd], S[hd][:], R=[S[hd]])
                if last:
                    kb.dma(sp, "out", ysT_d.rearrange("(c p) t -> p c t", p=128), xsT[:], R=[xsT])
                barrier(kb)
            with contextlib.ExitStack() as ss:
                Ssb = SB("Ssb", [128, NS, 4, 128], F32, ss)
                Sbf = SB("Sbf_s", [128, NS, 4, 128], BF16, ss)
                h0 = SB("h0_s", [128, 4, NS], F32, ss)
                c0s = SB("c0_s", [128, 4, 3, NS], F32, ss)
                cnew = SB("cnew_s", [128, 4, 3, NS], F32, ss)
                hsT = SB("mhsT", [128, 8, NS], BF16, ss)
                mixs = SB("mixs", [128, 8, NS], BF16, ss)
                ps_sb = SB("ps_sb", [128, 24, NS], F32, ss)
                sm = [SB(f"msm{i}", [128, 8, NS], F32, ss) for i in range(3)]
                sqs = SB("msqs", [128, 8, NS], BF16, ss)
                rs = SB("mrs", [128, NS], F32, ss)
                e = [SB(f"e{i}", [128, 4, NS], F32, ss) for i in range(10)]
                eb = [SB(f"eb{i}", [128, 4, NS], BF16, ss) for i in range(3)]
                kvt = SB("kvt", [16, 8, 128], BF16, ss)
                vbd = [SB("vbd0", [16, NS, 128], BF16, ss)] * 2
                osb = SB("osb", [128, 4, NS], F32, ss)
                ys_s = SB("ys_s", [128, 8, NS], F32, ss)
                wg3 = wout_chain(3, mixTs[1], gated=False)

                def adv(n_=1):
                    for _ in range(n_):
                        next(wg3, None)

                kb.dma(sp, "in", h0[:], h0T_d.rearrange("(c p) b -> p c b", p=128), W=[h0])
                kb.dma(sp, "in", c0s[:], conv0T_d.rearrange("(c p) k b -> p c k b", p=128), W=[c0s])
                for g_ in range(4):
                    kb.dma(sp, "in", Ssb[:, 4 * g_:4 * g_ + 4, :, :],
                           S_in_l[:, g_ * 2048:(g_ + 1) * 2048].rearrange("k (b h v) -> k b h v", b=4, v=128),
                           R=[Sl(S_in_l, g_)], W=[Sl(Ssb, *range(4 * g_, 4 * g_ + 4))])
                kb.A(sqs[:], xsT[:], AF.Square)
                for c in range(8):
                    kb.mm(pb[4][:, 0:NS], ones_d[:], sqs[:, c, :], start=(c == 0), stop=(c == 7), inc=True)
                kb.rsqrt(rs[:], pb[4][:, 0:NS], eps_col[:])
                kb.tt(sm[0][:], xsT[:], rs[:, None, :].broadcast_to([128, 8, NS]), ALU.mult)
                kb.tt(sm[1][:], sm[0][:], A[:, :, 1:17], ALU.mult)
                kb.tt(hsT[:], sm[1][:], sh[:, :, 1:17], ALU.add)
                for fc in range(24):
                    wb_ = w_get(fc)
                    for kcx in range(8):
                        kb.mm(pb[5][:, fc * NS:(fc + 1) * NS], wb_[:, kcx, :], hsT[:, kcx, :], kcx == 0, kcx == 7)
                kb.cp(ps_sb[:], pb[5][:, 0:24 * NS].rearrange("p (f t) -> p f t", t=NS))
                kb.hook = adv
                u, yv = ps_sb[:, 0:4, :], ps_sb[:, 4:8, :]
                q, fr, v, gg = ps_sb[:, 8:12, :], ps_sb[:, 12:16, :], ps_sb[:, 16:20, :], ps_sb[:, 20:24, :]
                cw = lambda k: cst[:, C_CONVW + k:C_CONVW + 16:4]
                uc = e[0]
                kb.tt(uc[:], u, bc(cw(3)), ALU.mult)
                kb.tt(uc[:], uc[:], bc(cst[:, C_CONVB:C_CONVB + 4]), ALU.add)
                for k in range(3):
                    kb.tt(e[1][:], c0s[:, :, k, :], bc(cw(k)), ALU.mult)
                    kb.tt(uc[:], uc[:], e[1][:], ALU.add)
                kb.cp(cnew[:, :, 0:2, :], c0s[:, :, 1:3, :])
                kb.cp(cnew[:, :, 2, :], u)
                kb.dma(pool, "outp", convS_d, cnew[:], R=[cnew])
                kb.cp(eb[0][:], uc[:])
                for c in range(4):
                    kb.mm(pb[5][:, c * NS:(c + 1) * NS], wabd[:, c, :], eb[0][:, c, :], True, True)
                for c in range(4):
                    kb.mm(pb[5][:, (4 + c) * NS:(5 + c) * NS], wxbd[:, c, :], eb[0][:, c, :], True, True)
                pa = pb[5][:, 0:4 * NS].rearrange("p (c t) -> p c t", t=NS)
                px = pb[5][:, 4 * NS:8 * NS].rearrange("p (c t) -> p c t", t=NS)
                kb.stt(e[1][:], pa, 0.5, bc(dc_[:, HBA]), ALU.mult, ALU.add)
                kb.A(e[1][:], e[1][:], AF.Tanh)
                kb.stt(e[2][:], px, 0.5, bc(dc_[:, HBX]), ALU.mult, ALU.add)
                kb.A(e[2][:], e[2][:], AF.Tanh)
                kb.stt(e[3][:], e[1][:], 1.0, bc(dc_[:, M4]), ALU.add, ALU.mult)
                kb.A(e[3][:], e[3][:], AF.Exp)
                kb.tt(e[4][:], e[3][:], e[3][:], ALU.mult)
                kb.ts(e[4][:], e[4][:], -1.0, ALU.mult, 1.0, ALU.add)
                kb.A(e[4][:], e[4][:], AF.Ln)
                kb.A(e[4][:], e[4][:], AF.Exp, scale=0.5)
                kb.stt(e[5][:], e[2][:], 1.0, uc[:], ALU.add, ALU.mult)
                kb.stt(e[5][:], e[4][:], 0.5, e[5][:], ALU.mult, ALU.mult)
                kb.tt(e[6][:], e[3][:], h0[:], ALU.mult)
                kb.tt(e[6][:], e[6][:], e[5][:], ALU.add)
                kb.dma(pool, "outp", hS_d, e[6][:], R=[e[6]])
                kb.tt(e[7][:], yv, yv, ALU.mult)
                kb.ts(e[7][:], e[7][:], 0.044715, ALU.mult, 1.0, ALU.add)
                kb.tt(e[7][:], e[7][:], yv, ALU.mult)
                kb.A(e[7][:], e[7][:], AF.Tanh, scale=GC)
                kb.stt(e[7][:], e[7][:], 1.0, yv, ALU.add, ALU.mult)
                kb.stt(mixs[:, 0:4, :], e[7][:], 0.5, e[6][:], ALU.mult, ALU.mult)
                kb.A(e[1][:], fr, AF.Tanh, scale=0.5)
                kb.tt(e[2][:], e[1][:], bc(dc_[:, C1]), ALU.mult)
                kb.tt(e[2][:], e[2][:], bc(dc_[:, C0]), ALU.add)
                kb.ts(e[3][:], e[2][:], -1.0, ALU.mult, 1.0, ALU.add)
                kb.cp(eb[0][:], e[3][:])
                kb.cp(eb[1][:], v)
                kb.ts(eb[2][:], q, QS, ALU.mult)
                ptb = pb[0][:, :].bitcast(BF16)
                for hd in range(4):
                    kb.tr(ptb[0:NS, hd * 128:(hd + 1) * 128], eb[0][:, hd, :], ident_bf[:])
                    kb.tr(ptb[0:NS, (4 + hd) * 128:(5 + hd) * 128], eb[1][:, hd, :], ident_bf[:])
                kb.cp(kvt[:], ptb[0:NS, :].rearrange("p (g k) -> p g k", k=128))
                n_ = 0
                for hd in range(4):
                    vb = vbd[hd % 2]
                    kb.tt(vb[:], kvt[:, 4 + hd, None, :].broadcast_to([NS, NS, 128]),
                          kc[0:NS, K_ID:K_ID + NS, None].broadcast_to([NS, NS, 128]), ALU.mult)
                    for bg in range(4):
                        pp = pb[1 + n_ % 2]
                        n_ += 1
                        kb.mm(pp[:], kvt[:, hd, :], vb[:, 4 * bg:4 * bg + 4, :], True, True)
                        ssl = Ssb[:, 4 * bg:4 * bg + 4, hd, :]
                        fb = e[2][:, hd, 4 * bg:4 * bg + 4, None].broadcast_to([128, 4, 128])
                        kb.tt(Sl(ssl, *range(4 * bg, 4 * bg + 4)), Sl(ssl, *range(4 * bg, 4 * bg + 4)), fb, ALU.mult)
                        kb.tt(Sl(ssl, *range(4 * bg, 4 * bg + 4)), Sl(ssl, *range(4 * bg, 4 * bg + 4)),
                              pp[:].rearrange("p (b v) -> p b v", v=128), ALU.add)
                for g_ in range(4):
                    kb.dma(pool, "outp", S_out_l[:, g_ * 2048:(g_ + 1) * 2048].rearrange("k (b h v) -> k b h v", b=4, v=128),
                           Ssb[:, 4 * g_:4 * g_ + 4, :, :], R=[Sl(Ssb, *range(4 * g_, 4 * g_ + 4))], W=[Sl(S_out_l, g_)])
                for b in range(NS):
                    kb.dma(pool, "outp", SS_d[b].rearrange("h k v -> k h v"),
                           S_out_l[:, b * 512:(b + 1) * 512].rearrange("k (h v) -> k h v", v=128), R=[Sl(S_out_l, b // 4)])
                    kb.A(Sl(Sbf[:, b, :, :], b), Sl(Ssb[:, b, :, :], b), AF.Copy)
                for hd in range(4):
                    for b in range(NS):
                        kb.mm(pb[3][:, hd * NS + b:hd * NS + b + 1], Sl(Sbf[:, b, hd, :], b), eb[2][:, hd, b:b + 1],
                              True, True)
                po = pb[3][:, 0:4 * NS].rearrange("p (h b) -> p h b", b=NS)
                kb.cp(osb[:], po)
                kb.tt(eb[0][:], osb[:], osb[:], ALU.mult)
                kb.mm(pb[4][:, 0:4 * NS], ones_h[:], eb[0][:].rearrange("p h b -> p (h b)"), True, True)
                kb.rsqrt(e[4][:], pb[4][:, 0:4 * NS].rearrange("p (h b) -> p h b", b=NS), eps_col[:])
                kb.A(e[5][:], gg, AF.Tanh, scale=0.5)
                kb.stt(e[5][:], e[5][:], 1.0, gg, ALU.add, ALU.mult)
                kb.stt(e[6][:], osb[:], dc_[:, 40:41], e[4][:], ALU.mult, ALU.mult)
                kb.tt(mixs[:, 4:8, :], e[6][:], e[5][:], ALU.mult)
                kb.hook = None
                for _ in wg3:
                    pass
                wout_phase(mixs, NS, lambda dcx: ys_s[:, dcx, :], lambda dcx: xsT[:, dcx, :], None, rs[:])
                barrier(kb)

            barrier(kb)

    ffn(0, 0, last=(stage == 1))
    if stage >= 2:
        mixer(last=(stage == 2))
    if stage >= 3:
        ffn(2, 1, last=True)
    kb.finish()


def barrier(kb):
    engs = [kb.pe, kb.act, kb.dve, kb.pool, kb.sp]
    for e in engs:
        for o in engs:
            if o is not e and o.count and e.seen.get(o, 0) < o.count:
                e.h.wait_ge(o.sem, o.count)
                e.seen[o] = o.count
        for ch in kb.chans.values():
            if ch.count and e.seen.get(ch, 0) < ch.count:
                e.h.wait_ge(ch.sem, ch.count * 16)
                e.seen[ch] = ch.count
    kb.res.clear()


def _consts():
    kcv = np.zeros((128, NKC), np.float32)
    kcv[:, K_ID:K_ID + 128] = np.eye(128, dtype=np.float32)
    s = np.arange(64)[:, None]
    t = np.arange(64)[None, :]
    kcv[:64, K_MASK:K_MASK + 64] = (s <= t).astype(np.float32)
    return kcv


def _col(v, n):
    return np.ascontiguousarray(np.asarray(v, np.float32).reshape(n, 128).T)


def make_in_maps(inp):
    f = lambda k: np.asarray(inp[k], np.float32)
    cst = np.zeros((128, NCST), np.float32)
    for i, k in enumerate(["ln_ffn1_pre", "ln_ffn1_post", "ln_mix_pre", "ln_mix_post", "ln_ffn2_pre", "ln_ffn2_post"]):
        cst[:, C_LN + 8 * i:C_LN + 8 * i + 8] = _col(f(k)[0], 8)
    cst[:, C_BADA:C_BADA + 72] = _col(f("b_ada")[0], 72)
    cw = f("lru_conv_w")[0]
    for c in range(4):
        for k in range(4):
            cst[:, C_CONVW + c * 4 + k] = cw[k, c * 128:(c + 1) * 128]
    cst[:, C_CONVB:C_CONVB + 4] = _col(f("lru_conv_b")[0], 4)
    cst[:, C_BA:C_BA + 4] = _col(f("lru_b_a")[0], 4)
    cst[:, C_BX:C_BX + 4] = _col(f("lru_b_x")[0], 4)
    cst[:, C_LAM:C_LAM + 4] = _col(f("lru_lambda")[0], 4)
    lbl = f("hg_lb_logits")
    for r in range(2):
        cst[:, C_LB + r * 4:C_LB + r * 4 + 4] = _col(lbl[r], 4)
    cst[:, C_NW] = f("hg_norm_w")[0]
    kcv = _consts()

    def bd(w):
        o = np.zeros((128, 4, 128), np.float32)
        for c in range(4):
            o[0:64, c, 0:64] = w[2 * c]
            o[64:128, c, 64:128] = w[2 * c + 1]
        return o

    shared = {
        "cst": cst, "kc": kcv,
        "w_ada": np.ascontiguousarray(f("w_ada")[0]),
        "w_gate1": np.ascontiguousarray(f("ffn1_w_gate")[0]), "w_up1": np.ascontiguousarray(f("ffn1_w_up")[0]),
        "w_down1": np.ascontiguousarray(f("ffn1_w_down")[0]),
        "w_gate2": np.ascontiguousarray(f("ffn2_w_gate")[0]), "w_up2": np.ascontiguousarray(f("ffn2_w_up")[0]),
        "w_down2": np.ascontiguousarray(f("ffn2_w_down")[0]),
        "w_in": np.ascontiguousarray(f("w_in")[0]), "w_out": np.ascontiguousarray(f("w_out")[0]),
        "w_a_bd": bd(f("lru_w_a")[0]), "w_x_bd": bd(f("lru_w_x")[0]),
    }
    xp, xs = f("x_prompt"), f("x_sample")
    cp, cs = f("c_prompt"), f("c_sample")
    sh, scv, sS = f("state_lru_h")[0], f("state_lru_conv")[0], f("state_hgrn_S")[0]
    maps = []
    for b in range(NCORES):
        rows = slice(NS * b, NS * (b + 1))
        m = dict(shared)
        m["xT"] = np.ascontiguousarray(xp[b].T)
        m["xsT"] = np.ascontiguousarray(xs[rows, 0, :].T)
        m["cT"] = np.ascontiguousarray(np.concatenate([cp[b:b + 1], cs[rows]], axis=0).T)
        m["h0T"] = np.ascontiguousarray(sh[rows].T)
        m["conv0T"] = np.ascontiguousarray(scv[rows].transpose(2, 1, 0))
        m["S0"] = np.ascontiguousarray(sS[rows])
        maps.append(m)
    return maps


_NC_CACHE = {}


def run(inp, stage=STAGE):
    if stage not in _NC_CACHE:
        _NC_CACHE[stage] = build(stage)
    nc = _NC_CACHE[stage]
    maps = make_in_maps(inp)
    res = run_bass_kernel_spmd(nc, maps, core_ids=list(range(NCORES)))
    return res.results


def kernel(**inp):
    rs = run(inp)
    y = np.stack([r["yT"].T for r in rs]).astype(np.float32)
    ys = np.concatenate([r["ysT"].T for r in rs])[:, None, :].astype(np.float32)
    hP = np.stack([r["hP"].T.reshape(512) for r in rs])[None]
    cP = np.stack([r["convP"].transpose(2, 1, 0).reshape(3, 512) for r in rs])[None]
    SPo = np.stack([r["SP"] for r in rs])[None]
    hS = np.concatenate([r["hS"].transpose(2, 1, 0).reshape(NS, 512) for r in rs])[None]
    cS = np.concatenate([r["convS"].transpose(3, 2, 1, 0).reshape(NS, 3, 512) for r in rs])[None]
    SSo = np.concatenate([r["SS"] for r in rs])[None]
    f32 = lambda a: np.ascontiguousarray(a, dtype=np.float32)
    return (f32(y), f32(ys), f32(hP), f32(cP), f32(SPo), f32(hS), f32(cS), f32(SSo))
```

```python
import contextlib
import numpy as np
import concourse.bass as bass
import concourse.mybir as mybir
from concourse.bass_utils import run_bass_kernel_spmd

F32 = mybir.dt.float32
BF16 = mybir.dt.bfloat16
ALU = mybir.AluOpType
AF = mybir.ActivationFunctionType

NCORES = 8
D = 1024
T = 2048
NS = 16
DFF = 2816
NJ = DFF // 128
EPS = 1e-6
CFG = {"order": "ABL", "steps": {"A": 1, "B": 1, "L": 1}, "delay": {"B": 6}}
STAGE = 99

C_LN = 0
C_BADA = 48
C_CONVW = 120
C_CONVB = 136
C_BA = 140
C_BX = 144
C_LAM = 148
C_LB = 152
C_NW = 160
NCST = 161
K_ID = 0
K_MASK = 128
NKC = 192


class Sl:
    def __init__(self, ap, *slots):
        self.ap = ap
        self.slots = slots


def _ap(x):
    return x.ap if isinstance(x, Sl) else x


class _Eng:
    def __init__(self, name, h, sem):
        self.name, self.h, self.sem = name, h, sem
        self.count = 0
        self.seen = {}


class _Chan:
    def __init__(self, name, sem):
        self.name, self.sem = name, sem
        self.count = 0


class KB:
    def __init__(self, nc, es):
        self.nc = nc
        self.es = es
        mk = lambda n: es.enter_context(nc.semaphore(n))
        self.pe = _Eng("pe", nc.tensor, mk("s_pe"))
        self.act = _Eng("act", nc.scalar, mk("s_act"))
        self.dve = _Eng("dve", nc.vector, mk("s_dve"))
        self.pool = _Eng("pool", nc.gpsimd, mk("s_pool"))
        self.sp = _Eng("sp", nc.sync, mk("s_sp"))
        self.chans = {}
        self.res = {}
        self.rr = {}
        self.dry = False
        self.hook = None
        self.hook_every = 3
        self._in_hook = False
        self._hcnt = 0

    def chan(self, name):
        if name not in self.chans:
            self.chans[name] = _Chan(name, self.es.enter_context(self.nc.semaphore("c_" + name)))
        return self.chans[name]

    def _keys(self, xs):
        out = []
        for x in xs:
            if x is None or isinstance(x, (int, float)):
                continue
            if isinstance(x, Sl):
                name = getattr(x.ap, "tensor", x.ap).name
                if x.slots and not name.startswith("pb"):
                    out += [(name, s) for s in x.slots]
                else:
                    out.append((name, None))
            else:
                out.append((getattr(x, "tensor", x).name, None))
        return out

    def _conf(self, name, slot):
        d = self.res.setdefault(name, {})
        if slot is None:
            return list(d.values())
        return [d[k] for k in (None, slot) if k in d]

    def _deps(self, eng, R, W):
        need = {}

        def add(src, cnt, raw):
            if src is eng and (eng is self.pe):
                return
            if cnt > need.get(src, 0):
                need[src] = cnt

        for k in R:
            for ent in self._conf(*k):
                if ent[0] is not None:
                    add(ent[0][0], ent[0][1], True)
        for k in W:
            for ent in self._conf(*k):
                if ent[0] is not None:
                    add(ent[0][0], ent[0][1], False)
                for src, cnt in ent[1].items():
                    add(src, cnt, False)
        return need

    def _wait(self, eng, need):
        for src, cnt in need.items():
            if eng.seen.get(src, 0) >= cnt:
                continue
            eng.h.wait_ge(src.sem, cnt * (16 if isinstance(src, _Chan) else 1))
            eng.seen[src] = cnt

    def _register(self, tok, R, W):
        for name, slot in W:
            d = self.res.setdefault(name, {})
            if slot is None:
                d.clear()
            d[slot] = [tok, {}]
        for name, slot in R:
            d = self.res.setdefault(name, {})
            ent = d.get(slot)
            if ent is None:
                ent = d[slot] = [None, {}]
            if tok[1] > ent[1].get(tok[0], 0):
                ent[1][tok[0]] = tok[1]

    def op(self, eng, fn, R=(), W=(), inc=True):
        if self.dry:
            return
        R = self._keys(R)
        W = self._keys(W)
        self._wait(eng, self._deps(eng, R, W))
        ins = fn(eng.h)
        if inc:
            ins.then_inc(eng.sem, 1)
            eng.count += 1
            tok = (eng, eng.count)
        else:
            tok = (eng, eng.count + 1)
        self._register(tok, R, W)
        if self.hook is not None and not self._in_hook:
            self._hcnt += 1
            if self._hcnt % self.hook_every == 0:
                self._in_hook = True
                self.hook()
                self._in_hook = False

    NCH = {"in": 8, "w": 8, "out": 8, "outp": 8}

    def dma(self, qeng, chan, out, in_, R=(), W=()):
        if self.dry:
            return
        i = self.rr.get(chan, 0)
        self.rr[chan] = i + 1
        ch = self.chan(f"{chan}{i % self.NCH[chan]}")
        R = self._keys(R)
        W = self._keys(W)
        need = self._deps(qeng, R, W)
        if ch.count:
            need[ch] = max(need.get(ch, 0), ch.count)
        self._wait(qeng, need)
        qeng.h.dma_start(out=_ap(out), in_=_ap(in_)).then_inc(ch.sem, 16)
        ch.count += 1
        self._register((ch, ch.count), R, W)

    def mm(self, out, lhsT, rhs, start, stop, inc=None):
        inc = stop if inc is None else inc
        self.op(self.pe, lambda e: e.matmul(_ap(out), _ap(lhsT), _ap(rhs), start=start, stop=stop),
                R=[lhsT, rhs], W=[out], inc=inc)

    def tr(self, out, in_, ident, inc=True):
        self.op(self.pe, lambda e: e.transpose(_ap(out), _ap(in_), _ap(ident)), R=[in_, ident], W=[out], inc=inc)

    def A(self, out, in_, func, bias=None, scale=1.0, eng=None):
        kw = {}
        if bias is not None:
            kw["bias"] = _ap(bias)
        kw["scale"] = _ap(scale)
        self.op(self.act, lambda e: e.activation(out=_ap(out), in_=_ap(in_), func=func, **kw),
                R=[in_, bias, scale], W=[out])

    def tt(self, out, in0, in1, op, eng=None):
        eng = eng or self.dve
        self.op(eng, lambda e: e.tensor_tensor(_ap(out), _ap(in0), _ap(in1), op), R=[in0, in1], W=[out])

    def ts(self, out, in0, s1, op0, s2=None, op1=None, eng=None):
        eng = eng or self.dve
        if op1 is None:
            fn = lambda e: e.tensor_scalar(_ap(out), _ap(in0), _ap(s1), None, op0)
        else:
            fn = lambda e: e.tensor_scalar(_ap(out), _ap(in0), _ap(s1), _ap(s2), op0, op1)
        self.op(eng, fn, R=[in0, s1, s2], W=[out])

    def stt(self, out, in0, sc, in1, op0, op1, eng=None):
        eng = eng or self.dve
        self.op(eng, lambda e: e.scalar_tensor_tensor(_ap(out), _ap(in0), _ap(sc), _ap(in1), op0, op1),
                R=[in0, sc, in1], W=[out])

    def scan(self, out, d0, d1, init, op0=ALU.mult, op1=ALU.add):
        self.op(self.dve, lambda e: e.tensor_tensor_scan(_ap(out), _ap(d0), _ap(d1), _ap(init), op0, op1),
                R=[d0, d1, init], W=[out])

    def cp(self, out, in_, eng=None):
        eng = eng or self.dve
        self.op(eng, lambda e: e.tensor_copy(_ap(out), _ap(in_)), R=[in_], W=[out])

    def rsqrt(self, out, in_, eps_col):
        self.A(out, in_, AF.Ln, bias=eps_col)
        self.A(out, out, AF.Exp, scale=-0.5)

    def finish(self):
        for ch in self.chans.values():
            if self.sp.seen.get(ch, 0) < ch.count:
                self.sp.h.wait_ge(ch.sem, ch.count * 16)
        for e in (self.pe, self.act, self.dve, self.pool):
            if e.count:
                self.sp.h.wait_ge(e.sem, e.count)


def build(stage=STAGE):
    nc = bass.Bass("TRN2", target_bir_lowering=False)
    es = contextlib.ExitStack()
    with es:
        _build(nc, es, stage)
    return nc


def _build(nc, es, stage):
    def DI(name, shape):
        return nc.dram_tensor(name, shape, F32, kind="ExternalInput").ap()

    def DO(name, shape):
        return nc.dram_tensor(name, shape, F32, kind="ExternalOutput").ap()

    xT_d = DI("xT", [D, T])
    xsT_d = DI("xsT", [D, NS])
    cT_d = DI("cT", [D, 17])
    cst_d = DI("cst", [128, NCST])
    kc_d = DI("kc", [128, NKC])
    wada_d = DI("w_ada", [D, 9 * D])
    wg_d = [DI("w_gate1", [D, DFF]), DI("w_gate2", [D, DFF])]
    wu_d = [DI("w_up1", [D, DFF]), DI("w_up2", [D, DFF])]
    wd_d = [DI("w_down1", [DFF, D]), DI("w_down2", [DFF, D])]
    win_d = DI("w_in", [D, 3 * D])
    wout_d = DI("w_out", [D, D])
    wabd_d = DI("w_a_bd", [128, 4, 128])
    wxbd_d = DI("w_x_bd", [128, 4, 128])
    h0T_d = DI("h0T", [512, NS])
    conv0T_d = DI("conv0T", [512, 3, NS])
    S0_d = DI("S0", [NS, 4, 128, 128])

    win_bf = nc.dram_tensor("win_bf16", [24, 128, 8, 128], BF16, kind="Internal").ap()
    wout_bf = nc.dram_tensor("wout_bf16", [8, 128, 8, 128], BF16, kind="Internal").ap()

    S_in_l = nc.dram_tensor("S_in_l", [128, NS * 512], F32, kind="Internal").ap()
    S_out_l = nc.dram_tensor("S_out_l", [128, NS * 512], F32, kind="Internal").ap()

    yT_d = DO("yT", [D, T])
    ysT_d = DO("ysT", [D, NS])
    hP_d = DO("hP", [128, 4])
    convP_d = DO("convP", [128, 4, 3])
    SP_d = DO("SP", [4, 128, 128])
    hS_d = DO("hS", [128, 4, NS])
    convS_d = DO("convS", [128, 4, 3, NS])
    SS_d = DO("SS", [NS, 4, 128, 128])

    kb = KB(nc, es)
    pe, act, dve, pool, sp = kb.pe, kb.act, kb.dve, kb.pool, kb.sp

    def SB(name, shape, dt=F32, stack=es):
        return stack.enter_context(nc.sbuf_tensor(name, shape, dt))

    xT = SB("xT_sb", [128, 8, T])
    xsT = SB("xsT_sb", [128, 8, NS])
    cst = SB("cst_sb", [128, NCST])
    kc = SB("kc_sb", [128, NKC])
    modT = SB("modT", [128, 72, 17])
    Acoef = [SB(f"Acoef{i}", [128, 8, 17]) for i in range(3)]
    Gcoef = [SB(f"Gcoef{i}", [128, 8, 17]) for i in range(3)]
    ones_d = SB("ones_d", [128, 128], BF16)
    ones_h = SB("ones_h", [128, 128], BF16)
    ident_bf = SB("ident_bf", [128, 128], BF16)
    eps_col = SB("eps_col", [128, 1])
    one_col = SB("one_col", [128, 1])
    pb = [es.enter_context(nc.psum_tensor(f"pb{i}", [128, 512], F32)) for i in range(8)]

    xT_dv = xT_d.rearrange("(c p) t -> p c t", p=128)
    yT_dv = yT_d.rearrange("(c p) t -> p c t", p=128)

    kb.dma(sp, "in", cst[:], cst_d, W=[cst])
    kb.dma(sp, "in", kc[:], kc_d, W=[kc])
    cT = SB("cT_sb", [128, 8, 17], F32)
    kb.dma(sp, "in", cT[:], cT_d.rearrange("(c p) t -> p c t", p=128), W=[cT])
    kb.dma(sp, "in", xsT[:], xsT_d.rearrange("(c p) t -> p c t", p=128), W=[xsT])
    for blk in range(4):
        kb.dma(sp, "in", xT[:, :, blk * 512:(blk + 1) * 512], xT_dv[:, :, blk * 512:(blk + 1) * 512],
               W=[Sl(xT, blk)])
    kb.op(dve, lambda e: e.memset(ones_d[:], 1.0 / 1024.0), W=[ones_d])
    kb.op(dve, lambda e: e.memset(ones_h[:], 1.0 / 128.0), W=[ones_h])
    kb.cp(ident_bf[:], kc[:, K_ID:K_ID + 128])
    kb.op(dve, lambda e: e.memset(eps_col[:], EPS), W=[eps_col])
    kb.op(dve, lambda e: e.memset(one_col[:], 1.0), W=[one_col])

    siluT = SB("siluT", [128, 8, 17], BF16)
    kb.A(siluT[:], cT[:], AF.Silu)
    wada_v = wada_d.rearrange("(c p) n -> p c n", p=128)
    mod_state = {"ring": None, "order": list(range(72)), "issued": 0, "done": 0, "nbank": 0}

    def mod_prefetch(upto):
        ring = mod_state["ring"]
        while mod_state["issued"] < min(upto, 72):
            n = mod_state["issued"]
            g = mod_state["order"][n]
            kb.dma(pool, "w", ring[n % len(ring)][:], wada_v[:, :, g * 128:(g + 1) * 128], W=[ring[n % len(ring)]])
            mod_state["issued"] += 1

    def mod_coefs(l, which):
        lnpre = cst[:, C_LN + 16 * l:C_LN + 16 * l + 8, None].broadcast_to([128, 8, 17])
        lnpost = cst[:, C_LN + 16 * l + 8:C_LN + 16 * l + 16, None].broadcast_to([128, 8, 17])
        if which == "A":
            kb.stt(Acoef[l][:], modT[:, (3 * l + 1) * 8:(3 * l + 2) * 8, :], 1.0, lnpre, ALU.add, ALU.mult)
        else:
            kb.stt(Gcoef[l][:], modT[:, (3 * l + 2) * 8:(3 * l + 3) * 8, :], 0.5 if l != 1 else 1.0, lnpost,
                   ALU.mult, ALU.mult)

    def mod_piece(banks):
        n = mod_state["done"]
        if n >= 72:
            return False
        ring = mod_state["ring"]
        mod_prefetch(n + len(ring))
        g = mod_state["order"][n]
        wb = ring[n % len(ring)]
        bank = banks[mod_state["nbank"] % len(banks)]
        mod_state["nbank"] += 1
        for kcx in range(8):
            kb.mm(bank[:, 0:17], wb[:, kcx, :], siluT[:, kcx, :], start=(kcx == 0), stop=(kcx == 7))
        kb.ts(Sl(modT[:, g, :], g), bank[:, 0:17], cst[:, C_BADA + g:C_BADA + g + 1], ALU.add)
        mod_state["done"] += 1
        i = g // 8
        if g % 8 == 7:
            if i % 3 == 1:
                mod_coefs(i // 3, "A")
            elif i % 3 == 2:
                mod_coefs(i // 3, "G")
        return True

    def shift(l):
        return modT[:, (3 * l) * 8:(3 * l + 1) * 8, :]

    def ffn(l, fi, last):
        A, G, sh = Acoef[l], Gcoef[l], shift(l)
        with contextlib.ExitStack() as fs:
            hy = SB(f"hy{fi}", [128, 8192], F32, fs)
            actT = [SB(f"actT{fi}_{i}", [128, NJ, 512], BF16, fs) for i in range(2)]
            actsT = SB(f"actsT{fi}", [128, NJ, NS], BF16, fs)
            wg = [SB(f"wg{fi}_{i}", [128, 8, 256], BF16, fs) for i in range(2)]
            wu = [SB(f"wu{fi}_{i}", [128, 8, 256], BF16, fs) for i in range(2)]
            wd = [SB(f"wd{fi}_{i}", [128, NJ, 128], BF16, fs) for i in range(2)]
            sq = [SB(f"sq{fi}_{i}", [128, 512], BF16, fs) for i in range(4)]
            tmp = [SB(f"tmp{fi}_{i}", [128, 512], F32, fs) for i in range(2)]
            sg = [SB(f"sg{fi}_{i}", [128, 512], F32, fs) for i in range(2)]
            rstd = [SB(f"rstd{fi}_{i}", [128, 512], F32, fs) for i in range(2)]
            hsT = SB(f"hsT{fi}", [128, 8, NS], BF16, fs)
            sm = [SB(f"sm{fi}_{i}", [128, 8, NS], F32, fs) for i in range(3)]
            sqs = SB(f"sqs{fi}", [128, 8, NS], BF16, fs)
            rs = SB(f"rs{fi}", [128, NS], F32, fs)
            sgs = SB(f"sgs{fi}", [128, NS], F32, fs)

            if fi == 0:
                mod_state["ring"] = [SB(f"wada{i}", [128, 8, 128], BF16, fs) for i in range(3)]
                for _ in range(16):
                    mod_piece([pb[5], pb[6], pb[7]])
            hT = [Sl(hy[:, b * 2048:(b + 1) * 2048].bitcast(BF16).rearrange("p (c t) -> p c t", t=512), b)
                  for b in range(2)]
            yv = [Sl(hy[:, 4096:8192].rearrange("p (c t) -> p c t", t=512), 2),
                  Sl(hy[:, 0:4096].rearrange("p (c t) -> p c t", t=512), 0, 1)]
            wg_v = wg_d[fi].rearrange("(c p) n -> p c n", p=128)
            wu_v = wu_d[fi].rearrange("(c p) n -> p c n", p=128)
            wd_v = wd_d[fi].rearrange("(j p) n -> p j n", p=128)
            cnt = {"n": 0, "m": 0, "q": 0, "t": 0, "r": 0}

            def prenorm(blk, bi):
                cols = slice(blk * 512, (blk + 1) * 512)
                for c in range(8):
                    s = sq[cnt["q"] % 4]
                    cnt["q"] += 1
                    kb.A(s[:], Sl(xT[:, c, cols], blk), AF.Square)
                    kb.mm(pb[4][:, :], ones_d[:], s[:], start=(c == 0), stop=(c == 7), inc=True)
                r = rstd[cnt["r"] % 2]
                cnt["r"] += 1
                kb.rsqrt(r[:], pb[4][:, :], eps_col[:])
                for c in range(8):
                    t = tmp[cnt["t"] % 2]
                    cnt["t"] += 1
                    kb.stt(t[:], Sl(xT[:, c, cols], blk), A[:, c, 0:1], r[:], ALU.mult, ALU.mult)
                    kb.A(Sl(hT[bi].ap[:, c, :], bi), t[:], AF.Identity, bias=sh[:, c, 0:1])

            def prenorm_s():
                kb.A(sqs[:], xsT[:], AF.Square)
                for c in range(8):
                    kb.mm(pb[4][:, 0:NS], ones_d[:], sqs[:, c, :], start=(c == 0), stop=(c == 7), inc=True)
                kb.rsqrt(rs[:], pb[4][:, 0:NS], eps_col[:])
                kb.tt(sm[0][:], xsT[:], rs[:, None, :].broadcast_to([128, 8, NS]), ALU.mult)
                kb.tt(sm[1][:], sm[0][:], A[:, :, 1:17], ALU.mult)
                kb.tt(hsT[:], sm[1][:], sh[:, :, 1:17], ALU.add)

            for half in range(2):
                blks = [2 * half, 2 * half + 1]
                if half == 0:
                    prenorm_s()
                for bi, blk in enumerate(blks):
                    prenorm(blk, bi)
                def load_gu(jg_):
                    kb.dma(pool, "w", wg[jg_ % 2][:], wg_v[:, :, jg_ * 256:(jg_ + 1) * 256], W=[wg[jg_ % 2]])
                    kb.dma(pool, "w", wu[jg_ % 2][:], wu_v[:, :, jg_ * 256:(jg_ + 1) * 256], W=[wu[jg_ % 2]])

                load_gu(0)
                for jg in range(NJ // 2):
                    wgb, wub = wg[jg % 2], wu[jg % 2]
                    if fi == 0 and half == 1 and jg < 8:
                        for b_ in (2 * jg, 2 * jg + 1):
                            kb.dma(sp, "in", S_in_l[:, b_ * 512:(b_ + 1) * 512].rearrange("k (h v) -> k h v", v=128),
                                   S0_d[b_].rearrange("h k v -> k h v"), W=[Sl(S_in_l, b_ // 4)])
                    if fi == 0 and half == 1:
                        for p_ in range(2) if jg < 8 else []:
                            i_ = 2 * jg + p_
                            for f_ in range(2):
                                if i_ < 12:
                                    fc_ = 2 * i_ + f_
                                    kb.dma(pool, "w", win_bf[fc_],
                                           win_d[:, fc_ * 128:(fc_ + 1) * 128].rearrange("(c p) n -> p c n", p=128),
                                           W=[Sl(win_bf, fc_)])
                                else:
                                    fc_ = 2 * (i_ - 12) + f_
                                    kb.dma(pool, "w", wout_bf[fc_],
                                           wout_d[:, fc_ * 128:(fc_ + 1) * 128].rearrange("(c p) n -> p c n", p=128),
                                           W=[Sl(wout_bf, fc_)])
                    if jg + 1 < NJ // 2:
                        load_gu(jg + 1)
                    for jj in range(2):
                        j = 2 * jg + jj
                        js = slice(jj * 128, (jj + 1) * 128)
                        for bi in range(2):
                            n = cnt["n"]
                            cnt["n"] += 1
                            pg, pu = pb[n % 2], pb[2 + n % 2]
                            for kcx in range(8):
                                kb.mm(pg[:], wgb[:, kcx, js], Sl(hT[bi].ap[:, kcx, :], bi), kcx == 0, kcx == 7)
                            for kcx in range(8):
                                kb.mm(pu[:], wub[:, kcx, js], Sl(hT[bi].ap[:, kcx, :], bi), kcx == 0, kcx == 7)
                            s = sg[n % 2]
                            kb.A(s[:], pg[:], AF.Silu)
                            kb.tt(Sl(actT[bi][:, j, :], j), s[:], pu[:], ALU.mult)
                            if fi == 0:
                                mod_piece([pb[5], pb[6], pb[7]])
                        if half == 0:
                            o = 64 + 32 * (j % 2)
                            psg = Sl(pb[4][:, o:o + NS], ("sg", j % 2))
                            psu = Sl(pb[4][:, o + NS:o + 2 * NS], ("su", j % 2))
                            for kcx in range(8):
                                kb.mm(psg, wgb[:, kcx, js], hsT[:, kcx, :], kcx == 0, kcx == 7)
                            for kcx in range(8):
                                kb.mm(psu, wub[:, kcx, js], hsT[:, kcx, :], kcx == 0, kcx == 7)
                            kb.A(sgs[:], psg, AF.Silu)
                            kb.tt(Sl(actsT[:, j, :], j), sgs[:], psu, ALU.mult)
                pend = []
                def load_d(dc_):
                    kb.dma(pool, "w", wd[dc_ % 2][:], wd_v[:, :, dc_ * 128:(dc_ + 1) * 128], W=[wd[dc_ % 2]])

                load_d(0)
                for dc in range(8):
                    wdb = wd[dc % 2]
                    if dc + 1 < 8:
                        load_d(dc + 1)
                    for bi in range(2):
                        m = cnt["m"]
                        cnt["m"] += 1
                        py = pb[5 + m % 2]
                        for j in range(NJ):
                            kb.mm(py[:], wdb[:, j, :], Sl(actT[bi][:, j, :], j), j == 0, j == NJ - 1)
                        while pend:
                            pend.pop(0)()
                        kb.A(Sl(yv[bi].ap[:, dc, :], *yv[bi].slots), py[:], AF.Copy)
                        s = sq[cnt["q"] % 4]
                        cnt["q"] += 1
                        kb.A(s[:], py[:], AF.Square)
                        pend.append(lambda s=s, bi=bi, dc=dc: kb.mm(pb[4][:, :] if bi == 0 else pb[7][:, :], ones_d[:],
                                                                   s[:], start=(dc == 0), stop=(dc == 7), inc=True))
                    if half == 0:
                        psy = Sl(pb[3][:, dc * NS:(dc + 1) * NS], ("ys", dc))
                        for j in range(NJ):
                            kb.mm(psy, wdb[:, j, :], Sl(actsT[:, j, :], j), j == 0, j == NJ - 1)
                while pend:
                    pend.pop(0)()
                for bi, blk in enumerate(blks):
                    cols = slice(blk * 512, (blk + 1) * 512)
                    stp = pb[4][:, :] if bi == 0 else pb[7][:, :]
                    r = rstd[cnt["r"] % 2]
                    cnt["r"] += 1
                    kb.rsqrt(r[:], stp, eps_col[:])
                    for dc in range(8):
                        t = tmp[cnt["t"] % 2]
                        cnt["t"] += 1
                        kb.stt(t[:], Sl(yv[bi].ap[:, dc, :], *yv[bi].slots), G[:, dc, 0:1], r[:], ALU.mult, ALU.mult)
                        kb.tt(Sl(xT[:, dc, cols], blk), Sl(xT[:, dc, cols], blk), t[:], ALU.add)
                    if last:
                        kb.dma(sp, "out", yT_dv[:, :, cols], Sl(xT[:, :, cols], blk), R=[Sl(xT, blk)])
                if half == 0:
                    ysv = pb[3][:, 0:8 * NS].rearrange("p (c t) -> p c t", t=NS)
                    kb.cp(sm[0][:], Sl(ysv, *[("ys", dc) for dc in range(8)]))
                    kb.tt(sqs[:], sm[0][:], sm[0][:], ALU.mult)
                    for c in range(8):
                        kb.mm(pb[2][:, 0:NS], ones_d[:], sqs[:, c, :], start=(c == 0), stop=(c == 7), inc=True)
                    kb.rsqrt(rs[:], pb[2][:, 0:NS], eps_col[:])
                    kb.tt(sm[1][:], sm[0][:], rs[:, None, :].broadcast_to([128, 8, NS]), ALU.mult)
                    kb.tt(sm[2][:], sm[1][:], G[:, :, 1:17], ALU.mult)
                    kb.tt(xsT[:], xsT[:], sm[2][:], ALU.add)
                    if last:
                        kb.dma(sp, "out", ysT_d.rearrange("(c p) t -> p c t", p=128), xsT[:], R=[xsT])
            barrier(kb)


    def mixer(last):
        A, G, sh = Acoef[1], Gcoef[1], shift(1)
        with contextlib.ExitStack() as mx:
            NWB = 4
            w_ring = [SB(f"w_ring{i}", [128, 8, 128], BF16, mx) for i in range(NWB)]
            wabd = SB("wabd", [128, 4, 128], BF16, mx)
            wxbd = SB("wxbd", [128, 4, 128], BF16, mx)
            wo = [SB(f"wo{i}", [128, 8, 128], BF16, mx) for i in range(2)]
            dc_ = SB("dcoef", [128, 48], F32, mx)
            sq = [SB(f"msq{i}", [128, 512], BF16, mx) for i in range(4)]
            tmp = [SB(f"mtmp{i}", [128, 512], F32, mx) for i in range(2)]
            rstd = SB("mrstd", [128, 512], F32, mx)
            pr_order = [c2 for c in range(4) for c2 in (c, 4 + c)] + [8 + 4 * i + h for h in range(4) for i in range(4)]
            hg_order = [8 + 4 * i + h for h in range(4) for i in range(4)]
            lru_order = [c2 for c in range(4) for c2 in (c, 4 + c)]
            tb_order = []
            for c in range(4):
                tb_order += [c, 4 + c] + [8 + 4 * i + c for i in range(4)]
            w_seq = []
            dry_order = []
            wst = {"issued": 0, "used": 0}

            def w_prefetch(upto):
                while wst["issued"] < min(upto, len(w_seq)):
                    i = wst["issued"]
                    fc = w_seq[i]
                    kb.dma(sp, "in", w_ring[i % NWB][:], win_bf[fc], R=[Sl(win_bf, fc)], W=[w_ring[i % NWB]])
                    wst["issued"] += 1

            def w_get(fc):
                i = wst["used"]
                assert w_seq[i] == fc, (i, w_seq[i], fc)
                w_prefetch(i + NWB)
                wst["used"] += 1
                return w_ring[i % NWB]

            w_prefetch(NWB)
            kb.dma(pool, "w", wabd[:], wabd_d, W=[wabd])
            kb.dma(pool, "w", wxbd[:], wxbd_d, W=[wxbd])

            lam = cst[:, C_LAM:C_LAM + 4]
            t4 = [SB(f"t4_{i}", [128, 4], F32, mx) for i in range(6)]
            kb.ts(t4[5][:], lam, -1.0, ALU.mult)
            kb.tt(t4[0][:], lam, t4[5][:], ALU.max)
            kb.A(t4[1][:], t4[0][:], AF.Exp, scale=-1.0)
            kb.ts(t4[2][:], t4[1][:], 2.0, ALU.add)
            kb.op(dve, lambda e_: e_.reciprocal(t4[2][:], t4[2][:]), R=[t4[2]], W=[t4[2]])
            kb.tt(t4[2][:], t4[1][:], t4[2][:], ALU.mult)
            kb.tt(t4[3][:], t4[2][:], t4[2][:], ALU.mult)
            kb.ts(t4[4][:], t4[3][:], 1.0 / 11.0, ALU.mult, 1.0 / 9.0, ALU.add)
            for cf in (1.0 / 7.0, 1.0 / 5.0, 1.0 / 3.0, 1.0):
                kb.tt(t4[4][:], t4[4][:], t4[3][:], ALU.mult)
                kb.ts(t4[4][:], t4[4][:], cf, ALU.add)
            kb.tt(t4[4][:], t4[4][:], t4[2][:], ALU.mult)
            kb.ts(t4[5][:], lam, -1.0, ALU.mult, 0.0, ALU.max)
            kb.stt(dc_[:, 0:4], t4[4][:], 2.0, t4[5][:], ALU.mult, ALU.add)
            kb.ts(dc_[:, 4:8], dc_[:, 0:4], -4.0, ALU.mult)
            kb.ts(dc_[:, 8:12], dc_[:, 0:4], -8.0, ALU.mult)
            kb.ts(dc_[:, 12:16], cst[:, C_BA:C_BA + 4], 0.5, ALU.mult)
            kb.ts(dc_[:, 16:20], cst[:, C_BX:C_BX + 4], 0.5, ALU.mult)
            kb.tt(t4[0][:], cst[:, C_LB:C_LB + 4], cst[:, C_LB + 4:C_LB + 8], ALU.subtract)
            kb.A(t4[1][:], t4[0][:], AF.Tanh, scale=0.5)
            kb.ts(dc_[:, 20:24], t4[1][:], 0.5, ALU.mult, 0.5, ALU.add)
            kb.ts(dc_[:, 24:28], dc_[:, 20:24], -0.5, ALU.mult, 0.5, ALU.add)
            kb.tt(dc_[:, 28:32], dc_[:, 20:24], dc_[:, 24:28], ALU.add)
            kb.ts(dc_[:, 32:36], dc_[:, 24:28], -1.0, ALU.mult)
            kb.ts(dc_[:, 36:40], dc_[:, 28:32], -1.0, ALU.mult, 1.0, ALU.add)
            kb.ts(dc_[:, 40:41], cst[:, C_NW:C_NW + 1], 0.5, ALU.mult)
            SP_, M4, M8, HBA, HBX, C1, C0, NC1, OMC0 = (slice(4 * i, 4 * i + 4) for i in range(9))
            SP_, M4, M8, HBA, HBX = slice(0, 4), slice(4, 8), slice(8, 12), slice(12, 16), slice(16, 20)
            C1, C0, NC1, OMC0 = slice(24, 28), slice(28, 32), slice(32, 36), slice(36, 40)
            QS = 128.0 ** -0.5
            GC = 0.7978845608028654

            def bc(ap4, n=NS):
                return ap4[:, :, None].broadcast_to([128, 4, n])

            def wout_phase(mixT_, ncol, ysb_fn, xdst, gsl, rs_t):
                for dcx in range(8):
                    wb = wo[dcx % 2]
                    kb.dma(sp, "in", wb[:], wout_bf[dcx], R=[Sl(wout_bf, dcx)], W=[wb])
                    py = pb[dcx % 2]
                    for kcx in range(8):
                        kb.mm(py[:, 0:ncol], wb[:, kcx, :], mixT_[:, kcx, :], kcx == 0, kcx == 7)
                    kb.A(ysb_fn(dcx), py[:, 0:ncol], AF.Copy)
                    s_ = sq[dcx % 4]
                    kb.A(s_[:, 0:ncol], py[:, 0:ncol], AF.Square)
                    kb.mm(pb[7][:, 0:ncol], ones_d[:], s_[:, 0:ncol], start=(dcx == 0), stop=(dcx == 7), inc=True)
                kb.rsqrt(rs_t, pb[7][:, 0:ncol], eps_col[:])
                for dcx in range(8):
                    t_ = tmp[dcx % 2]
                    if ncol == 512:
                        kb.stt(t_[:], ysb_fn(dcx), G[:, dcx, 0:1], rs_t, ALU.mult, ALU.mult)
                    else:
                        kb.tt(t_[:, 0:ncol], ysb_fn(dcx), rs_t, ALU.mult)
                        kb.tt(t_[:, 0:ncol], t_[:, 0:ncol], G[:, dcx, 1:17], ALU.mult)
                    kb.tt(xdst(dcx), xdst(dcx), t_[:, 0:ncol], ALU.add)

            mixTs = [SB(f"mixT{i}", [128, 8, 512], BF16, mx) for i in range(2)]
            prog = {"lru_done": False}

            def wout_chain(tbp, mixp, gated=True):
                colsp = slice(tbp * 512, (tbp + 1) * 512)
                while gated and not prog["lru_done"]:
                    yield
                pend_ = None
                for dcx in range(8):
                    wb = wo[dcx % 2]
                    kb.dma(sp, "in", wb[:], wout_bf[dcx], R=[Sl(wout_bf, dcx)], W=[wb])
                    for kcx in range(8):
                        kb.mm(pb[6][:, :], wb[:, kcx, :], mixp[:, kcx, :], kcx == 0, kcx == 7)
                    yield
                    if pend_ is not None:
                        pend_()
                    s_ = sq[dcx % 2]
                    kb.A(s_[:], pb[6][:, :], AF.Square)
                    pend_ = (lambda s_=s_, dcx=dcx: kb.mm(pb[7][:, :], ones_d[:], s_[:], start=(dcx == 0),
                                                          stop=(dcx == 7), inc=True))
                    yield
                pend_()
                yield
                kb.rsqrt(rstd[:], pb[7][:, :], eps_col[:])
                yield
                for dcx in range(8):
                    wb = wo[dcx % 2]
                    kb.dma(sp, "in", wb[:], wout_bf[dcx], R=[Sl(wout_bf, dcx)], W=[wb])
                    py = pb[6 + dcx % 2]
                    for kcx in range(8):
                        kb.mm(py[:, :], wb[:, kcx, :], mixp[:, kcx, :], kcx == 0, kcx == 7)
                    yield
                    t_ = tmp[dcx % 2]
                    kb.stt(t_[:], py[:, :], G[:, dcx, 0:1], rstd[:], ALU.mult, ALU.mult)
                    kb.tt(Sl(xT[:, dcx, colsp], tbp), Sl(xT[:, dcx, colsp], tbp), t_[:], ALU.add)
                    yield


            with contextlib.ExitStack() as ps:
                hT = SB("mhT", [128, 8, 512], BF16, ps)
                ub = [SB(f"ub{c}", [128, 515], F32, ps) for c in range(4)]
                hst = SB("hstate", [128, 4], F32, ps)
                cP = SB("convP_sb", [128, 4, 3], F32, ps)
                Tt = [SB(f"T{i}", [128, 512], F32, ps) for i in range(8)]
                TL = [SB(f"TL{i}", [128, 512], F32, ps) for i in range(7)]
                ucb = SB("ucb", [128, 512], BF16, ps)
                S = [SB(f"Sst{h}", [128, 128], F32, ps) for h in range(4)]
                HR = []
                for r_ in range(2):
                    HR.append({
                        "T": Tt[4 * r_:4 * r_ + 4] + [SB(f"TH{r_}_{i}", [128, 512], F32, ps) for i in range(2)],
                        "khT": SB(f"khT{r_}", [128, 512], BF16, ps), "qT": SB(f"qT{r_}", [128, 512], BF16, ps),
                        "vT": SB(f"vT{r_}", [128, 512], BF16, ps),
                        "ktok": SB(f"ktok{r_}", [64, 8, 128], BF16, ps), "vtok": SB(f"vtok{r_}", [64, 8, 128], BF16, ps),
                        "scT": SB(f"scT{r_}", [64, 8, 64], BF16, ps), "hsq": SB(f"hsq{r_}", [128, 512], BF16, ps),
                        "Zs": SB(f"Zs{r_}", [128, 128, 8], F32, ps), "AzB": SB(f"AzB{r_}", [128, 64, 8], F32, ps),
                        "Az": SB(f"Az{r_}", [128, 8], F32, ps), "Sbf": SB(f"Sbf{r_}", [128, 9, 128], BF16, ps),
                        "banks": (pb[3 * r_], pb[3 * r_ + 1], pb[3 * r_ + 2]),
                    })
                for c in range(4):
                    kb.op(dve, lambda e_, c=c: e_.memset(ub[c][:, 0:3], 0.0), W=[ub[c]])
                    kb.op(dve, lambda e_, c=c: e_.memset(S[c][:], 0.0), W=[S[c]])
                kb.op(dve, lambda e_: e_.memset(hst[:], 0.0), W=[hst])
                maskb = kc[0:64, None, K_MASK:K_MASK + 64].broadcast_to([64, 8, 64])
                for tb in range(4):
                    cols = slice(tb * 512, (tb + 1) * 512)
                    mixT = mixTs[tb % 2]
                    for c in range(8):
                        s_ = sq[c % 4]
                        kb.A(s_[:], Sl(xT[:, c, cols], tb), AF.Square)
                        kb.mm(pb[7][:, :], ones_d[:], s_[:], start=(c == 0), stop=(c == 7), inc=True)
                    kb.rsqrt(rstd[:], pb[7][:, :], eps_col[:])
                    for c in range(8):
                        t_ = tmp[c % 2]
                        kb.stt(t_[:], Sl(xT[:, c, cols], tb), A[:, c, 0:1], rstd[:], ALU.mult, ALU.mult)
                        kb.A(hT[:, c, :], t_[:], AF.Identity, bias=sh[:, c, 0:1])

                    def proj(pdst, fc):
                        if kb.dry:
                            dry_order.append(fc)
                            return
                        wb_ = w_get(fc)
                        for kcx in range(8):
                            kb.mm(pdst[:], wb_[:, kcx, :], hT[:, kcx, :], kcx == 0, kcx == 7)

                    def lru_chain():
                        for c in range(4):
                            uc, thr, a_, a2, thi, hs, ge = TL
                            proj(pb[6], c)
                            yield
                            proj(pb[7], 4 + c)
                            yield
                            kb.A(ub[c][:, 3:515], pb[6][:], AF.Copy)
                            yield
                            kb.A(ge[:], pb[7][:], AF.Gelu_apprx_tanh)
                            yield
                            cw_ = lambda k: cst[:, C_CONVW + 4 * c + k:C_CONVW + 4 * c + k + 1]
                            kb.ts(uc[:], ub[c][:, 3:515], cw_(3), ALU.mult, cst[:, C_CONVB + c:C_CONVB + c + 1], ALU.add,
                                  eng=pool)
                            for k in range(3):
                                kb.ts(hs[:], ub[c][:, k:k + 512], cw_(k), ALU.mult, 0.0, ALU.add, eng=pool)
                                kb.tt(uc[:], uc[:], hs[:], ALU.add, eng=pool)
                            yield
                            if tb == 3:
                                kb.cp(cP[:, c, :], ub[c][:, 512:515])
                            else:
                                kb.cp(ub[c][:, 0:3], ub[c][:, 512:515])
                            kb.A(ucb[:], uc[:], AF.Copy)
                            yield
                            kb.mm(pb[6][:], wabd[:, c, :], ucb[:], True, True)
                            yield
                            kb.mm(pb[7][:], wxbd[:, c, :], ucb[:], True, True)
                            yield
                            kb.A(thr[:], pb[6][:], AF.Tanh, bias=dc_[:, 12 + c:13 + c], scale=0.5)
                            yield
                            kb.A(thi[:], pb[7][:], AF.Tanh, bias=dc_[:, 16 + c:17 + c], scale=0.5)
                            yield
                            kb.A(a_[:], thr[:], AF.Exp, bias=dc_[:, 4 + c:5 + c], scale=dc_[:, 4 + c:5 + c])
                            yield
                            kb.A(a2[:], thr[:], AF.Exp, bias=dc_[:, 8 + c:9 + c], scale=dc_[:, 8 + c:9 + c])
                            yield
                            kb.A(a2[:], a2[:], AF.Ln, bias=one_col[:], scale=-1.0)
                            yield
                            kb.A(a2[:], a2[:], AF.Exp, scale=0.5)
                            kb.stt(thi[:], thi[:], 1.0, uc[:], ALU.add, ALU.mult)
                            kb.stt(thr[:], a2[:], 0.5, thi[:], ALU.mult, ALU.mult)
                            if tb == 0:
                                kb.ts(thr[:, 0:1], thi[:, 0:1], 0.5, ALU.mult)
                            kb.scan(hs[:], a_[:], thr[:], hst[:, c:c + 1])
                            yield
                            kb.cp(hst[:, c:c + 1], hs[:, 511:512])
                            yield
                            kb.tt(mixT[:, c, :], ge[:], hs[:], ALU.mult)
                            yield
                        prog["lru_done"] = True
                    def hgrn_chain(heads, R):
                        thf, d0, kk, P_, gq, d1 = R["T"]
                        osb_, t1 = kk, thf
                        khT, qT, vT, ktok, vtok, scT = R["khT"], R["qT"], R["vT"], R["ktok"], R["vtok"], R["scT"]
                        Zs, AzB, Az, Sb, hsq = R["Zs"], R["AzB"], R["Az"], R["Sbf"], R["hsq"]
                        bX, bY, bZ = R["banks"]
                        kb.op(dve, lambda e_: e_.memset(d1[:], 0.0), W=[d1])
                        for hd in heads:
                            proj(bX, 8 + hd)
                            yield
                            proj(bY, 12 + hd)
                            yield
                            proj(bZ, 16 + hd)
                            yield
                            kb.A(thf[:], bY[:], AF.Tanh, scale=0.5)
                            yield
                            proj(bY, 20 + hd)
                            yield
                            kb.A(vT[:], bZ[:], AF.Copy)
                            yield
                            kb.A(gq[:], bY[:], AF.Tanh, scale=0.5)
                            yield
                            kb.ts(d0[:], thf[:], dc_[:, 24 + hd:25 + hd], ALU.mult, dc_[:, 28 + hd:29 + hd], ALU.add,
                                  eng=pool)
                            kb.ts(kk[:], thf[:], dc_[:, 32 + hd:33 + hd], ALU.mult, dc_[:, 36 + hd:37 + hd], ALU.add,
                                  eng=pool)
                            yield
                            kb.stt(gq[:], gq[:], 1.0, bY[:], ALU.add, ALU.mult)
                            kb.cp(d1[:, 0:512:64], d0[:, 0:512:64])
                            kb.op(dve, lambda e_: e_.memset(d0[:, 0:512:64], 0.0), W=[d0])
                            yield
                            kb.scan(P_[:], d0[:], d1[:], 0.0)
                            yield
                            kb.op(dve, lambda e_: e_.reciprocal(d0[:], P_[:]), R=[P_], W=[d0])
                            yield
                            kb.tt(khT[:], kk[:], d0[:], ALU.mult)
                            yield
                            kb.stt(qT[:], bX[:], QS, P_[:], ALU.mult, ALU.mult)
                            yield
                            ptk = bY[:, :].bitcast(BF16)
                            ptv = bZ[:, :].bitcast(BF16)
                            for cc in range(8):
                                kb.tr(ptk[0:64, cc * 128:(cc + 1) * 128], khT[:, cc * 64:(cc + 1) * 64], ident_bf[:],
                                      inc=(cc == 7))
                            yield
                            for cc in range(8):
                                kb.tr(ptv[0:64, cc * 128:(cc + 1) * 128], vT[:, cc * 64:(cc + 1) * 64], ident_bf[:],
                                      inc=(cc == 7))
                            yield
                            kb.cp(ktok[:], ptk[0:64, :].rearrange("p (c k) -> p c k", k=128))
                            yield
                            kb.A(vtok[:], ptv[0:64, :].rearrange("p (c k) -> p c k", k=128), AF.Copy)
                            yield
                            for cc in range(8):
                                kb.mm(bX[0:64, cc * 64:(cc + 1) * 64], khT[:, cc * 64:(cc + 1) * 64],
                                      qT[:, cc * 64:(cc + 1) * 64], True, True, inc=(cc == 7))
                            yield
                            kb.tt(scT[:], bX[0:64, :].rearrange("p (c t) -> p c t", t=64), maskb, ALU.mult)
                            yield
                            for cc in range(8):
                                pd = (bY, bZ)[cc // 4]
                                o_ = (cc % 4) * 128
                                kb.mm(pd[:, o_:o_ + 128], ktok[:, cc, :], vtok[:, cc, :], True, True, inc=(cc % 4 == 3))
                            yield
                            Av = P_[:, 63:512:64]
                            kb.cp(Az[:], Av)
                            kb.op(dve, lambda e_: e_.memset(Az[:, 0:1], 0.0), W=[Az])
                            yield
                            kb.A(AzB[:], Az[:, None, :].broadcast_to([128, 64, 8]), AF.Copy)
                            yield
                            for b2 in range(2):
                                kb.tt(Zs[:].rearrange("k v c -> k c v")[:, 4 * b2:4 * b2 + 4, :],
                                      (bY, bZ)[b2][:, :].rearrange("k (c v) -> k c v", v=128),
                                      Av[:, 4 * b2:4 * b2 + 4, None].broadcast_to([128, 4, 128]), ALU.mult)
                                yield
                            kb.stt(Zs[:, :, 0], S[hd][:], Av[:, 0:1], Zs[:, :, 0], ALU.mult, ALU.add)
                            kb.cp(Sb[:, 0, :], S[hd][:])
                            yield
                            for vh in range(2):
                                zz = Zs[:, vh * 64:(vh + 1) * 64, :].rearrange("k v c -> k (v c)")
                                kb.scan(zz, AzB[:].rearrange("k v c -> k (v c)"), zz, 0.0)
                                yield
                            kb.cp(S[hd][:], Zs[:, :, 7])
                            yield
                            kb.A(Sb[:, 1:9, :], Zs[:].rearrange("k v c -> k c v"), AF.Copy)
                            yield
                            for cc in range(8):
                                po_ = bX[:, cc * 64:(cc + 1) * 64]
                                kb.mm(po_, Sb[:, cc, :], qT[:, cc * 64:(cc + 1) * 64], True, False, inc=False)
                                kb.mm(po_, vtok[:, cc, :], scT[:, cc, :], False, True, inc=(cc == 7))
                            yield
                            kb.A(osb_[:], bX[:], AF.Copy)
                            yield
                            kb.A(hsq[:], bX[:], AF.Square)
                            yield
                            kb.mm(bY[:], ones_h[:], hsq[:], True, True)
                            yield
                            kb.rsqrt(t1[:], bY[:], eps_col[:])
                            yield
                            kb.stt(osb_[:], osb_[:], dc_[:, 40:41], t1[:], ALU.mult, ALU.mult)
                            yield
                            kb.tt(mixT[:, 4 + hd, :], osb_[:], gq[:], ALU.mult)
                            yield

                    def run_chains():
                        prog["lru_done"] = False
                        gens = {"A": hgrn_chain((0, 1), HR[0]), "B": hgrn_chain((2, 3), HR[1]), "L": lru_chain()}
                        live = [c_ for c_ in CFG["order"]]
                        if tb > 0:
                            gens["W"] = wout_chain(tb - 1, mixTs[(tb - 1) % 2])
                            live.append("W")
                        rnd = 0
                        while live:
                            for c_ in list(live):
                                if rnd < CFG["delay"].get(c_, 0):
                                    continue
                                try:
                                    for _ in range(CFG["steps"].get(c_, 1)):
                                        next(gens[c_])
                                except StopIteration:
                                    live.remove(c_)
                            rnd += 1

                    if tb == 0:
                        kb.dry = True
                        run_chains()
                        kb.dry = False
                        w_seq.extend(dry_order * 4 + list(range(24)))
                    run_chains()
                    if last:
                        kb.dma(sp, "out", yT_dv[:, :, cols], Sl(xT[:, :, cols], tb), R=[Sl(xT, tb)])
                kb.dma(sp, "out", hP_d, hst[:], R=[hst])
                kb.dma(sp, "out", convP_d, cP[:], R=[cP])
                for hd in range(4):
                    kb.dma(sp, "out", SP_d[hd], S[hd][:], R=[S[hd]])
                if last:
                    kb.dma(sp, "out", ysT_d.rearrange("(c p) t -> p c t", p=128), xsT[:], R=[xsT])
                barrier(kb)
            with contextlib.ExitStack() as ss:
                Ssb = SB("Ssb", [128, NS, 4, 128], F32, ss)
                Sbf = SB("Sbf_s", [128, NS, 4, 128], BF16, ss)
                h0 = SB("h0_s", [128, 4, NS], F32, ss)
                c0s = SB("c0_s", [128, 4, 3, NS], F32, ss)
                cnew = SB("cnew_s", [128, 4, 3, NS], F32, ss)
                hsT = SB("mhsT", [128, 8, NS], BF16, ss)
                mixs = SB("mixs", [128, 8, NS], BF16, ss)
                ps_sb = SB("ps_sb", [128, 24, NS], F32, ss)
                sm = [SB(f"msm{i}", [128, 8, NS], F32, ss) for i in range(3)]
                sqs = SB("msqs", [128, 8, NS], BF16, ss)
                rs = SB("mrs", [128, NS], F32, ss)
                e = [SB(f"e{i}", [128, 4, NS], F32, ss) for i in range(10)]
                eb = [SB(f"eb{i}", [128, 4, NS], BF16, ss) for i in range(3)]
                kvt = SB("kvt", [16, 8, 128], BF16, ss)
                vbd = [SB("vbd0", [16, NS, 128], BF16, ss)] * 2
                osb = SB("osb", [128, 4, NS], F32, ss)
                ys_s = SB("ys_s", [128, 8, NS], F32, ss)
                wg3 = wout_chain(3, mixTs[1], gated=False)

                def adv(n_=1):
                    for _ in range(n_):
                        next(wg3, None)

                kb.dma(sp, "in", h0[:], h0T_d.rearrange("(c p) b -> p c b", p=128), W=[h0])
                kb.dma(sp, "in", c0s[:], conv0T_d.rearrange("(c p) k b -> p c k b", p=128), W=[c0s])
                for g_ in range(4):
                    kb.dma(sp, "in", Ssb[:, 4 * g_:4 * g_ + 4, :, :],
                           S_in_l[:, g_ * 2048:(g_ + 1) * 2048].rearrange("k (b h v) -> k b h v", b=4, v=128),
                           R=[Sl(S_in_l, g_)], W=[Sl(Ssb, *range(4 * g_, 4 * g_ + 4))])
                kb.A(sqs[:], xsT[:], AF.Square)
                for c in range(8):
                    kb.mm(pb[4][:, 0:NS], ones_d[:], sqs[:, c, :], start=(c == 0), stop=(c == 7), inc=True)
                kb.rsqrt(rs[:], pb[4][:, 0:NS], eps_col[:])
                kb.tt(sm[0][:], xsT[:], rs[:, None, :].broadcast_to([128, 8, NS]), ALU.mult)
                kb.tt(sm[1][:], sm[0][:], A[:, :, 1:17], ALU.mult)
                kb.tt(hsT[:], sm[1][:], sh[:, :, 1:17], ALU.add)
                for fc in range(24):
                    wb_ = w_get(fc)
                    for kcx in range(8):
                        kb.mm(pb[5][:, fc * NS:(fc + 1) * NS], wb_[:, kcx, :], hsT[:, kcx, :], kcx == 0, kcx == 7)
                kb.cp(ps_sb[:], pb[5][:, 0:24 * NS].rearrange("p (f t) -> p f t", t=NS))
                u, yv = ps_sb[:, 0:4, :], ps_sb[:, 4:8, :]
                q, fr, v, gg = ps_sb[:, 8:12, :], ps_sb[:, 12:16, :], ps_sb[:, 16:20, :], ps_sb[:, 20:24, :]
                cw = lambda k: cst[:, C_CONVW + k:C_CONVW + 16:4]
                def lru_s_chain():
                    uc = e[0]
                    kb.tt(uc[:], u, bc(cw(3)), ALU.mult)
                    yield
                    kb.tt(uc[:], uc[:], bc(cst[:, C_CONVB:C_CONVB + 4]), ALU.add)
                    yield
                    for k in range(3):
                        kb.tt(e[1][:], c0s[:, :, k, :], bc(cw(k)), ALU.mult)
                        yield
                        kb.tt(uc[:], uc[:], e[1][:], ALU.add)
                        yield
                    kb.cp(cnew[:, :, 0:2, :], c0s[:, :, 1:3, :])
                    yield
                    kb.cp(cnew[:, :, 2, :], u)
                    yield
                    kb.dma(pool, "outp", convS_d, cnew[:], R=[cnew])
                    yield
                    kb.cp(eb[0][:], uc[:])
                    yield
                    for c in range(4):
                        kb.mm(pb[5][:, c * NS:(c + 1) * NS], wabd[:, c, :], eb[0][:, c, :], True, True)
                        yield
                    for c in range(4):
                        kb.mm(pb[5][:, (4 + c) * NS:(5 + c) * NS], wxbd[:, c, :], eb[0][:, c, :], True, True)
                        yield
                    pa = pb[5][:, 0:4 * NS].rearrange("p (c t) -> p c t", t=NS)
                    px = pb[5][:, 4 * NS:8 * NS].rearrange("p (c t) -> p c t", t=NS)
                    kb.stt(e[1][:], pa, 0.5, bc(dc_[:, HBA]), ALU.mult, ALU.add)
                    yield
                    kb.A(e[1][:], e[1][:], AF.Tanh)
                    kb.stt(e[2][:], px, 0.5, bc(dc_[:, HBX]), ALU.mult, ALU.add)
                    yield
                    kb.A(e[2][:], e[2][:], AF.Tanh)
                    kb.stt(e[3][:], e[1][:], 1.0, bc(dc_[:, M4]), ALU.add, ALU.mult)
                    yield
                    kb.A(e[3][:], e[3][:], AF.Exp)
                    kb.tt(e[4][:], e[3][:], e[3][:], ALU.mult)
                    kb.ts(e[4][:], e[4][:], -1.0, ALU.mult, 1.0, ALU.add)
                    kb.A(e[4][:], e[4][:], AF.Ln)
                    yield
                    kb.A(e[4][:], e[4][:], AF.Exp, scale=0.5)
                    kb.stt(e[5][:], e[2][:], 1.0, uc[:], ALU.add, ALU.mult)
                    kb.stt(e[5][:], e[4][:], 0.5, e[5][:], ALU.mult, ALU.mult)
                    kb.tt(e[6][:], e[3][:], h0[:], ALU.mult)
                    yield
                    kb.tt(e[6][:], e[6][:], e[5][:], ALU.add)
                    kb.dma(pool, "outp", hS_d, e[6][:], R=[e[6]])
                    yield
                    kb.tt(e[7][:], yv, yv, ALU.mult)
                    yield
                    kb.ts(e[7][:], e[7][:], 0.044715, ALU.mult, 1.0, ALU.add)
                    yield
                    kb.tt(e[7][:], e[7][:], yv, ALU.mult)
                    yield
                    kb.A(e[7][:], e[7][:], AF.Tanh, scale=GC)
                    yield
                    kb.stt(e[7][:], e[7][:], 1.0, yv, ALU.add, ALU.mult)
                    yield
                    kb.stt(mixs[:, 0:4, :], e[7][:], 0.5, e[6][:], ALU.mult, ALU.mult)
                    yield
                lg_s = lru_s_chain()
                eh = [None, e[8], e[9]] + [SB(f"eh{i}", [128, 4, NS], F32, ss) for i in range(4)]
                ebh0 = SB("ebh0", [128, 4, NS], BF16, ss)
                kb.hook_every = 2
                kb.hook = lambda: (next(wg3, None), next(lg_s, None))
                kb.A(eh[1][:], fr, AF.Tanh, scale=0.5)
                kb.tt(eh[2][:], eh[1][:], bc(dc_[:, C1]), ALU.mult)
                kb.tt(eh[2][:], eh[2][:], bc(dc_[:, C0]), ALU.add)
                kb.ts(eh[3][:], eh[2][:], -1.0, ALU.mult, 1.0, ALU.add)
                kb.cp(ebh0[:], eh[3][:])
                kb.cp(eb[1][:], v)
                kb.ts(eb[2][:], q, QS, ALU.mult)
                ptb = pb[0][:, :].bitcast(BF16)
                for hd in range(4):
                    kb.tr(ptb[0:NS, hd * 128:(hd + 1) * 128], ebh0[:, hd, :], ident_bf[:])
                    kb.tr(ptb[0:NS, (4 + hd) * 128:(5 + hd) * 128], eb[1][:, hd, :], ident_bf[:])
                kb.cp(kvt[:], ptb[0:NS, :].rearrange("p (g k) -> p g k", k=128))
                n_ = 0
                for hd in range(4):
                    vb = vbd[hd % 2]
                    kb.tt(vb[:], kvt[:, 4 + hd, None, :].broadcast_to([NS, NS, 128]),
                          kc[0:NS, K_ID:K_ID + NS, None].broadcast_to([NS, NS, 128]), ALU.mult)
                    for bg in range(4):
                        pp = pb[1 + n_ % 2]
                        n_ += 1
                        kb.mm(pp[:], kvt[:, hd, :], vb[:, 4 * bg:4 * bg + 4, :], True, True)
                        ssl = Ssb[:, 4 * bg:4 * bg + 4, hd, :]
                        fb = eh[2][:, hd, 4 * bg:4 * bg + 4, None].broadcast_to([128, 4, 128])
                        kb.tt(Sl(ssl, *range(4 * bg, 4 * bg + 4)), Sl(ssl, *range(4 * bg, 4 * bg + 4)), fb, ALU.mult)
                        kb.tt(Sl(ssl, *range(4 * bg, 4 * bg + 4)), Sl(ssl, *range(4 * bg, 4 * bg + 4)),
                              pp[:].rearrange("p (b v) -> p b v", v=128), ALU.add)
                for g_ in range(4):
                    kb.dma(pool, "outp", S_out_l[:, g_ * 2048:(g_ + 1) * 2048].rearrange("k (b h v) -> k b h v", b=4, v=128),
                           Ssb[:, 4 * g_:4 * g_ + 4, :, :], R=[Sl(Ssb, *range(4 * g_, 4 * g_ + 4))], W=[Sl(S_out_l, g_)])
                for b in range(NS):
                    kb.dma(pool, "outp", SS_d[b].rearrange("h k v -> k h v"),
                           S_out_l[:, b * 512:(b + 1) * 512].rearrange("k (h v) -> k h v", v=128), R=[Sl(S_out_l, b // 4)])
                    kb.A(Sl(Sbf[:, b, :, :], b), Sl(Ssb[:, b, :, :], b), AF.Copy)
                for hd in range(4):
                    for b in range(NS):
                        kb.mm(pb[3][:, hd * NS + b:hd * NS + b + 1], Sl(Sbf[:, b, hd, :], b), eb[2][:, hd, b:b + 1],
                              True, True)
                po = pb[3][:, 0:4 * NS].rearrange("p (h b) -> p h b", b=NS)
                kb.cp(osb[:], po)
                kb.tt(ebh0[:], osb[:], osb[:], ALU.mult)
                kb.mm(pb[4][:, 0:4 * NS], ones_h[:], ebh0[:].rearrange("p h b -> p (h b)"), True, True)
                kb.rsqrt(eh[4][:], pb[4][:, 0:4 * NS].rearrange("p (h b) -> p h b", b=NS), eps_col[:])
                kb.A(eh[5][:], gg, AF.Tanh, scale=0.5)
                kb.stt(eh[5][:], eh[5][:], 1.0, gg, ALU.add, ALU.mult)
                kb.stt(eh[6][:], osb[:], dc_[:, 40:41], eh[4][:], ALU.mult, ALU.mult)
                kb.tt(mixs[:, 4:8, :], eh[6][:], eh[5][:], ALU.mult)
                kb.hook = None
                for _ in lg_s:
                    pass
                for _ in wg3:
                    pass
                wout_phase(mixs, NS, lambda dcx: ys_s[:, dcx, :], lambda dcx: xsT[:, dcx, :], None, rs[:])
                barrier(kb)

            barrier(kb)

    ffn(0, 0, last=(stage == 1))
    if stage >= 2:
        mixer(last=(stage == 2))
    if stage >= 3:
        ffn(2, 1, last=True)
    kb.finish()


def barrier(kb):
    engs = [kb.pe, kb.act, kb.dve, kb.pool, kb.sp]
    for e in engs:
        for o in engs:
            if o is not e and o.count and e.seen.get(o, 0) < o.count:
                e.h.wait_ge(o.sem, o.count)
                e.seen[o] = o.count
        for ch in kb.chans.values():
            if ch.count and e.seen.get(ch, 0) < ch.count:
                e.h.wait_ge(ch.sem, ch.count * 16)
                e.seen[ch] = ch.count
    kb.res.clear()


def _consts():
    kcv = np.zeros((128, NKC), np.float32)
    kcv[:, K_ID:K_ID + 128] = np.eye(128, dtype=np.float32)
    s = np.arange(64)[:, None]
    t = np.arange(64)[None, :]
    kcv[:64, K_MASK:K_MASK + 64] = (s <= t).astype(np.float32)
    return kcv


def _col(v, n):
    return np.ascontiguousarray(np.asarray(v, np.float32).reshape(n, 128).T)


def make_in_maps(inp):
    f = lambda k: np.asarray(inp[k], np.float32)
    cst = np.zeros((128, NCST), np.float32)
    for i, k in enumerate(["ln_ffn1_pre", "ln_ffn1_post", "ln_mix_pre", "ln_mix_post", "ln_ffn2_pre", "ln_ffn2_post"]):
        cst[:, C_LN + 8 * i:C_LN + 8 * i + 8] = _col(f(k)[0], 8)
    cst[:, C_BADA:C_BADA + 72] = _col(f("b_ada")[0], 72)
    cw = f("lru_conv_w")[0]
    for c in range(4):
        for k in range(4):
            cst[:, C_CONVW + c * 4 + k] = cw[k, c * 128:(c + 1) * 128]
    cst[:, C_CONVB:C_CONVB + 4] = _col(f("lru_conv_b")[0], 4)
    cst[:, C_BA:C_BA + 4] = _col(f("lru_b_a")[0], 4)
    cst[:, C_BX:C_BX + 4] = _col(f("lru_b_x")[0], 4)
    cst[:, C_LAM:C_LAM + 4] = _col(f("lru_lambda")[0], 4)
    lbl = f("hg_lb_logits")
    for r in range(2):
        cst[:, C_LB + r * 4:C_LB + r * 4 + 4] = _col(lbl[r], 4)
    cst[:, C_NW] = f("hg_norm_w")[0]
    kcv = _consts()

    def bd(w):
        o = np.zeros((128, 4, 128), np.float32)
        for c in range(4):
            o[0:64, c, 0:64] = w[2 * c]
            o[64:128, c, 64:128] = w[2 * c + 1]
        return o

    shared = {
        "cst": cst, "kc": kcv,
        "w_ada": np.ascontiguousarray(f("w_ada")[0]),
        "w_gate1": np.ascontiguousarray(f("ffn1_w_gate")[0]), "w_up1": np.ascontiguousarray(f("ffn1_w_up")[0]),
        "w_down1": np.ascontiguousarray(f("ffn1_w_down")[0]),
        "w_gate2": np.ascontiguousarray(f("ffn2_w_gate")[0]), "w_up2": np.ascontiguousarray(f("ffn2_w_up")[0]),
        "w_down2": np.ascontiguousarray(f("ffn2_w_down")[0]),
        "w_in": np.ascontiguousarray(f("w_in")[0]), "w_out": np.ascontiguousarray(f("w_out")[0]),
        "w_a_bd": bd(f("lru_w_a")[0]), "w_x_bd": bd(f("lru_w_x")[0]),
    }
    xp, xs = f("x_prompt"), f("x_sample")
    cp, cs = f("c_prompt"), f("c_sample")
    sh, scv, sS = f("state_lru_h")[0], f("state_lru_conv")[0], f("state_hgrn_S")[0]
    maps = []
    for b in range(NCORES):
        rows = slice(NS * b, NS * (b + 1))
        m = dict(shared)
        m["xT"] = np.ascontiguousarray(xp[b].T)
        m["xsT"] = np.ascontiguousarray(xs[rows, 0, :].T)
        m["cT"] = np.ascontiguousarray(np.concatenate([cp[b:b + 1], cs[rows]], axis=0).T)
        m["h0T"] = np.ascontiguousarray(sh[rows].T)
        m["conv0T"] = np.ascontiguousarray(scv[rows].transpose(2, 1, 0))
        m["S0"] = np.ascontiguousarray(sS[rows])
        maps.append(m)
    return maps


_NC_CACHE = {}


def run(inp, stage=STAGE):
    if stage not in _NC_CACHE:
        _NC_CACHE[stage] = build(stage)
    nc = _NC_CACHE[stage]
    maps = make_in_maps(inp)
    res = run_bass_kernel_spmd(nc, maps, core_ids=list(range(NCORES)))
    return res.results


def kernel(**inp):
    rs = run(inp)
    y = np.stack([r["yT"].T for r in rs]).astype(np.float32)
    ys = np.concatenate([r["ysT"].T for r in rs])[:, None, :].astype(np.float32)
    hP = np.stack([r["hP"].T.reshape(512) for r in rs])[None]
    cP = np.stack([r["convP"].transpose(2, 1, 0).reshape(3, 512) for r in rs])[None]
    SPo = np.stack([r["SP"] for r in rs])[None]
    hS = np.concatenate([r["hS"].transpose(2, 1, 0).reshape(NS, 512) for r in rs])[None]
    cS = np.concatenate([r["convS"].transpose(3, 2, 1, 0).reshape(NS, 3, 512) for r in rs])[None]
    SSo = np.concatenate([r["SS"] for r in rs])[None]
    f32 = lambda a: np.ascontiguousarray(a, dtype=np.float32)
    return (f32(y), f32(ys), f32(hP), f32(cP), f32(SPo), f32(hS), f32(cS), f32(SSo))
```
